# Optimizing a Trainium2 kernel written in Bass

```python
import math
import jax, jax.numpy as jnp
from jax import lax
import numpy as np

D_MODEL = 1024
BATCH = 8
SEQ = 2048
DEPTH = 4

HEAD_DIM = 64
MIX_WIDTH = D_MODEL
DIFF_WIDTH = MIX_WIDTH // 2
MOBA_WIDTH = MIX_WIDTH - DIFF_WIDTH
DIFF_HEADS = DIFF_WIDTH // (2 * HEAD_DIM)
DIFF_V_DIM = 2 * HEAD_DIM
MOBA_HEADS = MOBA_WIDTH // HEAD_DIM
PROJ_WIDTH = 3 * DIFF_WIDTH + 3 * MOBA_WIDTH
N_BIAS_HEADS = DIFF_HEADS + MOBA_HEADS
D_FF = ((8 * D_MODEL // 3 + 127) // 128) * 128
CONV_WIDTH = 3
DIFF_Q_BLOCK = 128
MOBA_BLOCK = 256
MOBA_TOPK = 3
MOBA_Q_CHUNK = 64
REL_BUCKETS = 32
REL_MAX_EXACT = REL_BUCKETS // 2
REL_MAX_DISTANCE = 1024
NORM_EPS = 1e-6

kernel_name = "hymba_diffattn_moba_convffn_trunk"


def rms_norm(x, g):
    xf = x.astype(jnp.float32)
    y = xf * lax.rsqrt(jnp.mean(xf * xf, axis=-1, keepdims=True) + NORM_EPS)
    return (y * g).astype(x.dtype)


def rel_bucket(dist):
    n = jnp.maximum(dist, 0)
    nf = jnp.maximum(n, REL_MAX_EXACT).astype(jnp.float32)
    large = REL_MAX_EXACT + (jnp.log(nf / REL_MAX_EXACT) / math.log(REL_MAX_DISTANCE / REL_MAX_EXACT)
                             * (REL_BUCKETS - REL_MAX_EXACT)).astype(jnp.int32)
    large = jnp.minimum(large, REL_BUCKETS - 1)
    return jnp.where(n < REL_MAX_EXACT, n, large)


def diff_attention(q, k, v, lam, lam_init, sub_g, bias_tab):
    B, S, H = q.shape[:3]
    nqb = S // DIFF_Q_BLOCK
    qb = q.reshape(B, nqb, DIFF_Q_BLOCK, H, 2, HEAD_DIM).transpose(1, 0, 2, 3, 4, 5)
    kpos = jnp.arange(S)
    scale = HEAD_DIM ** -0.5

    def block(args):
        qblk, i = args
        qpos = i * DIFF_Q_BLOCK + jnp.arange(DIFF_Q_BLOCK)
        dist = qpos[:, None] - kpos[None, :]
        bias = jnp.moveaxis(bias_tab[rel_bucket(dist)], -1, 0)
        logits = jnp.einsum('bqhmd,bkhmd->bmhqk', qblk, k).astype(jnp.float32) * scale + bias
        logits = jnp.where(dist >= 0, logits, -jnp.inf)
        p = jax.nn.softmax(logits, axis=-1)
        a = p[:, 0] - lam * p[:, 1]
        return jnp.einsum('bhqk,bkhe->bqhe', a.astype(v.dtype), v)

    out = lax.map(block, (qb, jnp.arange(nqb)))
    out = out.transpose(1, 0, 2, 3, 4).reshape(B, S, H, DIFF_V_DIM)
    out = rms_norm(out, sub_g) * (1.0 - lam_init)
    return out.reshape(B, S, H * DIFF_V_DIM)


def moba_attention(q, k, v, bias_tab):
    B, S, H, D = q.shape
    nb = -(-S // MOBA_BLOCK)
    pad = nb * MOBA_BLOCK - S
    padw = ((0, 0), (0, pad), (0, 0), (0, 0))
    kb = jnp.pad(k, padw).reshape(B, nb, MOBA_BLOCK, H, D).transpose(0, 3, 1, 2, 4)
    vb = jnp.pad(v, padw).reshape(B, nb, MOBA_BLOCK, H, D).transpose(0, 3, 1, 2, 4)
    kmean = jnp.mean(kb.astype(jnp.float32), axis=3)
    gate = jnp.einsum('bshd,bhnd->bhsn', q.astype(jnp.float32), kmean)
    past = jnp.arange(nb)[None, :] < (jnp.arange(S) // MOBA_BLOCK)[:, None]
    gate = jnp.where(past, gate, -jnp.inf)
    topk = min(MOBA_TOPK, nb)
    sel_score, sel_idx = lax.top_k(gate, topk)
    sel_valid = jnp.isfinite(sel_score)

    nc = S // MOBA_Q_CHUNK
    qc = q.reshape(B, nc, MOBA_Q_CHUNK, H, D).transpose(1, 0, 3, 2, 4)
    idx_c = sel_idx.reshape(B, H, nc, MOBA_Q_CHUNK, topk).transpose(2, 0, 1, 3, 4)
    val_c = sel_valid.reshape(B, H, nc, MOBA_Q_CHUNK, topk).transpose(2, 0, 1, 3, 4)
    bi = jnp.arange(B)[:, None, None, None]
    hi = jnp.arange(H)[None, :, None, None]
    hi5 = jnp.arange(H)[None, :, None, None, None]
    tab_h = bias_tab.T
    j = jnp.arange(MOBA_BLOCK)
    scale = D ** -0.5
    n_sel = topk * MOBA_BLOCK

    def chunk(args):
        qch, idx, valid, c = args
        qpos = c * MOBA_Q_CHUNK + jnp.arange(MOBA_Q_CHUNK)
        gk = kb[bi, hi, idx]
        gv = vb[bi, hi, idx]
        dist_p = qpos[None, None, :, None, None] - (idx[..., None] * MOBA_BLOCK + j)
        bias_p = tab_h[hi5, rel_bucket(dist_p)]
        lp = jnp.einsum('bhqd,bhqnjd->bhqnj', qch, gk).astype(jnp.float32) * scale + bias_p
        lp = jnp.where(valid[..., None], lp, -jnp.inf).reshape(B, H, MOBA_Q_CHUNK, n_sel)
        ob = (c * MOBA_Q_CHUNK) // MOBA_BLOCK
        ko = lax.dynamic_index_in_dim(kb, ob, axis=2, keepdims=False)
        vo = lax.dynamic_index_in_dim(vb, ob, axis=2, keepdims=False)
        dist_o = qpos[:, None] - (ob * MOBA_BLOCK + j)[None, :]
        bias_o = tab_h[:, rel_bucket(dist_o)]
        lo = jnp.einsum('bhqd,bhjd->bhqj', qch, ko).astype(jnp.float32) * scale + bias_o
        lo = jnp.where(dist_o >= 0, lo, -jnp.inf)
        p = jax.nn.softmax(jnp.concatenate([lp, lo], axis=-1), axis=-1).astype(v.dtype)
        pp = p[..., :n_sel].reshape(B, H, MOBA_Q_CHUNK, topk, MOBA_BLOCK)
        po = p[..., n_sel:]
        return (jnp.einsum('bhqnj,bhqnje->bhqe', pp, gv)
                + jnp.einsum('bhqj,bhje->bhqe', po, vo))

    out = lax.map(chunk, (qc, idx_c, val_c, jnp.arange(nc)))
    return out.transpose(1, 0, 3, 2, 4).reshape(B, S, H * D)


def causal_dwconv(u, w, b):
    C = u.shape[-1]
    y = lax.conv_general_dilated(u, w[:, None, :], window_strides=(1,),
                                 padding=[(CONV_WIDTH - 1, 0)],
                                 dimension_numbers=('NWC', 'WIO', 'NWC'),
                                 feature_group_count=C)
    return y + b


def conv_ffn(h, w_up, conv_w, conv_b, w_down):
    u = causal_dwconv(h @ w_up, conv_w, conv_b)
    g, up = jnp.split(u, 2, axis=-1)
    return (jax.nn.silu(g) * up) @ w_down


def setup_inputs(seed: int = 0) -> dict:
    key = jax.random.key(seed)
    ks = jax.random.split(key, 14)
    f32 = jnp.float32
    nrm = lambda k, s: jax.random.normal(k, s, f32)
    resid = (2 * DEPTH) ** -0.5
    return {
        'x': nrm(ks[0], (BATCH, SEQ, D_MODEL)),
        'ln_attn_g': 1.0 + 0.02 * nrm(ks[1], (DEPTH, D_MODEL)),
        'w_in': nrm(ks[2], (DEPTH, D_MODEL, PROJ_WIDTH)) * D_MODEL ** -0.5,
        'qk_norm_g': 1.0 + 0.02 * nrm(ks[3], (DEPTH, 4, HEAD_DIM)),
        'diff_lambda': 0.1 * nrm(ks[4], (DEPTH, 4, HEAD_DIM)),
        'diff_subln_g': 1.0 + 0.02 * nrm(ks[5], (DEPTH, DIFF_V_DIM)),
        'w_out': nrm(ks[6], (DEPTH, MIX_WIDTH, D_MODEL)) * MIX_WIDTH ** -0.5 * resid,
        'ln_ffn_g': 1.0 + 0.02 * nrm(ks[7], (DEPTH, D_MODEL)),
        'w_up': nrm(ks[8], (DEPTH, D_MODEL, 2 * D_FF)) * D_MODEL ** -0.5,
        'conv_w': nrm(ks[9], (DEPTH, CONV_WIDTH, 2 * D_FF)) * CONV_WIDTH ** -0.5,
        'conv_b': 0.02 * nrm(ks[10], (DEPTH, 2 * D_FF)),
        'w_down': nrm(ks[11], (DEPTH, D_FF, D_MODEL)) * D_FF ** -0.5 * resid,
        'rel_bias': 0.5 * nrm(ks[12], (REL_BUCKETS, N_BIAS_HEADS)),
    }


def reference(x, ln_attn_g, w_in, qk_norm_g, diff_lambda, diff_subln_g, w_out,
              ln_ffn_g, w_up, conv_w, conv_b, w_down, rel_bias):
    B, S = x.shape[:2]
    cuts = [DIFF_WIDTH, 2 * DIFF_WIDTH, 3 * DIFF_WIDTH,
            3 * DIFF_WIDTH + MOBA_WIDTH, 3 * DIFF_WIDTH + 2 * MOBA_WIDTH]
    bias_diff = rel_bias[:, :DIFF_HEADS]
    bias_moba = rel_bias[:, DIFF_HEADS:]
    for i in range(DEPTH):
        lam_init = 0.8 - 0.6 * math.exp(-0.3 * i)
        h = rms_norm(x, ln_attn_g[i])
        dq, dk, dv, mq, mk, mv = jnp.split(h @ w_in[i], cuts, axis=-1)
        dq = rms_norm(dq.reshape(B, S, DIFF_HEADS, 2, HEAD_DIM), qk_norm_g[i, 0])
        dk = rms_norm(dk.reshape(B, S, DIFF_HEADS, 2, HEAD_DIM), qk_norm_g[i, 1])
        dv = dv.reshape(B, S, DIFF_HEADS, DIFF_V_DIM)
        lf = diff_lambda[i].astype(jnp.float32)
        lam = jnp.exp(jnp.sum(lf[0] * lf[1])) - jnp.exp(jnp.sum(lf[2] * lf[3])) + lam_init
        y_diff = diff_attention(dq, dk, dv, lam, lam_init, diff_subln_g[i], bias_diff)
        mq = rms_norm(mq.reshape(B, S, MOBA_HEADS, HEAD_DIM), qk_norm_g[i, 2])
        mk = rms_norm(mk.reshape(B, S, MOBA_HEADS, HEAD_DIM), qk_norm_g[i, 3])
        mv = mv.reshape(B, S, MOBA_HEADS, HEAD_DIM)
        y_moba = moba_attention(mq, mk, mv, bias_moba)
        x = x + jnp.concatenate([y_diff, y_moba], axis=-1) @ w_out[i]
        x = x + conv_ffn(rms_norm(x, ln_ffn_g[i]), w_up[i], conv_w[i], conv_b[i], w_down[i])
    return x
```

```python
import math
from contextlib import ExitStack
import numpy as np
import concourse.bass as bass
import concourse.mybir as mybir
from concourse.bass_utils import run_bass_kernel_spmd

F32 = mybir.dt.float32
BF16 = mybir.dt.bfloat16
ALU = mybir.AluOpType
AF = mybir.ActivationFunctionType
AX = mybir.AxisListType

S = 2048
D = 1024
NT = 16
DEPTH = 4
DFF = 2816
NCH = 22
EPS = 1e-6
NEG = -30000.0
TW = 1792
TOFF = 384
SCALE = 0.125

GC_LN1 = 0
GC_LN2 = 8
GC_QK = 16
GC_SUB = 20
GC_CW = 21
GC_CB = 21 + 132
GC_L = 21 + 132 + 44
GC = GC_L * DEPTH


class Op:
    __slots__ = ("eng", "fn", "deps", "sig", "pos", "dma", "sem", "val")

    def __init__(self, eng, fn, dma):
        self.eng = eng
        self.fn = fn
        self.dma = dma
        self.deps = ()
        self.sig = dma
        self.pos = 0
        self.sem = None
        self.val = 0


class Prog:
    ENGS = ("pe", "act", "dve", "pool", "sp")

    def __init__(self):
        self.ops = []
        self.lastw = {}
        self.readers = {}
        self.cnt = {e: 0 for e in self.ENGS}
        self.last_on = {e: None for e in self.ENGS}
        self.dma_since_barrier = []

    def add(self, eng, fn, r=(), w=(), dma=False):
        op = Op(eng, fn, dma)
        idx = len(self.ops)
        op.pos = self.cnt[eng]
        self.cnt[eng] += 1
        deps = set()
        for k in r:
            lw = self.lastw.get(k)
            if lw is not None:
                deps.add(lw)
        for k in w:
            lw = self.lastw.get(k)
            if lw is not None:
                deps.add(lw)
            rd = self.readers.get(k)
            if rd:
                deps.update(rd[0].values())
                deps.update(rd[1])
        for k in r:
            rd = self.readers.setdefault(k, ({}, []))
            if dma:
                rd[1].append(idx)
            else:
                rd[0][eng] = idx
        for k in w:
            self.lastw[k] = idx
            self.readers[k] = ({}, [])
        deps.discard(idx)
        op.deps = self._filter(op, deps)
        self.ops.append(op)
        self.last_on[eng] = idx
        if dma:
            self.dma_since_barrier.append(idx)
        return idx

    def _filter(self, op, deps):
        out = []
        for d in deps:
            p = self.ops[d]
            if (not p.dma) and p.eng == op.eng and not op.dma:
                if op.eng == "pe":
                    continue
                if op.pos - p.pos > 3:
                    continue
            p.sig = True
            out.append(d)
        return tuple(out)

    def barrier(self):
        lasts = [v for v in self.last_on.values() if v is not None]
        dmas = list(self.dma_since_barrier)
        self.dma_since_barrier = []
        for e in self.ENGS:
            op = Op(e, None, False)
            op.pos = self.cnt[e]
            self.cnt[e] += 1
            deps = set(lasts) | set(dmas)
            op.deps = self._filter(op, deps)
            self.ops.append(op)
            self.last_on[e] = len(self.ops) - 1

    def emit(self, nc, stack):
        eng_sem = {e: stack.enter_context(nc.semaphore("sem_" + e)) for e in ("pe", "act", "dve", "pool")}
        NDS = 20
        dsems = {q: [stack.enter_context(nc.semaphore("dq_%s_%d" % (q, i))) for i in range(NDS)]
                 for q in ("sp", "pool", "act")}
        duse = {q: [0] * NDS for q in dsems}
        dnext = {q: 0 for q in dsems}
        ccount = {e: 0 for e in eng_sem}
        prev_wait = {}
        for i, op in enumerate(self.ops):
            if op.fn is None:
                continue
            if op.dma:
                q = op.eng
                k = dnext[q]
                dnext[q] = (k + 1) % NDS
                op.sem = dsems[q][k]
                prev_wait[i] = (op.sem, duse[q][k] * 16)
                duse[q][k] += 1
                op.val = duse[q][k] * 16
            elif op.sig:
                ccount[op.eng] += 1
                op.sem = eng_sem[op.eng]
                op.val = ccount[op.eng]
        per_eng = {e: [] for e in self.ENGS}
        for i, op in enumerate(self.ops):
            per_eng[op.eng].append(i)
        ops = self.ops

        def run(e, handle):
            seen = {}
            for i in per_eng[e]:
                op = ops[i]
                waits = {}
                for d in op.deps:
                    p = ops[d]
                    if p.sem is None:
                        continue
                    key = p.sem
                    if waits.get(key, (None, 0))[1] < p.val:
                        waits[key] = (p.sem, p.val)
                if i in prev_wait:
                    s_, v_ = prev_wait[i]
                    if v_ > 0 and waits.get(s_, (None, 0))[1] < v_:
                        waits[s_] = (s_, v_)
                for key, (s_, v_) in waits.items():
                    if seen.get(key, 0) >= v_:
                        continue
                    seen[key] = v_
                    handle.wait_ge(s_, v_)
                if op.fn is None:
                    continue
                inst = op.fn(handle)
                if op.dma:
                    inst.then_inc(op.sem, 16)
                elif op.sig:
                    inst.then_inc(op.sem, 1)

        with nc.Block() as block:
            @block.tensor
            def _(h):
                run("pe", h)

            @block.scalar
            def _(h):
                run("act", h)

            @block.vector
            def _(h):
                run("dve", h)

            @block.gpsimd
            def _(h):
                run("pool", h)

            @block.sync
            def _(h):
                run("sp", h)


def bcast_ap(ap, pattern):
    return bass.AP(tensor=ap.tensor, offset=ap.offset, ap=[list(ap.ap[0])] + [list(p) for p in pattern])


def build(n_layers=DEPTH, first_layer=0):
    nc = bass.Bass("TRN2", target_bir_lowering=False)
    dt = nc.dram_tensor
    x_d = dt("x", [S, D], F32, kind="ExternalInput").ap()
    win_d = dt("win", [DEPTH, 8, 128, 3072], F32, kind="ExternalInput").ap()
    wout_d = dt("wout", [DEPTH, 3, 128, 3072], F32, kind="ExternalInput").ap()
    wffn_d = dt("wffn", [DEPTH, NCH, 128, 3072], F32, kind="ExternalInput").ap()
    bias_d = dt("biasT", [12, 128, TW], F32, kind="ExternalInput").ap()
    gcols_d = dt("gcols", [128, GC], F32, kind="ExternalInput").ap()
    lam_d = dt("lamb", [128, DEPTH * 256], F32, kind="ExternalInput").ap()
    consts_d = dt("consts", [128, 384], F32, kind="ExternalInput").ap()
    cnte_d = dt("cnte", [65, 8 * 72], F32, kind="ExternalInput").ap()
    cnto_d = dt("cnto", [65, 8 * 8], F32, kind="ExternalInput").ap()
    kind_d = dt("kind", [8, S], F32, kind="ExternalInput").ap()
    out_d = dt("out", [S, D], F32, kind="ExternalOutput").ap()

    st = ExitStack()
    sb = lambda name, shape, dtype: st.enter_context(nc.sbuf_tensor(name, shape, dtype))
    x = sb("x_sb", [128, NT, D], F32)
    hT = sb("hT", [128, 8, S], BF16)
    yT = sb("yT", [128, 8, S], BF16)
    qk = sb("qk", [128, 6, S], BF16)
    V = sb("V", [128, 2, NT, 128], BF16)
    bias = sb("bias", [128, TW], F32)
    ft = sb("ft", [128, 6, 512], F32)
    bf = sb("bf", [128, 4, 512], BF16)
    wsl = sb("wsl", [128, 3, 3072], BF16)
    gcols = sb("gcols_sb", [128, GC], F32)
    consts = sb("consts_sb", [128, 384], BF16)
    cnte = sb("cnte_sb", [65, 8 * 72], BF16)
    cnto = sb("cnto_sb", [65, 8 * 8], BF16)
    ind = sb("ind", [65, 256], BF16)
    kdiff = sb("kdiff", [128, 64], BF16)
    km = sb("km", [128, 8], F32)
    small = sb("small", [128, 64], F32)
    ps = [st.enter_context(nc.psum_tensor("ps%d" % i, [128, 512], F32)) for i in range(8)]

    ident = consts[:, 0:128]
    ones = consts[:, 128:256]
    bones = consts[:, 256:384]
    SS, RSTD, EPSC, LAM, NLAM, GSUB, TMPC = 0, 16, 32, 33, 37, 41, 45
    eps_col = small[:, EPSC:EPSC + 1]

    P = Prog()

    def A(eng, fn, r=(), w=(), dma=False):
        return P.add(eng, fn, r, w, dma)

    def mm(out, lhsT, rhs, start, stop, r, w):
        A("pe", lambda e: e.matmul(out, lhsT, rhs, start=start, stop=stop), r, w)

    misc_i = [0]

    def misc():
        misc_i[0] ^= 1
        return 6 + misc_i[0]

    wslot_i = [0]

    def load_w(src):
        k = wslot_i[0] % 3
        wslot_i[0] += 1
        A("pool", lambda e: e.dma_start(out=wsl[:, k, :], in_=src), w=[("w", k)], dma=True)
        return k

    def gc(l, base, j=0):
        c = l * GC_L + base + j
        return gcols[:, c:c + 1]

    A("sp", lambda e: e.dma_start(out=gcols[:], in_=gcols_d), w=["gcols"], dma=True)
    A("sp", lambda e: e.dma_start(out=ft[:, 0:2, :].rearrange("p a b -> p (a b)"), in_=lam_d), w=[("f", 0), ("f", 1)], dma=True)
    A("pool", lambda e: e.dma_start(out=consts[:], in_=consts_d), w=["consts"], dma=True)
    A("pool", lambda e: e.dma_start(out=cnte[:], in_=cnte_d), w=["cnt"], dma=True)
    A("pool", lambda e: e.dma_start(out=cnto[:], in_=cnto_d), w=["cnt"], dma=True)
    for i in range(NT):
        A("sp", lambda e, i=i: e.dma_start(out=x[:, i, :], in_=x_d[i * 128:(i + 1) * 128, :]), w=[("x", i)], dma=True)
    for t in range(2, 6):
        A("pool", lambda e, t=t: e.memset(qk[:, t, :], 0.0), w=[("qk", t, tt) for tt in range(4)])
    A("pool", lambda e: e.dma_start(out=qk[64:72, 4, :], in_=kind_d), w=[("qk", 4, tt) for tt in range(4)], dma=True)
    A("pool", lambda e: e.dma_start(out=qk[0:8, 5, :], in_=kind_d), w=[("qk", 5, tt) for tt in range(4)], dma=True)
    A("dve", lambda e: e.memset(small[:], 0.0), w=["small"])
    A("dve", lambda e: e.memset(eps_col, EPS), w=["small"])
    A("dve", lambda e: e.memset(ind[:], 1.0), w=["ind"])
    lamt = ft[:, 0:2, :].rearrange("p a b -> p (a b)")
    for l in range(DEPTH):
        lam_init = 0.8 - 0.6 * math.exp(-0.3 * l)
        b0 = l * 256
        for t in range(2):
            A("dve", lambda e, b0=b0, t=t: e.tensor_tensor(out=ft[:, 2, t * 64:(t + 1) * 64], in0=lamt[:, b0 + t * 128:b0 + t * 128 + 64],
                                                          in1=lamt[:, b0 + t * 128 + 64:b0 + t * 128 + 128], op=ALU.mult),
              r=[("f", 0), ("f", 1)], w=[("f", 2)])
            A("dve", lambda e, t=t: e.reduce_sum(out=small[:, TMPC + t:TMPC + t + 1], in_=ft[:, 2, t * 64:(t + 1) * 64], axis=AX.X),
              r=[("f", 2)], w=["small"])
        A("act", lambda e: e.activation(out=small[:, TMPC + 2:TMPC + 4], in_=small[:, TMPC:TMPC + 2], func=AF.Exp), r=["small"], w=["small"])
        A("dve", lambda e, l=l: e.tensor_tensor(out=small[:, LAM + l:LAM + l + 1], in0=small[:, TMPC + 2:TMPC + 3],
                                                in1=small[:, TMPC + 3:TMPC + 4], op=ALU.subtract), r=["small"], w=["small"])
        A("dve", lambda e, l=l, li=lam_init: e.tensor_scalar(out=small[:, NLAM + l:NLAM + l + 1], in0=small[:, LAM + l:LAM + l + 1],
                                                             scalar1=li, scalar2=-1.0, op0=ALU.add, op1=ALU.mult), r=["small"], w=["small"])
        A("dve", lambda e, l=l, li=lam_init: e.tensor_scalar(out=small[:, GSUB + l:GSUB + l + 1], in0=gc(l, GC_SUB),
                                                             scalar1=1.0 - li, scalar2=0.0, op0=ALU.mult, op1=ALU.add),
          r=["small", "gcols"], w=["small"])

    def norm_to_hT(l, gbase):
        junk = ft[:, 4:6, :].rearrange("p a b -> p (a b)")
        A("dve", lambda e: e.memset(small[:, SS:SS + 16], 0.0), w=[("ss", i) for i in range(NT)])
        for i in range(NT):
            A("act", lambda e, i=i: e.activation(out=junk, in_=x[:, i, :], func=AF.Square, accum_out=small[:, SS + i:SS + i + 1]),
              r=[("x", i)], w=[("f", 4), ("f", 5), ("ss", i)])
        A("act", lambda e: e.activation(out=small[:, RSTD:RSTD + 16], in_=small[:, SS:SS + 16], func=AF.Ln, scale=1.0 / D, bias=eps_col),
          r=[("ss", i) for i in range(NT)] + ["small"], w=["rstd"])
        A("act", lambda e: e.activation(out=small[:, RSTD:RSTD + 16], in_=small[:, RSTD:RSTD + 16], func=AF.Exp, scale=-0.5),
          r=["rstd"], w=["rstd"])
        hb = yT[:].rearrange("p c s -> p (c s)")
        for i in range(NT):
            A("dve", lambda e, i=i: e.tensor_scalar(out=hb[:, i * D:(i + 1) * D], in0=x[:, i, :], scalar1=small[:, RSTD + i:RSTD + i + 1],
                                                    scalar2=0.0, op0=ALU.mult, op1=ALU.add),
              r=[("x", i), "rstd"], w=[("yT", i // 2)])
        for c in range(8):
            for half in range(2):
                b = misc()
                pst = ps[b][:].bitcast(BF16)
                for ii in range(8):
                    i = half * 8 + ii
                    A("pe", lambda e, i=i, ii=ii, c=c, pst=pst: e.transpose(out=pst[:, ii * 128:(ii + 1) * 128],
                                                                            in_=hb[:, i * D + c * 128:i * D + (c + 1) * 128], identity=ident),
                      r=[("yT", i // 2), "consts"], w=[("ps", b)])
                A("act", lambda e, c=c, half=half, pst=pst: e.activation(out=hT[:, c, half * 1024:(half + 1) * 1024], in_=pst[:, 0:1024],
                                                                         func=AF.Identity, scale=gc(l, gbase, c)),
                  r=[("ps", b), "gcols"], w=[("hT", c, half * 2), ("hT", c, half * 2 + 1)])

    def proj_qk(l, slot, g, gq_idx, dests):
        for tt in range(4):
            b = misc()
            for kc in range(8):
                mm(ps[b][:], wsl[:, slot, kc * 384 + g * 128:kc * 384 + (g + 1) * 128], hT[:, kc, tt * 512:(tt + 1) * 512],
                   kc == 0, kc == 7, [("w", slot), ("hT", kc, tt)], [("ps", b)])
            A("act", lambda e, b=b: e.activation(out=bf[:, 3, :], in_=ps[b][:], func=AF.Square), r=[("ps", b)], w=[("bf", 3)])
            b2 = misc()
            mm(ps[b2][:], bones, bf[:, 3, :], True, True, [("bf", 3), "consts"], [("ps", b2)])
            A("act", lambda e, b2=b2: e.activation(out=ft[:, 5, :], in_=ps[b2][:], func=AF.Ln, scale=1.0 / 64, bias=eps_col),
              r=[("ps", b2), "small"], w=[("f", 5)])
            A("act", lambda e: e.activation(out=ft[:, 5, :], in_=ft[:, 5, :], func=AF.Exp, scale=-0.5), r=[("f", 5)], w=[("f", 5)])
            for (r0, r1, t) in dests:
                A("dve", lambda e, b=b, r0=r0, r1=r1, t=t, tt=tt: e.scalar_tensor_tensor(
                    out=qk[r0:r1, t, tt * 512:(tt + 1) * 512], in0=ps[b][r0:r1, :], scalar=gc(l, GC_QK, gq_idx)[r0:r1, :],
                    in1=ft[r0:r1, 5, :], op0=ALU.mult, op1=ALU.mult),
                  r=[("ps", b), ("f", 5), "gcols"], w=[("qk", t, tt)])

    def proj_v(slot, vb):
        for i4 in range(4):
            b = misc()
            for ii in range(4):
                i = i4 * 4 + ii
                for kc in range(8):
                    mm(ps[b][:, ii * 128:(ii + 1) * 128], hT[:, kc, i * 128:(i + 1) * 128], wsl[:, slot, kc * 384 + 256:kc * 384 + 384],
                       kc == 0, kc == 7, [("w", slot), ("hT", kc, i // 4)], [("ps", b)])
            A("act", lambda e, b=b, i4=i4, vb=vb: e.activation(out=V[:, vb, i4 * 4:(i4 + 1) * 4, :].rearrange("p a b -> p (a b)"),
                                                               in_=ps[b][:], func=AF.Copy),
              r=[("ps", b)], w=[("V", vb, i4)])

    def gating(r, qt, kt):
        r0 = r * 64
        kview = qk[r0:r0 + 64, kt, :].rearrange("p (n j) -> p n j", j=256)
        A("dve", lambda e: e.reduce_sum(out=km[r0:r0 + 64, :], in_=kview, axis=AX.X),
          r=[("qk", kt, tt) for tt in range(4)], w=["km"])
        kmv = km[r0:r0 + 64, :]
        A("dve", lambda e: e.tensor_tensor(out=kdiff[r0:r0 + 64, :].rearrange("p (a b) -> p a b", b=8),
                                           in0=bcast_ap(kmv, [[0, 8], [1, 8]]), in1=bcast_ap(kmv, [[1, 8], [0, 8]]), op=ALU.subtract),
          r=["km"], w=["kdiff"])
        for blk in range(8):
            b = misc()
            mm(ps[b][0:64, 0:256], kdiff[r0:r0 + 64, :], qk[r0:r0 + 64, qt, blk * 256:(blk + 1) * 256], True, True,
               ["kdiff", ("qk", qt, blk // 2)], [("ps", b)])
            A("dve", lambda e, b=b: e.tensor_single_scalar(out=ind[0:64, :], in_=ps[b][0:64, 0:256], scalar=0.0, op=ALU.is_gt),
              r=[("ps", b)], w=["ind"])
            b2 = misc()
            if r == 0:
                mm(ps[b2][0:72, 0:256], cnte[:, blk * 72:(blk + 1) * 72], ind[:], True, True, ["ind", "cnt"], [("ps", b2)])
                m0, m1 = 64, 72
            else:
                mm(ps[b2][0:8, 0:256], cnto[:, blk * 8:(blk + 1) * 8], ind[:], True, True, ["ind", "cnt"], [("ps", b2)])
                m0, m1 = 0, 8
            A("dve", lambda e, b2=b2, m0=m0, m1=m1, blk=blk: e.tensor_scalar(
                out=qk[m0:m1, qt, blk * 256:(blk + 1) * 256], in0=ps[b2][m0:m1, 0:256], scalar1=2.5, scalar2=NEG,
                op0=ALU.is_ge, op1=ALU.mult), r=[("ps", b2)], w=[("qk", qt, blk // 2)])

    grp_i = [0]
    e_i = [0]
    lt_i = [0]

    def attention(l, subs, finalize, sub_outer=False):
        tiles = []
        if sub_outer:
            order = [(j, si) for si in range(len(subs)) for j in range(4)]
        else:
            order = [(j, si) for j in range(4) for si in range(len(subs))]
        for (j, si) in order:
            n = 4 * j + 4
            for i in range(n):
                tiles.append((j, si, i, n))
        cur_bias = [None]
        state = {}

        def emit_S(t):
            j, si, i, n = tiles[t]
            sdef = subs[si]
            sbank = t % 2
            qt, q0, q1 = sdef["q"]
            kt, k0, k1 = sdef["k"]
            mm(ps[sbank][:], qk[k0:k1, kt, i * 128:(i + 1) * 128], qk[q0:q1, qt, j * 512:(j + 1) * 512], True, True,
               [("qk", kt, i // 4), ("qk", qt, j)], [("ps", sbank)])

        def emit_rest(t):
            j, si, i, n = tiles[t]
            sdef = subs[si]
            sbank = t % 2
            if i == 0:
                state[(j, si)] = grp_i[0] % 2
                grp_i[0] += 1
            gset = state[(j, si)]
            ob, zb = 2 + 2 * gset, 3 + 2 * gset
            hd = sdef["bias_head"]
            if cur_bias[0] != hd:
                A("sp", lambda e, hd=hd: e.dma_start(out=bias[:], in_=bias_d[hd]), w=["bias"], dma=True)
                cur_bias[0] = hd
            o = 512 * j - 128 * i
            eb = e_i[0] % 3
            e_i[0] += 1
            if o >= 917:
                A("act", lambda e, sbank=sbank, eb=eb: e.activation(out=bf[:, eb, :], in_=ps[sbank][:], func=AF.Exp, scale=SCALE,
                                                                    bias=bias[:, TW - 1:TW]),
                  r=[("ps", sbank), "bias"], w=[("bf", eb)])
            else:
                lt = lt_i[0] % 2
                lt_i[0] += 1
                c0 = o + TOFF
                A("dve", lambda e, sbank=sbank, lt=lt, c0=c0: e.scalar_tensor_tensor(
                    out=ft[:, lt, :], in0=ps[sbank][:], scalar=SCALE, in1=bias[:, c0:c0 + 512], op0=ALU.mult, op1=ALU.add),
                  r=[("ps", sbank), "bias"], w=[("f", lt)])
                A("act", lambda e, lt=lt, eb=eb: e.activation(out=bf[:, eb, :], in_=ft[:, lt, :], func=AF.Exp),
                  r=[("f", lt)], w=[("bf", eb)])
            vb, vc = sdef["vb"], sdef["vcol0"]
            mm(ps[ob][:], V[:, vb, i, vc:vc + 128], bf[:, eb, :], i == 0, i == n - 1, [("V", vb, i // 4), ("bf", eb)], [("ps", ob)])
            mm(ps[zb][:], ones, bf[:, eb, :], i == 0, i == n - 1, [("bf", eb), "consts"], [("ps", zb)])
            if i == n - 1:
                finalize(j, si, ob, zb)

        T = len(tiles)
        emit_S(0)
        if T > 1:
            emit_S(1)
        for t in range(T):
            emit_rest(t)
            if t + 2 < T:
                emit_S(t + 2)

    def layer(l):
        norm_to_hT(l, GC_LN1)
        units = []
        for h in range(4):
            units.append(("d", h))
            units.append(("m", h))
        slots = {}
        slots[0] = load_w(win_d[l, 0])
        slots[1] = load_w(win_d[l, 4])
        for ui, (kind, h) in enumerate(units):
            slot = slots[ui]
            vb = ui % 2
            if kind == "d":
                proj_qk(l, slot, 0, 0, [(0, 128, 0)])
                proj_qk(l, slot, 1, 1, [(0, 128, 1)])
                proj_v(slot, vb)
            else:
                proj_qk(l, slot, 0, 2, [(0, 64, 2), (64, 128, 3)])
                proj_qk(l, slot, 1, 3, [(0, 64, 4), (64, 128, 5)])
                proj_v(slot, vb)
            nxt = ui + 2
            if nxt < 8:
                k2, h2 = units[nxt]
                slots[nxt] = load_w(win_d[l, h2 if k2 == "d" else 4 + h2])
            elif nxt == 8:
                slots[8] = load_w(wout_d[l, 0])
            elif nxt == 9:
                slots[9] = load_w(wout_d[l, 1])
            if kind == "d":
                subs = [dict(q=(0, m * 64, m * 64 + 64), k=(1, m * 64, m * 64 + 64), vb=vb, vcol0=0, bias_head=h) for m in range(2)]

                def fin(j, si, ob, zb, h=h):
                    cs = slice(j * 512, (j + 1) * 512)
                    A("dve", lambda e: e.reciprocal(out=ft[:, 2, :], in_=ps[zb][:]), r=[("ps", zb)], w=[("f", 2)])
                    if si == 0:
                        A("dve", lambda e: e.tensor_tensor(out=ft[:, 3, :], in0=ps[ob][:], in1=ft[:, 2, :], op=ALU.mult),
                          r=[("ps", ob), ("f", 2)], w=[("f", 3)])
                    else:
                        A("dve", lambda e: e.tensor_tensor(out=ft[:, 2, :], in0=ps[ob][:], in1=ft[:, 2, :], op=ALU.mult),
                          r=[("ps", ob), ("f", 2)], w=[("f", 2)])
                        A("dve", lambda e: e.scalar_tensor_tensor(out=ft[:, 3, :], in0=ft[:, 2, :], scalar=small[:, NLAM + l:NLAM + l + 1],
                                                                  in1=ft[:, 3, :], op0=ALU.mult, op1=ALU.add),
                          r=[("f", 2), ("f", 3), "small"], w=[("f", 3)])
                        A("act", lambda e: e.activation(out=bf[:, 3, :], in_=ft[:, 3, :], func=AF.Square), r=[("f", 3)], w=[("bf", 3)])
                        b2 = misc()
                        mm(ps[b2][:], ones, bf[:, 3, :], True, True, [("bf", 3), "consts"], [("ps", b2)])
                        A("act", lambda e: e.activation(out=ft[:, 4, :], in_=ps[b2][:], func=AF.Ln, scale=1.0 / 128, bias=eps_col),
                          r=[("ps", b2), "small"], w=[("f", 4)])
                        A("act", lambda e: e.activation(out=ft[:, 4, :], in_=ft[:, 4, :], func=AF.Exp, scale=-0.5), r=[("f", 4)], w=[("f", 4)])
                        A("dve", lambda e: e.scalar_tensor_tensor(out=yT[:, h, cs], in0=ft[:, 3, :], scalar=small[:, GSUB + l:GSUB + l + 1],
                                                                  in1=ft[:, 4, :], op0=ALU.mult, op1=ALU.mult),
                          r=[("f", 3), ("f", 4), "small"], w=[("yT", h)])
                attention(l, subs, fin)
            else:
                gating(0, 2, 4)
                gating(1, 3, 5)
                subs = [dict(q=(2 + r, 0, 128), k=(4 + r, 0, 128), vb=vb, vcol0=0, bias_head=4 + 2 * h + r) for r in range(2)]

                def fin(j, si, ob, zb, h=h):
                    cs = slice(j * 512, (j + 1) * 512)
                    r0 = si * 64
                    A("dve", lambda e: e.reciprocal(out=ft[r0:r0 + 64, 2, :], in_=ps[zb][r0:r0 + 64, :]), r=[("ps", zb)], w=[("f", 2)])
                    A("dve", lambda e: e.tensor_tensor(out=yT[r0:r0 + 64, 4 + h, cs], in0=ps[ob][r0:r0 + 64, :], in1=ft[r0:r0 + 64, 2, :],
                                                       op=ALU.mult), r=[("ps", ob), ("f", 2)], w=[("yT", 4 + h)])
                attention(l, subs, fin, sub_outer=True)
        slots[10] = load_w(wout_d[l, 2])
        for s3 in range(3):
            slot = slots[8 + s3]
            ns = 384 if s3 < 2 else 256
            d0 = s3 * 384
            for i in range(NT):
                b = misc()
                for fc in range(8):
                    mm(ps[b][:, 0:ns], yT[:, fc, i * 128:(i + 1) * 128], wsl[:, slot, fc * 384:fc * 384 + ns], fc == 0, fc == 7,
                       [("yT", fc), ("w", slot)], [("ps", b)])
                A("dve", lambda e, b=b, i=i, d0=d0, ns=ns: e.tensor_tensor(out=x[:, i, d0:d0 + ns], in0=ps[b][:, 0:ns], in1=x[:, i, d0:d0 + ns],
                                                                           op=ALU.add), r=[("ps", b), ("x", i)], w=[("x", i)])
        fslots = {}
        fslots[0] = load_w(wffn_d[l, 0])
        fslots[1] = load_w(wffn_d[l, 1])
        norm_to_hT(l, GC_LN2)
        P.barrier()
        qkf = qk[:].rearrange("p a b -> p (a b)").bitcast(F32)
        U = [qkf[:, 0:2050], qkf[:, 2052:4102]]
        for br in range(2):
            A("dve", lambda e, br=br: e.memset(U[br][:, 0:2], 0.0), w=[("U", br, 0)])
        ub = [0]
        for j in range(NCH):
            slot = fslots[j]
            for tt in range(4):
                tb = []
                for br in range(2):
                    b = ub[0] % 4
                    ub[0] += 1
                    for kc in range(8):
                        mm(ps[b][:], wsl[:, slot, kc * 256 + br * 128:kc * 256 + (br + 1) * 128], hT[:, kc, tt * 512:(tt + 1) * 512],
                           kc == 0, kc == 7, [("w", slot), ("hT", kc, tt)], [("ps", b)])
                    A("act", lambda e, b=b, br=br, tt=tt: e.activation(out=U[br][:, 2 + tt * 512:2 + (tt + 1) * 512], in_=ps[b][:], func=AF.Copy),
                      r=[("ps", b)], w=[("U", br, tt)])
                    tmp = br * 2
                    cw0, cw1, cw2 = [gc(l, GC_CW, k * 44 + br * NCH + j) for k in range(3)]
                    cbb = gc(l, GC_CB, br * NCH + j)
                    u2 = U[br][:, 2 + tt * 512:2 + (tt + 1) * 512]
                    u1 = U[br][:, 1 + tt * 512:1 + (tt + 1) * 512]
                    u0 = U[br][:, tt * 512:(tt + 1) * 512]
                    A("dve", lambda e, tmp=tmp, u2=u2, cw2=cw2, cbb=cbb: e.tensor_scalar(
                        out=ft[:, tmp, :], in0=u2, scalar1=cw2, scalar2=cbb,
                        op0=ALU.mult, op1=ALU.add), r=[("U", br, tt), "gcols"], w=[("f", tmp)])
                    A("dve", lambda e, tmp=tmp, u1=u1, cw1=cw1: e.scalar_tensor_tensor(
                        out=ft[:, tmp + 1, :], in0=u1, scalar=cw1, in1=ft[:, tmp, :],
                        op0=ALU.mult, op1=ALU.add), r=[("U", br, tt), ("U", br, max(tt - 1, 0)), ("f", tmp), "gcols"], w=[("f", tmp + 1)])
                    A("dve", lambda e, tmp=tmp, u0=u0, cw0=cw0: e.scalar_tensor_tensor(
                        out=ft[:, tmp, :], in0=u0, scalar=cw0, in1=ft[:, tmp + 1, :],
                        op0=ALU.mult, op1=ALU.add), r=[("U", br, tt), ("U", br, max(tt - 1, 0)), ("f", tmp + 1), "gcols"], w=[("f", tmp)])
                    tb.append(tmp)
                A("act", lambda e: e.activation(out=ft[:, 4, :], in_=ft[:, 0, :], func=AF.Silu), r=[("f", 0)], w=[("f", 4)])
                A("dve", lambda e, j=j, tt=tt: e.tensor_tensor(out=yT[:, j % 8, tt * 512:(tt + 1) * 512], in0=ft[:, 4, :], in1=ft[:, 2, :],
                                                               op=ALU.mult), r=[("f", 4), ("f", 2)], w=[("yT", j % 8)])
            if j % 2 == 1:
                for i in range(NT):
                    for dh in range(2):
                        b = 4 + (i * 2 + dh) % 2
                        for jj in (j - 1, j):
                            mm(ps[b][:], yT[:, jj % 8, i * 128:(i + 1) * 128], wsl[:, fslots[jj], 2048 + dh * 512:2048 + (dh + 1) * 512],
                               jj == j - 1, jj == j, [("yT", jj % 8), ("w", fslots[jj])], [("ps", b)])
                        A("dve", lambda e, b=b, i=i, dh=dh: e.tensor_tensor(out=x[:, i, dh * 512:(dh + 1) * 512], in0=ps[b][:],
                                                                             in1=x[:, i, dh * 512:(dh + 1) * 512], op=ALU.add),
                          r=[("ps", b), ("x", i)], w=[("x", i)])
            if j + 2 < NCH:
                fslots[j + 2] = load_w(wffn_d[l, j + 2])
        P.barrier()
        if l + 1 < first_layer + n_layers:
            for t in range(2, 6):
                A("pool", lambda e, t=t: e.memset(qk[:, t, :], 0.0), w=[("qk", t, tt) for tt in range(4)])
            A("pool", lambda e: e.dma_start(out=qk[64:72, 4, :], in_=kind_d), w=[("qk", 4, tt) for tt in range(4)], dma=True)
            A("pool", lambda e: e.dma_start(out=qk[0:8, 5, :], in_=kind_d), w=[("qk", 5, tt) for tt in range(4)], dma=True)

    for l in range(first_layer, first_layer + n_layers):
        layer(l)

    for i in range(NT):
        A("sp", lambda e, i=i: e.dma_start(out=out_d[i * 128:(i + 1) * 128, :], in_=x[:, i, :]), r=[("x", i)], w=[("out", i)], dma=True)
    A("sp", None, r=[("out", i) for i in range(NT)])
    P.ops[-1].fn = None

    P.emit(nc, st)
    st.close()
    return nc


def rel_bucket_np(dist):
    n = np.maximum(dist, 0)
    nf = np.maximum(n, 16).astype(np.float32)
    large = 16 + (np.log(nf / np.float32(16)) / np.float32(math.log(1024 / 16)) * np.float32(16)).astype(np.int32)
    large = np.minimum(large, 31)
    return np.where(n < 16, n, large)


def host_layout(inp):
    f = lambda a: np.ascontiguousarray(np.asarray(a, dtype=np.float32))
    w_in, w_out, w_up, w_down = f(inp["w_in"]), f(inp["w_out"]), f(inp["w_up"]), f(inp["w_down"])
    L = DEPTH
    win = np.zeros((L, 8, 128, 8, 384), np.float32)
    for u in range(8):
        if u < 4:
            cols = np.concatenate([np.arange(128) + 128 * u, 512 + np.arange(128) + 128 * u, 1024 + np.arange(128) + 128 * u])
        else:
            p = u - 4
            cols = np.concatenate([1536 + np.arange(128) + 128 * p, 2048 + np.arange(128) + 128 * p, 2560 + np.arange(128) + 128 * p])
        win[:, u] = w_in[:, :, cols].reshape(L, 8, 128, 384).transpose(0, 2, 1, 3)
    wout = np.zeros((L, 3, 128, 8, 384), np.float32)
    for s3 in range(3):
        ns = 384 if s3 < 2 else 256
        wout[:, s3, :, :, :ns] = w_out[:, :, s3 * 384:s3 * 384 + ns].reshape(L, 8, 128, ns).transpose(0, 2, 1, 3)
    wffn = np.zeros((L, NCH, 128, 3072), np.float32)
    for j in range(NCH):
        cols = np.concatenate([np.arange(128) + 128 * j, DFF + np.arange(128) + 128 * j])
        wffn[:, j, :, :2048] = w_up[:, :, cols].reshape(L, 8, 128, 256).transpose(0, 2, 1, 3).reshape(L, 128, 2048)
        wffn[:, j, :, 2048:] = w_down[:, j * 128:(j + 1) * 128, :]
    rb = f(inp["rel_bias"])
    pp = np.arange(128)[:, None]
    cc = np.arange(TW)[None, :]
    dist = cc - pp - TOFF
    bidx = rel_bucket_np(dist)
    biasT = np.empty((12, 128, TW), np.float32)
    for h in range(12):
        biasT[h] = np.where(dist >= 0, rb[bidx, h], np.float32(NEG))
    gcols = np.zeros((128, GC), np.float32)
    ln1, ln2 = f(inp["ln_attn_g"]), f(inp["ln_ffn_g"])
    qkg, sub = f(inp["qk_norm_g"]), f(inp["diff_subln_g"])
    cw, cb = f(inp["conv_w"]), f(inp["conv_b"])
    for l in range(L):
        o = l * GC_L
        gcols[:, o + GC_LN1:o + GC_LN1 + 8] = ln1[l].reshape(8, 128).T
        gcols[:, o + GC_LN2:o + GC_LN2 + 8] = ln2[l].reshape(8, 128).T
        for k in range(4):
            gcols[:, o + GC_QK + k] = np.tile(qkg[l, k], 2)
        gcols[:, o + GC_SUB] = sub[l]
        for k in range(3):
            gcols[:, o + GC_CW + k * 44:o + GC_CW + (k + 1) * 44] = cw[l, k].reshape(44, 128).T
        gcols[:, o + GC_CB:o + GC_CB + 44] = cb[l].reshape(44, 128).T
    lamb = np.broadcast_to(f(inp["diff_lambda"]).reshape(1, L * 256), (128, L * 256)).copy()
    consts = np.zeros((128, 384), np.float32)
    consts[:, 0:128] = np.eye(128, dtype=np.float32)
    consts[:, 128:256] = 1.0
    consts[0:64, 256:320] = 1.0
    consts[64:128, 320:384] = 1.0
    cnte = np.zeros((65, 8, 72), np.float32)
    cnto = np.zeros((65, 8, 8), np.float32)
    for b in range(8):
        for n in range(8):
            c = 0.0 if n < b else (-10.0 if n == b else 10.0)
            cnte[64, b, 64 + n] = c
            cnto[64, b, n] = c
            if n < b:
                for n2 in range(b):
                    cnte[n * 8 + n2, b, 64 + n] = 1.0
                    cnto[n * 8 + n2, b, n] = 1.0
    kind = np.zeros((8, S), np.float32)
    for n in range(8):
        kind[n, n * 256:(n + 1) * 256] = 1.0
    shared = dict(win=win.reshape(L, 8, 128, 3072), wout=wout.reshape(L, 3, 128, 3072), wffn=wffn, biasT=biasT, gcols=gcols,
                  lamb=lamb, consts=consts, cnte=cnte.reshape(65, 576), cnto=cnto.reshape(65, 64), kind=kind)
    return shared


_NC_CACHE = {}


def kernel(**inputs):
    x = np.ascontiguousarray(np.asarray(inputs["x"], dtype=np.float32))
    shared = host_layout(inputs)
    if "nc" not in _NC_CACHE:
        _NC_CACHE["nc"] = build(DEPTH, 0)
    nc = _NC_CACHE["nc"]
    in_maps = [dict(shared, x=x[b]) for b in range(8)]
    res = run_bass_kernel_spmd(nc, in_maps, core_ids=list(range(8)))
    return np.stack([np.asarray(r["out"], dtype=np.float32) for r in res.results], axis=0)
```

```python
import math
from contextlib import ExitStack
import numpy as np
import concourse.bass as bass
import concourse.mybir as mybir
from concourse.bass_utils import run_bass_kernel_spmd

F32 = mybir.dt.float32
BF16 = mybir.dt.bfloat16
ALU = mybir.AluOpType
AF = mybir.ActivationFunctionType
AX = mybir.AxisListType

S = 2048
D = 1024
NT = 16
DEPTH = 4
DFF = 2816
NCH = 22
EPS = 1e-6
NEG = -30000.0
TW = 1408
TOFF = 0
SCALE = 0.125

GC_LN1 = 0
GC_LN2 = 8
GC_QK = 16
GC_SUB = 20
GC_CW = 21
GC_CB = 21 + 132
GC_L = 21 + 132 + 44
GC = GC_L * DEPTH


class Op:
    __slots__ = ("eng", "fn", "deps", "sig", "pos", "dma", "sem", "val")

    def __init__(self, eng, fn, dma):
        self.eng = eng
        self.fn = fn
        self.dma = dma
        self.deps = ()
        self.sig = dma
        self.pos = 0
        self.sem = None
        self.val = 0


class Prog:
    ENGS = ("pe", "act", "dve", "pool", "sp")

    def __init__(self):
        self.ops = []
        self.lastw = {}
        self.readers = {}
        self.cnt = {e: 0 for e in self.ENGS}
        self.last_on = {e: None for e in self.ENGS}
        self.dma_since_barrier = []

    def add(self, eng, fn, r=(), w=(), dma=False):
        op = Op(eng, fn, dma)
        idx = len(self.ops)
        op.pos = self.cnt[eng]
        self.cnt[eng] += 1
        deps = set()
        for k in r:
            lw = self.lastw.get(k)
            if lw is not None:
                deps.add(lw)
        for k in w:
            lw = self.lastw.get(k)
            if lw is not None:
                deps.add(lw)
            rd = self.readers.get(k)
            if rd:
                deps.update(rd[0].values())
                deps.update(rd[1])
        for k in r:
            rd = self.readers.setdefault(k, ({}, []))
            if dma:
                rd[1].append(idx)
            else:
                rd[0][eng] = idx
        for k in w:
            self.lastw[k] = idx
            self.readers[k] = ({}, [])
        deps.discard(idx)
        op.deps = self._filter(op, deps)
        self.ops.append(op)
        self.last_on[eng] = idx
        if dma:
            self.dma_since_barrier.append(idx)
        return idx

    def _filter(self, op, deps):
        out = []
        for d in deps:
            p = self.ops[d]
            if (not p.dma) and p.eng == op.eng and not op.dma:
                if op.eng == "pe":
                    continue
                if op.pos - p.pos > 3:
                    continue
            p.sig = True
            out.append(d)
        return tuple(out)

    def barrier(self):
        lasts = [v for v in self.last_on.values() if v is not None]
        dmas = list(self.dma_since_barrier)
        self.dma_since_barrier = []
        for e in self.ENGS:
            op = Op(e, None, False)
            op.pos = self.cnt[e]
            self.cnt[e] += 1
            deps = set(lasts) | set(dmas)
            op.deps = self._filter(op, deps)
            self.ops.append(op)
            self.last_on[e] = len(self.ops) - 1

    def emit(self, nc, stack):
        eng_sem = {e: stack.enter_context(nc.semaphore("sem_" + e)) for e in ("pe", "act", "dve", "pool")}
        NDS = 20
        dsems = {q: [stack.enter_context(nc.semaphore("dq_%s_%d" % (q, i))) for i in range(NDS)]
                 for q in ("sp", "pool", "act")}
        duse = {q: [0] * NDS for q in dsems}
        dnext = {q: 0 for q in dsems}
        ccount = {e: 0 for e in eng_sem}
        prev_wait = {}
        for i, op in enumerate(self.ops):
            if op.fn is None:
                continue
            if op.dma:
                q = op.eng
                k = dnext[q]
                dnext[q] = (k + 1) % NDS
                op.sem = dsems[q][k]
                prev_wait[i] = (op.sem, duse[q][k] * 16)
                duse[q][k] += 1
                op.val = duse[q][k] * 16
            elif op.sig:
                ccount[op.eng] += 1
                op.sem = eng_sem[op.eng]
                op.val = ccount[op.eng]
        per_eng = {e: [] for e in self.ENGS}
        for i, op in enumerate(self.ops):
            per_eng[op.eng].append(i)
        ops = self.ops

        def run(e, handle):
            seen = {}
            for i in per_eng[e]:
                op = ops[i]
                waits = {}
                for d in op.deps:
                    p = ops[d]
                    if p.sem is None:
                        continue
                    key = p.sem
                    if waits.get(key, (None, 0))[1] < p.val:
                        waits[key] = (p.sem, p.val)
                if i in prev_wait:
                    s_, v_ = prev_wait[i]
                    if v_ > 0 and waits.get(s_, (None, 0))[1] < v_:
                        waits[s_] = (s_, v_)
                for key, (s_, v_) in waits.items():
                    if seen.get(key, 0) >= v_:
                        continue
                    seen[key] = v_
                    handle.wait_ge(s_, v_)
                if op.fn is None:
                    continue
                inst = op.fn(handle)
                if op.dma:
                    inst.then_inc(op.sem, 16)
                elif op.sig:
                    inst.then_inc(op.sem, 1)

        with nc.Block() as block:
            @block.tensor
            def _(h):
                run("pe", h)

            @block.scalar
            def _(h):
                run("act", h)

            @block.vector
            def _(h):
                run("dve", h)

            @block.gpsimd
            def _(h):
                run("pool", h)

            @block.sync
            def _(h):
                run("sp", h)


def bcast_ap(ap, pattern):
    return bass.AP(tensor=ap.tensor, offset=ap.offset, ap=[list(ap.ap[0])] + [list(p) for p in pattern])


def build(n_layers=DEPTH, first_layer=0):
    nc = bass.Bass("TRN2", target_bir_lowering=False)
    dt = nc.dram_tensor
    x_d = dt("x", [S, D], F32, kind="ExternalInput").ap()
    win_d = dt("win", [DEPTH, 8, 128, 3072], F32, kind="ExternalInput").ap()
    wout_d = dt("wout", [DEPTH, 3, 128, 3072], F32, kind="ExternalInput").ap()
    wffn_d = dt("wffn", [DEPTH, NCH, 128, 3072], F32, kind="ExternalInput").ap()
    bias_d = dt("biasT", [12, 128, TW], F32, kind="ExternalInput").ap()
    gcols_d = dt("gcols", [128, GC], F32, kind="ExternalInput").ap()
    lam_d = dt("lamb", [128, DEPTH * 256], F32, kind="ExternalInput").ap()
    consts_d = dt("consts", [128, 384], F32, kind="ExternalInput").ap()
    cnte_d = dt("cnte", [65, 8 * 72], F32, kind="ExternalInput").ap()
    cnto_d = dt("cnto", [65, 8 * 8], F32, kind="ExternalInput").ap()
    kind_d = dt("kind", [8, S], F32, kind="ExternalInput").ap()
    out_d = dt("out", [S, D], F32, kind="ExternalOutput").ap()

    st = ExitStack()
    sb = lambda name, shape, dtype: st.enter_context(nc.sbuf_tensor(name, shape, dtype))
    x = sb("x_sb", [128, NT, D], F32)
    hT = sb("hT", [128, 8, S], BF16)
    yT = sb("yT", [128, 8, S], BF16)
    qk = sb("qk", [128, 6, S], BF16)
    V = sb("V", [128, 1, NT, 128], BF16)
    bias = sb("bias", [128, 2, TW], F32)
    ft = sb("ft", [128, 6, 512], F32)
    bf = sb("bf", [128, 4, 512], BF16)
    wsl = sb("wsl", [128, 3, 3072], BF16)
    gcols = sb("gcols_sb", [128, GC], F32)
    consts = sb("consts_sb", [128, 384], BF16)
    cnte = sb("cnte_sb", [65, 8 * 72], BF16)
    cnto = sb("cnto_sb", [65, 8 * 8], BF16)
    ind = sb("ind", [65, 256], BF16)
    kdiff = sb("kdiff", [128, 64], BF16)
    km = sb("km", [128, 8], F32)
    small = sb("small", [128, 64], F32)
    ps = [st.enter_context(nc.psum_tensor("ps%d" % i, [128, 512], F32)) for i in range(8)]

    ident = consts[:, 0:128]
    ones = consts[:, 128:256]
    bones = consts[:, 256:384]
    SS, RSTD, EPSC, LAM, NLAM, GSUB, TMPC = 0, 16, 32, 33, 37, 41, 45
    eps_col = small[:, EPSC:EPSC + 1]

    P = Prog()

    def A(eng, fn, r=(), w=(), dma=False):
        return P.add(eng, fn, r, w, dma)

    def mm(out, lhsT, rhs, start, stop, r, w):
        A("pe", lambda e: e.matmul(out, lhsT, rhs, start=start, stop=stop), r, w)

    misc_i = [0]

    def misc():
        misc_i[0] = (misc_i[0] + 1) % 8
        return misc_i[0]

    ft_i = [0]

    def ftile():
        ft_i[0] = (ft_i[0] + 1) % 6
        return ft_i[0]

    bf_i = [0]

    def bftile():
        bf_i[0] = (bf_i[0] + 1) % 4
        return bf_i[0]

    bias_seq = []
    bias_loaded = [0]

    def bias_prefetch(upto):
        while bias_loaded[0] < min(upto, len(bias_seq)):
            n = bias_loaded[0]
            hd = bias_seq[n]
            A("sp", lambda e, hd=hd, n=n: e.dma_start(out=bias[:, n % 2, :], in_=bias_d[hd]), w=[("bias", n % 2)], dma=True)
            bias_loaded[0] += 1

    wslot_i = [0]

    def load_w(src):
        k = wslot_i[0] % 3
        wslot_i[0] += 1
        A("pool", lambda e: e.dma_start(out=wsl[:, k, :], in_=src), w=[("w", k)], dma=True)
        return k

    def gc(l, base, j=0):
        c = l * GC_L + base + j
        return gcols[:, c:c + 1]

    A("sp", lambda e: e.dma_start(out=gcols[:], in_=gcols_d), w=["gcols"], dma=True)
    A("sp", lambda e: e.dma_start(out=ft[:, 0:2, :].rearrange("p a b -> p (a b)"), in_=lam_d), w=[("f", 0), ("f", 1)], dma=True)
    A("pool", lambda e: e.dma_start(out=consts[:], in_=consts_d), w=["consts"], dma=True)
    A("pool", lambda e: e.dma_start(out=cnte[:], in_=cnte_d), w=["cnt"], dma=True)
    A("pool", lambda e: e.dma_start(out=cnto[:], in_=cnto_d), w=["cnt"], dma=True)
    for i in range(NT):
        A("sp", lambda e, i=i: e.dma_start(out=x[:, i, :], in_=x_d[i * 128:(i + 1) * 128, :]), w=[("x", i)], dma=True)
    for t in range(2, 6):
        A("pool", lambda e, t=t: e.memset(qk[:, t, :], 0.0), w=[("qk", t, tt) for tt in range(4)])
    A("pool", lambda e: e.dma_start(out=qk[64:72, 4, :], in_=kind_d), w=[("qk", 4, tt) for tt in range(4)], dma=True)
    A("pool", lambda e: e.dma_start(out=qk[0:8, 5, :], in_=kind_d), w=[("qk", 5, tt) for tt in range(4)], dma=True)
    A("dve", lambda e: e.memset(small[:], 0.0), w=["small"])
    A("dve", lambda e: e.memset(eps_col, EPS), w=["small"])
    A("dve", lambda e: e.memset(ind[:], 1.0), w=["ind"])
    lamt = ft[:, 0:2, :].rearrange("p a b -> p (a b)")
    for l in range(DEPTH):
        lam_init = 0.8 - 0.6 * math.exp(-0.3 * l)
        b0 = l * 256
        for t in range(2):
            A("dve", lambda e, b0=b0, t=t: e.tensor_tensor(out=ft[:, 2, t * 64:(t + 1) * 64], in0=lamt[:, b0 + t * 128:b0 + t * 128 + 64],
                                                          in1=lamt[:, b0 + t * 128 + 64:b0 + t * 128 + 128], op=ALU.mult),
              r=[("f", 0), ("f", 1)], w=[("f", 2)])
            A("dve", lambda e, t=t: e.reduce_sum(out=small[:, TMPC + t:TMPC + t + 1], in_=ft[:, 2, t * 64:(t + 1) * 64], axis=AX.X),
              r=[("f", 2)], w=["small"])
        A("act", lambda e: e.activation(out=small[:, TMPC + 2:TMPC + 4], in_=small[:, TMPC:TMPC + 2], func=AF.Exp), r=["small"], w=["small"])
        A("dve", lambda e, l=l: e.tensor_tensor(out=small[:, LAM + l:LAM + l + 1], in0=small[:, TMPC + 2:TMPC + 3],
                                                in1=small[:, TMPC + 3:TMPC + 4], op=ALU.subtract), r=["small"], w=["small"])
        A("dve", lambda e, l=l, li=lam_init: e.tensor_scalar(out=small[:, NLAM + l:NLAM + l + 1], in0=small[:, LAM + l:LAM + l + 1],
                                                             scalar1=li, scalar2=-1.0, op0=ALU.add, op1=ALU.mult), r=["small"], w=["small"])
        A("dve", lambda e, l=l, li=lam_init: e.tensor_scalar(out=small[:, GSUB + l:GSUB + l + 1], in0=gc(l, GC_SUB),
                                                             scalar1=1.0 - li, scalar2=0.0, op0=ALU.mult, op1=ALU.add),
          r=["small", "gcols"], w=["small"])

    def norm_to_hT(l, gbase):
        junk = ft[:, 4:6, :].rearrange("p a b -> p (a b)")
        A("dve", lambda e: e.memset(small[:, SS:SS + 16], 0.0), w=[("ss", i) for i in range(NT)])
        for i in range(NT):
            A("act", lambda e, i=i: e.activation(out=junk, in_=x[:, i, :], func=AF.Square, accum_out=small[:, SS + i:SS + i + 1]),
              r=[("x", i)], w=[("f", 4), ("f", 5), ("ss", i)])
        A("act", lambda e: e.activation(out=small[:, RSTD:RSTD + 16], in_=small[:, SS:SS + 16], func=AF.Ln, scale=1.0 / D, bias=eps_col),
          r=[("ss", i) for i in range(NT)] + ["small"], w=["rstd"])
        A("act", lambda e: e.activation(out=small[:, RSTD:RSTD + 16], in_=small[:, RSTD:RSTD + 16], func=AF.Exp, scale=-0.5),
          r=["rstd"], w=["rstd"])
        hb = yT[:].rearrange("p c s -> p (c s)")
        for i in range(NT):
            A("dve", lambda e, i=i: e.tensor_scalar(out=hb[:, i * D:(i + 1) * D], in0=x[:, i, :], scalar1=small[:, RSTD + i:RSTD + i + 1],
                                                    scalar2=0.0, op0=ALU.mult, op1=ALU.add),
              r=[("x", i), "rstd"], w=[("yT", i // 2)])
        for c in range(8):
            for half in range(2):
                b = misc()
                pst = ps[b][:].bitcast(BF16)
                for ii in range(8):
                    i = half * 8 + ii
                    A("pe", lambda e, i=i, ii=ii, c=c, pst=pst: e.transpose(out=pst[:, ii * 128:(ii + 1) * 128],
                                                                            in_=hb[:, i * D + c * 128:i * D + (c + 1) * 128], identity=ident),
                      r=[("yT", i // 2), "consts"], w=[("ps", b)])
                A("act", lambda e, c=c, half=half, pst=pst: e.activation(out=hT[:, c, half * 1024:(half + 1) * 1024], in_=pst[:, 0:1024],
                                                                         func=AF.Identity, scale=gc(l, gbase, c)),
                  r=[("ps", b), "gcols"], w=[("hT", c, half * 2), ("hT", c, half * 2 + 1)])

    def proj_qk_unit(l, slot, specs):
        chunks = [(g, gq_idx, dests, tt) for (g, gq_idx, dests) in specs for tt in range(4)]
        st_ = {}

        def emit_proj(ci):
            g, gq_idx, dests, tt = chunks[ci]
            b = misc()
            for kc in range(8):
                mm(ps[b][:], wsl[:, slot, kc * 384 + g * 128:kc * 384 + (g + 1) * 128], hT[:, kc, tt * 512:(tt + 1) * 512],
                   kc == 0, kc == 7, [("w", slot), ("hT", kc, tt)], [("ps", b)])
            sq = bftile()
            A("act", lambda e, b=b, sq=sq: e.activation(out=bf[:, sq, :], in_=ps[b][:], func=AF.Square), r=[("ps", b)], w=[("bf", sq)])
            st_[ci] = (b, sq)

        def emit_chain(ci):
            g, gq_idx, dests, tt = chunks[ci]
            b, sq = st_[ci]
            b2 = misc()
            mm(ps[b2][:], bones, bf[:, sq, :], True, True, [("bf", sq), "consts"], [("ps", b2)])
            rs = ftile()
            A("act", lambda e, b2=b2, rs=rs: e.activation(out=ft[:, rs, :], in_=ps[b2][:], func=AF.Ln, scale=1.0 / 64, bias=eps_col),
              r=[("ps", b2), "small"], w=[("f", rs)])
            A("act", lambda e, rs=rs: e.activation(out=ft[:, rs, :], in_=ft[:, rs, :], func=AF.Exp, scale=-0.5), r=[("f", rs)], w=[("f", rs)])
            for (r0, r1, t) in dests:
                A("dve", lambda e, b=b, r0=r0, r1=r1, t=t, tt=tt, rs=rs, gq_idx=gq_idx: e.scalar_tensor_tensor(
                    out=qk[r0:r1, t, tt * 512:(tt + 1) * 512], in0=ps[b][r0:r1, :], scalar=gc(l, GC_QK, gq_idx)[r0:r1, :],
                    in1=ft[r0:r1, rs, :], op0=ALU.mult, op1=ALU.mult),
                  r=[("ps", b), ("f", rs), "gcols"], w=[("qk", t, tt)])

        n = len(chunks)
        emit_proj(0)
        for ci in range(n):
            if ci + 1 < n:
                emit_proj(ci + 1)
            emit_chain(ci)

    def proj_v(slot, vb=0):
        for i4 in range(4):
            b = misc()
            for ii in range(4):
                i = i4 * 4 + ii
                for kc in range(8):
                    mm(ps[b][:, ii * 128:(ii + 1) * 128], hT[:, kc, i * 128:(i + 1) * 128], wsl[:, slot, kc * 384 + 256:kc * 384 + 384],
                       kc == 0, kc == 7, [("w", slot), ("hT", kc, i // 4)], [("ps", b)])
            A("act", lambda e, b=b, i4=i4, vb=vb: e.activation(out=V[:, vb, i4 * 4:(i4 + 1) * 4, :].rearrange("p a b -> p (a b)"),
                                                               in_=ps[b][:], func=AF.Copy),
              r=[("ps", b)], w=[("V", vb, i4)])

    def gating(r, qt, kt):
        r0 = r * 64
        kview = qk[r0:r0 + 64, kt, :].rearrange("p (n j) -> p n j", j=256)
        A("dve", lambda e: e.reduce_sum(out=km[r0:r0 + 64, :], in_=kview, axis=AX.X),
          r=[("qk", kt, tt) for tt in range(4)], w=["km"])
        kmv = km[r0:r0 + 64, :]
        A("dve", lambda e: e.tensor_tensor(out=kdiff[r0:r0 + 64, :].rearrange("p (a b) -> p a b", b=8),
                                           in0=bcast_ap(kmv, [[0, 8], [1, 8]]), in1=bcast_ap(kmv, [[1, 8], [0, 8]]), op=ALU.subtract),
          r=["km"], w=["kdiff"])
        for blk in range(8):
            b = misc()
            mm(ps[b][0:64, 0:256], kdiff[r0:r0 + 64, :], qk[r0:r0 + 64, qt, blk * 256:(blk + 1) * 256], True, True,
               ["kdiff", ("qk", qt, blk // 2)], [("ps", b)])
            A("dve", lambda e, b=b: e.tensor_single_scalar(out=ind[0:64, :], in_=ps[b][0:64, 0:256], scalar=0.0, op=ALU.is_gt),
              r=[("ps", b)], w=["ind"])
            b2 = misc()
            if r == 0:
                mm(ps[b2][0:72, 0:256], cnte[:, blk * 72:(blk + 1) * 72], ind[:], True, True, ["ind", "cnt"], [("ps", b2)])
                m0, m1 = 64, 72
            else:
                mm(ps[b2][0:8, 0:256], cnto[:, blk * 8:(blk + 1) * 8], ind[:], True, True, ["ind", "cnt"], [("ps", b2)])
                m0, m1 = 0, 8
            A("dve", lambda e, b2=b2, m0=m0, m1=m1, blk=blk: e.tensor_scalar(
                out=qk[m0:m1, qt, blk * 256:(blk + 1) * 256], in0=ps[b2][m0:m1, 0:256], scalar1=2.5, scalar2=NEG,
                op0=ALU.is_ge, op1=ALU.mult), r=[("ps", b2)], w=[("qk", qt, blk // 2)])

    grp_i = [0]
    e_i = [0]
    lt_i = [0]
    bias_n = [0]

    def attention(l, subs, finalize, sub_outer=False):
        tiles = []
        if sub_outer:
            order = [(j, si) for si in range(len(subs)) for j in range(4)]
        else:
            order = [(j, si) for j in range(4) for si in range(len(subs))]
        for (j, si) in order:
            n = 4 * j + 4
            for i in range(n):
                tiles.append((j, si, i, n))
        cur = {"hd": None, "slot": 0}
        state = {}
        LOOK = 3

        def c0_of(j, i):
            d = i - 4 * j
            return 128 * d if d > 0 else 0

        def emit_S(t):
            j, si, i, n = tiles[t]
            sdef = subs[si]
            sbank = t % 3
            qt, q0, q1 = sdef["q"]
            kt, k0, k1 = sdef["k"]
            c0 = c0_of(j, i)
            mm(ps[sbank][:, c0:512], qk[k0:k1, kt, i * 128:(i + 1) * 128], qk[q0:q1, qt, j * 512 + c0:(j + 1) * 512], True, True,
               [("qk", kt, i // 4), ("qk", qt, j)], [("ps", sbank)])

        def emit_rest(t):
            j, si, i, n = tiles[t]
            sdef = subs[si]
            sbank = t % 3
            if i == 0:
                state[(j, si)] = grp_i[0] % 2
                grp_i[0] += 1
            gset = state[(j, si)]
            ob, zb = 3 + 2 * gset, 4 + 2 * gset
            hd = sdef["bias_head"]
            if cur["hd"] != hd:
                assert bias_seq[bias_n[0]] == hd, (bias_seq[bias_n[0]], hd)
                cur["hd"] = hd
                cur["slot"] = bias_n[0] % 2
                bias_n[0] += 1
                bias_prefetch(bias_n[0] + 1)
            bsl = cur["slot"]
            o = 512 * j - 128 * i
            c0 = c0_of(j, i)
            eb = e_i[0] % 3
            e_i[0] += 1
            if o >= 917:
                A("act", lambda e, sbank=sbank, eb=eb, bsl=bsl: e.activation(out=bf[:, eb, :], in_=ps[sbank][:], func=AF.Exp, scale=SCALE,
                                                                             bias=bias[:, bsl, TW - 1:TW]),
                  r=[("ps", sbank), ("bias", bsl)], w=[("bf", eb)])
            else:
                lt = lt_i[0] % 3
                lt_i[0] += 1
                tc0 = o + c0 + TOFF
                A("dve", lambda e, sbank=sbank, lt=lt, tc0=tc0, c0=c0, bsl=bsl: e.scalar_tensor_tensor(
                    out=ft[:, lt, c0:512], in0=ps[sbank][:, c0:512], scalar=SCALE, in1=bias[:, bsl, tc0:tc0 + 512 - c0], op0=ALU.mult, op1=ALU.add),
                  r=[("ps", sbank), ("bias", bsl)], w=[("f", lt)])
                A("act", lambda e, lt=lt, eb=eb, c0=c0: e.activation(out=bf[:, eb, c0:512], in_=ft[:, lt, c0:512], func=AF.Exp),
                  r=[("f", lt)], w=[("bf", eb)])
            vc = sdef["vcol0"]
            mm(ps[ob][:, c0:512], V[:, 0, i, vc:vc + 128], bf[:, eb, c0:512], i == 0, i == n - 1, [("V", 0, i // 4), ("bf", eb)], [("ps", ob)])
            mm(ps[zb][:, c0:512], ones, bf[:, eb, c0:512], i == 0, i == n - 1, [("bf", eb), "consts"], [("ps", zb)])
            if i == n - 1:
                finalize(j, si, ob, zb)

        T = len(tiles)
        for t in range(min(LOOK, T)):
            emit_S(t)
        for t in range(T):
            emit_rest(t)
            if t + LOOK < T:
                emit_S(t + LOOK)

    def layer(l):
        norm_to_hT(l, GC_LN1)
        units = []
        for h in range(4):
            units.append(("d", h))
            units.append(("m", h))
        slots = {}
        slots[0] = load_w(win_d[l, 0])
        slots[1] = load_w(win_d[l, 4])
        for (kind, h) in units:
            if kind == "d":
                bias_seq.append(h)
            else:
                bias_seq.extend([4 + 2 * h, 4 + 2 * h + 1])
        bias_prefetch(bias_n[0] + 2)
        for ui, (kind, h) in enumerate(units):
            slot = slots[ui]
            if kind == "d":
                proj_qk_unit(l, slot, [(0, 0, [(0, 128, 0)]), (1, 1, [(0, 128, 1)])])
                proj_v(slot)
            else:
                proj_qk_unit(l, slot, [(0, 2, [(0, 64, 2), (64, 128, 3)]), (1, 3, [(0, 64, 4), (64, 128, 5)])])
                proj_v(slot)
            nxt = ui + 2
            if nxt < 8:
                k2, h2 = units[nxt]
                slots[nxt] = load_w(win_d[l, h2 if k2 == "d" else 4 + h2])
            elif nxt == 8:
                slots[8] = load_w(wout_d[l, 0])
            elif nxt == 9:
                slots[9] = load_w(wout_d[l, 1])

            def recip(zb, dst, r0=0, r1=128):
                A("act", lambda e: e.activation(out=ft[r0:r1, dst, :], in_=ps[zb][r0:r1, :], func=AF.Ln), r=[("ps", zb)], w=[("f", dst)])
                A("act", lambda e: e.activation(out=ft[r0:r1, dst, :], in_=ft[r0:r1, dst, :], func=AF.Exp, scale=-1.0), r=[("f", dst)], w=[("f", dst)])

            if kind == "d":
                subs = [dict(q=(0, m * 64, m * 64 + 64), k=(1, m * 64, m * 64 + 64), vcol0=0, bias_head=h) for m in range(2)]

                def fin(j, si, ob, zb, h=h):
                    cs = slice(j * 512, (j + 1) * 512)
                    recip(zb, 3)
                    if si == 0:
                        A("dve", lambda e: e.tensor_tensor(out=ft[:, 4, :], in0=ps[ob][:], in1=ft[:, 3, :], op=ALU.mult),
                          r=[("ps", ob), ("f", 3)], w=[("f", 4)])
                    else:
                        A("dve", lambda e: e.tensor_tensor(out=ft[:, 3, :], in0=ps[ob][:], in1=ft[:, 3, :], op=ALU.mult),
                          r=[("ps", ob), ("f", 3)], w=[("f", 3)])
                        A("dve", lambda e: e.scalar_tensor_tensor(out=ft[:, 4, :], in0=ft[:, 3, :], scalar=small[:, NLAM + l:NLAM + l + 1],
                                                                  in1=ft[:, 4, :], op0=ALU.mult, op1=ALU.add),
                          r=[("f", 3), ("f", 4), "small"], w=[("f", 4)])
                        A("act", lambda e: e.activation(out=bf[:, 3, :], in_=ft[:, 4, :], func=AF.Square), r=[("f", 4)], w=[("bf", 3)])
                        b2 = 7
                        mm(ps[b2][:], ones, bf[:, 3, :], True, True, [("bf", 3), "consts"], [("ps", b2)])
                        A("act", lambda e: e.activation(out=ft[:, 5, :], in_=ps[b2][:], func=AF.Ln, scale=1.0 / 128, bias=eps_col),
                          r=[("ps", b2), "small"], w=[("f", 5)])
                        A("act", lambda e: e.activation(out=ft[:, 5, :], in_=ft[:, 5, :], func=AF.Exp, scale=-0.5), r=[("f", 5)], w=[("f", 5)])
                        A("dve", lambda e: e.scalar_tensor_tensor(out=yT[:, h, cs], in0=ft[:, 4, :], scalar=small[:, GSUB + l:GSUB + l + 1],
                                                                  in1=ft[:, 5, :], op0=ALU.mult, op1=ALU.mult),
                          r=[("f", 4), ("f", 5), "small"], w=[("yT", h)])
                attention(l, subs, fin)
            else:
                gating(0, 2, 4)
                gating(1, 3, 5)
                subs = [dict(q=(2 + r, 0, 128), k=(4 + r, 0, 128), vcol0=0, bias_head=4 + 2 * h + r) for r in range(2)]

                def fin(j, si, ob, zb, h=h):
                    cs = slice(j * 512, (j + 1) * 512)
                    r0 = si * 64
                    recip(zb, 3, r0, r0 + 64)
                    A("dve", lambda e: e.tensor_tensor(out=yT[r0:r0 + 64, 4 + h, cs], in0=ps[ob][r0:r0 + 64, :], in1=ft[r0:r0 + 64, 3, :],
                                                       op=ALU.mult), r=[("ps", ob), ("f", 3)], w=[("yT", 4 + h)])
                attention(l, subs, fin, sub_outer=True)
        slots[10] = load_w(wout_d[l, 2])
        for s3 in range(3):
            slot = slots[8 + s3]
            ns = 384 if s3 < 2 else 256
            d0 = s3 * 384
            for i in range(NT):
                b = misc()
                for fc in range(8):
                    mm(ps[b][:, 0:ns], yT[:, fc, i * 128:(i + 1) * 128], wsl[:, slot, fc * 384:fc * 384 + ns], fc == 0, fc == 7,
                       [("yT", fc), ("w", slot)], [("ps", b)])
                A("dve", lambda e, b=b, i=i, d0=d0, ns=ns: e.tensor_tensor(out=x[:, i, d0:d0 + ns], in0=ps[b][:, 0:ns], in1=x[:, i, d0:d0 + ns],
                                                                           op=ALU.add), r=[("ps", b), ("x", i)], w=[("x", i)])
        fslots = {}
        fslots[0] = load_w(wffn_d[l, 0])
        fslots[1] = load_w(wffn_d[l, 1])
        norm_to_hT(l, GC_LN2)
        P.barrier()
        qkf = qk[:].rearrange("p a b -> p (a b)").bitcast(F32)
        U = [qkf[:, 0:2050], qkf[:, 2052:4102]]
        for br in range(2):
            A("dve", lambda e, br=br: e.memset(U[br][:, 0:2], 0.0), w=[("U", br, 0)])
        ptmp = bf[:, 0:2, :].rearrange("p a b -> p (a b)").bitcast(F32)
        ub = [0]
        fset = [0]
        for j in range(NCH):
            slot = fslots[j]
            for tt in range(4):
                sset = fset[0] % 3
                fset[0] += 1
                tl = [2 * sset, 2 * sset + 1]
                for br in range(2):
                    b = ub[0] % 4
                    ub[0] += 1
                    for kc in range(8):
                        mm(ps[b][:], wsl[:, slot, kc * 256 + br * 128:kc * 256 + (br + 1) * 128], hT[:, kc, tt * 512:(tt + 1) * 512],
                           kc == 0, kc == 7, [("w", slot), ("hT", kc, tt)], [("ps", b)])
                    A("act", lambda e, b=b, br=br, tt=tt: e.activation(out=U[br][:, 2 + tt * 512:2 + (tt + 1) * 512], in_=ps[b][:], func=AF.Copy),
                      r=[("ps", b)], w=[("U", br, tt)])
                    tmp = tl[br]
                    cw0, cw1, cw2 = [gc(l, GC_CW, k * 44 + br * NCH + j) for k in range(3)]
                    cbb = gc(l, GC_CB, br * NCH + j)
                    u1 = U[br][:, 1 + tt * 512:1 + (tt + 1) * 512]
                    u0 = U[br][:, tt * 512:(tt + 1) * 512]
                    A("act", lambda e, b=b, tmp=tmp, cw2=cw2, cbb=cbb: e.activation(out=ft[:, tmp, :], in_=ps[b][:], func=AF.Identity,
                                                                                   scale=cw2, bias=cbb),
                      r=[("ps", b), "gcols"], w=[("f", tmp)])
                    rk = [("U", br, tt), ("U", br, max(tt - 1, 0)), ("f", tmp), "gcols"]
                    if br == 0:
                        A("dve", lambda e, tmp=tmp, u1=u1, cw1=cw1: e.scalar_tensor_tensor(
                            out=ft[:, tmp, :], in0=u1, scalar=cw1, in1=ft[:, tmp, :], op0=ALU.mult, op1=ALU.add), r=rk, w=[("f", tmp)])
                        A("dve", lambda e, tmp=tmp, u0=u0, cw0=cw0: e.scalar_tensor_tensor(
                            out=ft[:, tmp, :], in0=u0, scalar=cw0, in1=ft[:, tmp, :], op0=ALU.mult, op1=ALU.add), r=rk, w=[("f", tmp)])
                    else:
                        for (uu, cw) in ((u1, cw1), (u0, cw0)):
                            A("pool", lambda e, uu=uu, cw=cw: e.tensor_scalar(out=ptmp[:], in0=uu, scalar1=cw, scalar2=0.0,
                                                                              op0=ALU.mult, op1=ALU.add),
                              r=[("U", br, tt), ("U", br, max(tt - 1, 0)), "gcols"], w=[("bf", 0), ("bf", 1)])
                            A("pool", lambda e, tmp=tmp: e.tensor_tensor(out=ft[:, tmp, :], in0=ft[:, tmp, :], in1=ptmp[:], op=ALU.add),
                              r=[("bf", 0), ("bf", 1), ("f", tmp)], w=[("f", tmp)])
                tg, tu = tl
                A("act", lambda e, tg=tg: e.activation(out=ft[:, tg, :], in_=ft[:, tg, :], func=AF.Silu), r=[("f", tg)], w=[("f", tg)])
                A("dve", lambda e, j=j, tt=tt, tg=tg, tu=tu: e.tensor_tensor(out=yT[:, j % 8, tt * 512:(tt + 1) * 512], in0=ft[:, tg, :],
                                                                             in1=ft[:, tu, :], op=ALU.mult),
                  r=[("f", tg), ("f", tu)], w=[("yT", j % 8)])
            if j % 2 == 1:
                for i in range(NT):
                    for dh in range(2):
                        b = 4 + (i * 2 + dh) % 2
                        for jj in (j - 1, j):
                            mm(ps[b][:], yT[:, jj % 8, i * 128:(i + 1) * 128], wsl[:, fslots[jj], 2048 + dh * 512:2048 + (dh + 1) * 512],
                               jj == j - 1, jj == j, [("yT", jj % 8), ("w", fslots[jj])], [("ps", b)])
                        A("dve", lambda e, b=b, i=i, dh=dh: e.tensor_tensor(out=x[:, i, dh * 512:(dh + 1) * 512], in0=ps[b][:],
                                                                             in1=x[:, i, dh * 512:(dh + 1) * 512], op=ALU.add),
                          r=[("ps", b), ("x", i)], w=[("x", i)])
            if j + 2 < NCH:
                fslots[j + 2] = load_w(wffn_d[l, j + 2])
        P.barrier()
        if l + 1 < first_layer + n_layers:
            for t in range(2, 6):
                A("pool", lambda e, t=t: e.memset(qk[:, t, :], 0.0), w=[("qk", t, tt) for tt in range(4)])
            A("pool", lambda e: e.dma_start(out=qk[64:72, 4, :], in_=kind_d), w=[("qk", 4, tt) for tt in range(4)], dma=True)
            A("pool", lambda e: e.dma_start(out=qk[0:8, 5, :], in_=kind_d), w=[("qk", 5, tt) for tt in range(4)], dma=True)

    for l in range(first_layer, first_layer + n_layers):
        layer(l)

    for i in range(NT):
        A("sp", lambda e, i=i: e.dma_start(out=out_d[i * 128:(i + 1) * 128, :], in_=x[:, i, :]), r=[("x", i)], w=[("out", i)], dma=True)
    A("sp", None, r=[("out", i) for i in range(NT)])
    P.ops[-1].fn = None

    P.emit(nc, st)
    st.close()
    return nc


def rel_bucket_np(dist):
    n = np.maximum(dist, 0)
    nf = np.maximum(n, 16).astype(np.float32)
    large = 16 + (np.log(nf / np.float32(16)) / np.float32(math.log(1024 / 16)) * np.float32(16)).astype(np.int32)
    large = np.minimum(large, 31)
    return np.where(n < 16, n, large)


def host_layout(inp):
    f = lambda a: np.ascontiguousarray(np.asarray(a, dtype=np.float32))
    w_in, w_out, w_up, w_down = f(inp["w_in"]), f(inp["w_out"]), f(inp["w_up"]), f(inp["w_down"])
    L = DEPTH
    win = np.zeros((L, 8, 128, 8, 384), np.float32)
    for u in range(8):
        if u < 4:
            cols = np.concatenate([np.arange(128) + 128 * u, 512 + np.arange(128) + 128 * u, 1024 + np.arange(128) + 128 * u])
        else:
            p = u - 4
            cols = np.concatenate([1536 + np.arange(128) + 128 * p, 2048 + np.arange(128) + 128 * p, 2560 + np.arange(128) + 128 * p])
        win[:, u] = w_in[:, :, cols].reshape(L, 8, 128, 384).transpose(0, 2, 1, 3)
    wout = np.zeros((L, 3, 128, 8, 384), np.float32)
    for s3 in range(3):
        ns = 384 if s3 < 2 else 256
        wout[:, s3, :, :, :ns] = w_out[:, :, s3 * 384:s3 * 384 + ns].reshape(L, 8, 128, ns).transpose(0, 2, 1, 3)
    wffn = np.zeros((L, NCH, 128, 3072), np.float32)
    for j in range(NCH):
        cols = np.concatenate([np.arange(128) + 128 * j, DFF + np.arange(128) + 128 * j])
        wffn[:, j, :, :2048] = w_up[:, :, cols].reshape(L, 8, 128, 256).transpose(0, 2, 1, 3).reshape(L, 128, 2048)
        wffn[:, j, :, 2048:] = w_down[:, j * 128:(j + 1) * 128, :]
    rb = f(inp["rel_bias"])
    pp = np.arange(128)[:, None]
    cc = np.arange(TW)[None, :]
    dist = cc - pp - TOFF
    bidx = rel_bucket_np(dist)
    biasT = np.empty((12, 128, TW), np.float32)
    for h in range(12):
        biasT[h] = np.where(dist >= 0, rb[bidx, h], np.float32(NEG))
    gcols = np.zeros((128, GC), np.float32)
    ln1, ln2 = f(inp["ln_attn_g"]), f(inp["ln_ffn_g"])
    qkg, sub = f(inp["qk_norm_g"]), f(inp["diff_subln_g"])
    cw, cb = f(inp["conv_w"]), f(inp["conv_b"])
    for l in range(L):
        o = l * GC_L
        gcols[:, o + GC_LN1:o + GC_LN1 + 8] = ln1[l].reshape(8, 128).T
        gcols[:, o + GC_LN2:o + GC_LN2 + 8] = ln2[l].reshape(8, 128).T
        for k in range(4):
            gcols[:, o + GC_QK + k] = np.tile(qkg[l, k], 2)
        gcols[:, o + GC_SUB] = sub[l]
        for k in range(3):
            gcols[:, o + GC_CW + k * 44:o + GC_CW + (k + 1) * 44] = cw[l, k].reshape(44, 128).T
        gcols[:, o + GC_CB:o + GC_CB + 44] = cb[l].reshape(44, 128).T
    lamb = np.broadcast_to(f(inp["diff_lambda"]).reshape(1, L * 256), (128, L * 256)).copy()
    consts = np.zeros((128, 384), np.float32)
    consts[:, 0:128] = np.eye(128, dtype=np.float32)
    consts[:, 128:256] = 1.0
    consts[0:64, 256:320] = 1.0
    consts[64:128, 320:384] = 1.0
    cnte = np.zeros((65, 8, 72), np.float32)
    cnto = np.zeros((65, 8, 8), np.float32)
    for b in range(8):
        for n in range(8):
            c = 0.0 if n < b else (-10.0 if n == b else 10.0)
            cnte[64, b, 64 + n] = c
            cnto[64, b, n] = c
            if n < b:
                for n2 in range(b):
                    cnte[n * 8 + n2, b, 64 + n] = 1.0
                    cnto[n * 8 + n2, b, n] = 1.0
    kind = np.zeros((8, S), np.float32)
    for n in range(8):
        kind[n, n * 256:(n + 1) * 256] = 1.0
    shared = dict(win=win.reshape(L, 8, 128, 3072), wout=wout.reshape(L, 3, 128, 3072), wffn=wffn, biasT=biasT, gcols=gcols,
                  lamb=lamb, consts=consts, cnte=cnte.reshape(65, 576), cnto=cnto.reshape(65, 64), kind=kind)
    return shared


_NC_CACHE = {}


def kernel(**inputs):
    x = np.ascontiguousarray(np.asarray(inputs["x"], dtype=np.float32))
    shared = host_layout(inputs)
    if "nc" not in _NC_CACHE:
        _NC_CACHE["nc"] = build(DEPTH, 0)
    nc = _NC_CACHE["nc"]
    in_maps = [dict(shared, x=x[b]) for b in range(8)]
    res = run_bass_kernel_spmd(nc, in_maps, core_ids=list(range(8)))
    return np.stack([np.asarray(r["out"], dtype=np.float32) for r in res.results], axis=0)
```

```python
import math
from contextlib import ExitStack
import numpy as np
import concourse.bass as bass
import concourse.mybir as mybir
from concourse.bass_utils import run_bass_kernel_spmd

F32 = mybir.dt.float32
BF16 = mybir.dt.bfloat16
ALU = mybir.AluOpType
AF = mybir.ActivationFunctionType
AX = mybir.AxisListType

S = 2048
D = 1024
NT = 16
DEPTH = 4
DFF = 2816
NCH = 22
EPS = 1e-6
NEG = -30000.0
TW = 1408
TOFF = 0
SCALE = 0.125

GC_LN1 = 0
GC_LN2 = 8
GC_QK = 16
GC_SUB = 20
GC_CW = 21
GC_CB = 21 + 132
GC_L = 21 + 132 + 44
GC = GC_L * DEPTH


class Op:
    __slots__ = ("eng", "fn", "deps", "sig", "pos", "dma", "sem", "val")

    def __init__(self, eng, fn, dma):
        self.eng = eng
        self.fn = fn
        self.dma = dma
        self.deps = ()
        self.sig = dma
        self.pos = 0
        self.sem = None
        self.val = 0


class Prog:
    ENGS = ("pe", "act", "dve", "pool", "sp")

    def __init__(self):
        self.ops = []
        self.lastw = {}
        self.readers = {}
        self.cnt = {e: 0 for e in self.ENGS}
        self.last_on = {e: None for e in self.ENGS}
        self.dma_since_barrier = []

    def add(self, eng, fn, r=(), w=(), dma=False):
        op = Op(eng, fn, dma)
        idx = len(self.ops)
        op.pos = self.cnt[eng]
        self.cnt[eng] += 1
        deps = set()
        for k in r:
            lw = self.lastw.get(k)
            if lw is not None:
                deps.add(lw)
        for k in w:
            lw = self.lastw.get(k)
            if lw is not None:
                deps.add(lw)
            rd = self.readers.get(k)
            if rd:
                deps.update(rd[0].values())
                deps.update(rd[1])
        for k in r:
            rd = self.readers.setdefault(k, ({}, []))
            if dma:
                rd[1].append(idx)
            else:
                rd[0][eng] = idx
        for k in w:
            self.lastw[k] = idx
            self.readers[k] = ({}, [])
        deps.discard(idx)
        op.deps = self._filter(op, deps)
        self.ops.append(op)
        self.last_on[eng] = idx
        if dma:
            self.dma_since_barrier.append(idx)
        return idx

    def _filter(self, op, deps):
        out = []
        for d in deps:
            p = self.ops[d]
            if (not p.dma) and p.eng == op.eng and not op.dma:
                if op.eng == "pe":
                    continue
                if op.pos - p.pos > 3:
                    continue
            p.sig = True
            out.append(d)
        return tuple(out)

    def barrier(self):
        lasts = [v for v in self.last_on.values() if v is not None]
        dmas = list(self.dma_since_barrier)
        self.dma_since_barrier = []
        for e in self.ENGS:
            op = Op(e, None, False)
            op.pos = self.cnt[e]
            self.cnt[e] += 1
            deps = set(lasts) | set(dmas)
            op.deps = self._filter(op, deps)
            self.ops.append(op)
            self.last_on[e] = len(self.ops) - 1

    def emit(self, nc, stack):
        eng_sem = {e: stack.enter_context(nc.semaphore("sem_" + e)) for e in ("pe", "act", "dve", "pool")}
        NDS = 20
        dsems = {q: [stack.enter_context(nc.semaphore("dq_%s_%d" % (q, i))) for i in range(NDS)]
                 for q in ("sp", "pool", "act")}
        duse = {q: [0] * NDS for q in dsems}
        dnext = {q: 0 for q in dsems}
        ccount = {e: 0 for e in eng_sem}
        prev_wait = {}
        for i, op in enumerate(self.ops):
            if op.fn is None:
                continue
            if op.dma:
                q = op.eng
                k = dnext[q]
                dnext[q] = (k + 1) % NDS
                op.sem = dsems[q][k]
                prev_wait[i] = (op.sem, duse[q][k] * 16)
                duse[q][k] += 1
                op.val = duse[q][k] * 16
            elif op.sig:
                ccount[op.eng] += 1
                op.sem = eng_sem[op.eng]
                op.val = ccount[op.eng]
        per_eng = {e: [] for e in self.ENGS}
        for i, op in enumerate(self.ops):
            per_eng[op.eng].append(i)
        ops = self.ops

        def run(e, handle):
            seen = {}
            for i in per_eng[e]:
                op = ops[i]
                waits = {}
                for d in op.deps:
                    p = ops[d]
                    if p.sem is None:
                        continue
                    key = p.sem
                    if waits.get(key, (None, 0))[1] < p.val:
                        waits[key] = (p.sem, p.val)
                if i in prev_wait:
                    s_, v_ = prev_wait[i]
                    if v_ > 0 and waits.get(s_, (None, 0))[1] < v_:
                        waits[s_] = (s_, v_)
                for key, (s_, v_) in waits.items():
                    if seen.get(key, 0) >= v_:
                        continue
                    seen[key] = v_
                    handle.wait_ge(s_, v_)
                if op.fn is None:
                    continue
                inst = op.fn(handle)
                if op.dma:
                    inst.then_inc(op.sem, 16)
                elif op.sig:
                    inst.then_inc(op.sem, 1)

        with nc.Block() as block:
            @block.tensor
            def _(h):
                run("pe", h)

            @block.scalar
            def _(h):
                run("act", h)

            @block.vector
            def _(h):
                run("dve", h)

            @block.gpsimd
            def _(h):
                run("pool", h)

            @block.sync
            def _(h):
                run("sp", h)


def bcast_ap(ap, pattern):
    return bass.AP(tensor=ap.tensor, offset=ap.offset, ap=[list(ap.ap[0])] + [list(p) for p in pattern])


def build(n_layers=DEPTH, first_layer=0):
    nc = bass.Bass("TRN2", target_bir_lowering=False)
    dt = nc.dram_tensor
    x_d = dt("x", [S, D], F32, kind="ExternalInput").ap()
    win_d = dt("win", [DEPTH, 8, 128, 3072], F32, kind="ExternalInput").ap()
    wout_d = dt("wout", [DEPTH, 3, 128, 3072], F32, kind="ExternalInput").ap()
    wffn_d = dt("wffn", [DEPTH, NCH, 128, 3072], F32, kind="ExternalInput").ap()
    bias_d = dt("biasT", [12, 128, TW], F32, kind="ExternalInput").ap()
    gcols_d = dt("gcols", [128, GC], F32, kind="ExternalInput").ap()
    lam_d = dt("lamb", [128, DEPTH * 256], F32, kind="ExternalInput").ap()
    consts_d = dt("consts", [128, 384], F32, kind="ExternalInput").ap()
    cnte_d = dt("cnte", [65, 8 * 72], F32, kind="ExternalInput").ap()
    cnto_d = dt("cnto", [65, 8 * 8], F32, kind="ExternalInput").ap()
    kind_d = dt("kind", [8, S], F32, kind="ExternalInput").ap()
    out_d = dt("out", [S, D], F32, kind="ExternalOutput").ap()

    st = ExitStack()
    sb = lambda name, shape, dtype: st.enter_context(nc.sbuf_tensor(name, shape, dtype))
    x = sb("x_sb", [128, NT, D], F32)
    hT = sb("hT", [128, 8, S], BF16)
    yT = sb("yT", [128, 8, S], BF16)
    qk = sb("qk", [128, 6, S], BF16)
    V = sb("V", [128, 1, NT, 128], BF16)
    bias = sb("bias", [128, 2, TW], F32)
    ft = sb("ft", [128, 6, 512], F32)
    bf = sb("bf", [128, 4, 512], BF16)
    wsl = sb("wsl", [128, 3, 3072], BF16)
    gcols = sb("gcols_sb", [128, GC], F32)
    consts = sb("consts_sb", [128, 384], BF16)
    cnte = sb("cnte_sb", [65, 8 * 72], BF16)
    cnto = sb("cnto_sb", [65, 8 * 8], BF16)
    ind = sb("ind", [65, 256], BF16)
    kdiff = sb("kdiff", [128, 64], BF16)
    km = sb("km", [128, 8], F32)
    small = sb("small", [128, 64], F32)
    ps = [st.enter_context(nc.psum_tensor("ps%d" % i, [128, 512], F32)) for i in range(8)]

    ident = consts[:, 0:128]
    ones = consts[:, 128:256]
    bones = consts[:, 256:384]
    SS, RSTD, EPSC, LAM, NLAM, GSUB, TMPC = 0, 16, 32, 33, 37, 41, 45
    eps_col = small[:, EPSC:EPSC + 1]

    P = Prog()

    def A(eng, fn, r=(), w=(), dma=False):
        return P.add(eng, fn, r, w, dma)

    def mm(out, lhsT, rhs, start, stop, r, w):
        A("pe", lambda e: e.matmul(out, lhsT, rhs, start=start, stop=stop), r, w)

    misc_i = [0]

    def misc():
        misc_i[0] = (misc_i[0] + 1) % 8
        return misc_i[0]

    ft_i = [0]

    def ftile():
        ft_i[0] = (ft_i[0] + 1) % 6
        return ft_i[0]

    bf_i = [0]

    def bftile():
        bf_i[0] = (bf_i[0] + 1) % 4
        return bf_i[0]

    bias_seq = []
    bias_loaded = [0]

    def bias_prefetch(upto):
        while bias_loaded[0] < min(upto, len(bias_seq)):
            n = bias_loaded[0]
            hd = bias_seq[n]
            A("sp", lambda e, hd=hd, n=n: e.dma_start(out=bias[:, n % 2, :], in_=bias_d[hd]), w=[("bias", n % 2)], dma=True)
            bias_loaded[0] += 1

    wslot_i = [0]

    def load_w(src):
        k = wslot_i[0] % 3
        wslot_i[0] += 1
        A("pool", lambda e: e.dma_start(out=wsl[:, k, :], in_=src), w=[("w", k)], dma=True)
        return k

    def gc(l, base, j=0):
        c = l * GC_L + base + j
        return gcols[:, c:c + 1]

    A("sp", lambda e: e.dma_start(out=gcols[:], in_=gcols_d), w=["gcols"], dma=True)
    A("sp", lambda e: e.dma_start(out=ft[:, 0:2, :].rearrange("p a b -> p (a b)"), in_=lam_d), w=[("f", 0), ("f", 1)], dma=True)
    A("pool", lambda e: e.dma_start(out=consts[:], in_=consts_d), w=["consts"], dma=True)
    A("pool", lambda e: e.dma_start(out=cnte[:], in_=cnte_d), w=["cnt"], dma=True)
    A("pool", lambda e: e.dma_start(out=cnto[:], in_=cnto_d), w=["cnt"], dma=True)
    for i in range(NT):
        A("sp", lambda e, i=i: e.dma_start(out=x[:, i, :], in_=x_d[i * 128:(i + 1) * 128, :]), w=[("x", i)], dma=True)
    for t in (0, 2, 3, 4, 5):
        A("pool", lambda e, t=t: e.memset(qk[:, t, :], 0.0), w=[("qk", t, tt) for tt in range(4)])
    A("pool", lambda e: e.dma_start(out=qk[64:72, 4, :], in_=kind_d), w=[("qk", 4, tt) for tt in range(4)], dma=True)
    A("pool", lambda e: e.dma_start(out=qk[0:8, 5, :], in_=kind_d), w=[("qk", 5, tt) for tt in range(4)], dma=True)
    A("dve", lambda e: e.memset(small[:], 0.0), w=["small"])
    A("dve", lambda e: e.memset(eps_col, EPS), w=["small"])
    A("dve", lambda e: e.memset(ind[:], 1.0), w=["ind"])
    lamt = ft[:, 0:2, :].rearrange("p a b -> p (a b)")
    for l in range(DEPTH):
        lam_init = 0.8 - 0.6 * math.exp(-0.3 * l)
        b0 = l * 256
        for t in range(2):
            A("dve", lambda e, b0=b0, t=t: e.tensor_tensor(out=ft[:, 2, t * 64:(t + 1) * 64], in0=lamt[:, b0 + t * 128:b0 + t * 128 + 64],
                                                          in1=lamt[:, b0 + t * 128 + 64:b0 + t * 128 + 128], op=ALU.mult),
              r=[("f", 0), ("f", 1)], w=[("f", 2)])
            A("dve", lambda e, t=t: e.reduce_sum(out=small[:, TMPC + t:TMPC + t + 1], in_=ft[:, 2, t * 64:(t + 1) * 64], axis=AX.X),
              r=[("f", 2)], w=["small"])
        A("act", lambda e: e.activation(out=small[:, TMPC + 2:TMPC + 4], in_=small[:, TMPC:TMPC + 2], func=AF.Exp), r=["small"], w=["small"])
        A("dve", lambda e, l=l: e.tensor_tensor(out=small[:, LAM + l:LAM + l + 1], in0=small[:, TMPC + 2:TMPC + 3],
                                                in1=small[:, TMPC + 3:TMPC + 4], op=ALU.subtract), r=["small"], w=["small"])
        A("dve", lambda e, l=l, li=lam_init: e.tensor_scalar(out=small[:, NLAM + l:NLAM + l + 1], in0=small[:, LAM + l:LAM + l + 1],
                                                             scalar1=li, scalar2=-1.0, op0=ALU.add, op1=ALU.mult), r=["small"], w=["small"])
        A("dve", lambda e, l=l, li=lam_init: e.tensor_scalar(out=small[:, GSUB + l:GSUB + l + 1], in0=gc(l, GC_SUB),
                                                             scalar1=1.0 - li, scalar2=0.0, op0=ALU.mult, op1=ALU.add),
          r=["small", "gcols"], w=["small"])

    def norm_to_hT(l, gbase):
        junk = ft[:, 4:6, :].rearrange("p a b -> p (a b)")
        A("dve", lambda e: e.memset(small[:, SS:SS + 16], 0.0), w=[("ss", i) for i in range(NT)])
        for i in range(NT):
            A("act", lambda e, i=i: e.activation(out=junk, in_=x[:, i, :], func=AF.Square, accum_out=small[:, SS + i:SS + i + 1]),
              r=[("x", i)], w=[("f", 4), ("f", 5), ("ss", i)])
        A("act", lambda e: e.activation(out=small[:, RSTD:RSTD + 16], in_=small[:, SS:SS + 16], func=AF.Ln, scale=1.0 / D, bias=eps_col),
          r=[("ss", i) for i in range(NT)] + ["small"], w=["rstd"])
        A("act", lambda e: e.activation(out=small[:, RSTD:RSTD + 16], in_=small[:, RSTD:RSTD + 16], func=AF.Exp, scale=-0.5),
          r=["rstd"], w=["rstd"])
        hb = yT[:].rearrange("p c s -> p (c s)")
        for i in range(NT):
            A("dve", lambda e, i=i: e.tensor_scalar(out=hb[:, i * D:(i + 1) * D], in0=x[:, i, :], scalar1=small[:, RSTD + i:RSTD + i + 1],
                                                    scalar2=0.0, op0=ALU.mult, op1=ALU.add),
              r=[("x", i), "rstd"], w=[("yT", i // 2)])
        for c in range(8):
            for half in range(2):
                b = misc()
                pst = ps[b][:].bitcast(BF16)
                for ii in range(8):
                    i = half * 8 + ii
                    A("pe", lambda e, i=i, ii=ii, c=c, pst=pst: e.transpose(out=pst[:, ii * 128:(ii + 1) * 128],
                                                                            in_=hb[:, i * D + c * 128:i * D + (c + 1) * 128], identity=ident),
                      r=[("yT", i // 2), "consts"], w=[("ps", b)])
                A("act", lambda e, c=c, half=half, pst=pst: e.activation(out=hT[:, c, half * 1024:(half + 1) * 1024], in_=pst[:, 0:1024],
                                                                         func=AF.Identity, scale=gc(l, gbase, c)),
                  r=[("ps", b), "gcols"], w=[("hT", c, half * 2), ("hT", c, half * 2 + 1)])

    def proj_qk_unit(l, slot, specs):
        chunks = [(g, gq_idx, dests, tt) for (g, gq_idx, dests) in specs for tt in range(4)]
        st_ = {}

        def emit_proj(ci):
            g, gq_idx, dests, tt = chunks[ci]
            b = misc()
            for kc in range(8):
                mm(ps[b][:], wsl[:, slot, kc * 384 + g * 128:kc * 384 + (g + 1) * 128], hT[:, kc, tt * 512:(tt + 1) * 512],
                   kc == 0, kc == 7, [("w", slot), ("hT", kc, tt)], [("ps", b)])
            sq = bftile()
            A("act", lambda e, b=b, sq=sq: e.activation(out=bf[:, sq, :], in_=ps[b][:], func=AF.Square), r=[("ps", b)], w=[("bf", sq)])
            st_[ci] = (b, sq)

        def emit_chain(ci):
            g, gq_idx, dests, tt = chunks[ci]
            b, sq = st_[ci]
            b2 = misc()
            mm(ps[b2][:], bones, bf[:, sq, :], True, True, [("bf", sq), "consts"], [("ps", b2)])
            rs = ftile()
            A("act", lambda e, b2=b2, rs=rs: e.activation(out=ft[:, rs, :], in_=ps[b2][:], func=AF.Ln, scale=1.0 / 64, bias=eps_col),
              r=[("ps", b2), "small"], w=[("f", rs)])
            A("act", lambda e, rs=rs: e.activation(out=ft[:, rs, :], in_=ft[:, rs, :], func=AF.Exp, scale=-0.5), r=[("f", rs)], w=[("f", rs)])
            for (r0, r1, t) in dests:
                A("dve", lambda e, b=b, r0=r0, r1=r1, t=t, tt=tt, rs=rs, gq_idx=gq_idx: e.scalar_tensor_tensor(
                    out=qk[r0:r1, t, tt * 512:(tt + 1) * 512], in0=ps[b][r0:r1, :], scalar=gc(l, GC_QK, gq_idx)[r0:r1, :],
                    in1=ft[r0:r1, rs, :], op0=ALU.mult, op1=ALU.mult),
                  r=[("ps", b), ("f", rs), "gcols"], w=[("qk", t, tt)])

        n = len(chunks)
        emit_proj(0)
        for ci in range(n):
            if ci + 1 < n:
                emit_proj(ci + 1)
            emit_chain(ci)

    def proj_v(slot, vb=0):
        for i4 in range(4):
            b = misc()
            for ii in range(4):
                i = i4 * 4 + ii
                for kc in range(8):
                    mm(ps[b][:, ii * 128:(ii + 1) * 128], hT[:, kc, i * 128:(i + 1) * 128], wsl[:, slot, kc * 384 + 256:kc * 384 + 384],
                       kc == 0, kc == 7, [("w", slot), ("hT", kc, i // 4)], [("ps", b)])
            A("act", lambda e, b=b, i4=i4, vb=vb: e.activation(out=V[:, vb, i4 * 4:(i4 + 1) * 4, :].rearrange("p a b -> p (a b)"),
                                                               in_=ps[b][:], func=AF.Copy),
              r=[("ps", b)], w=[("V", vb, i4)])

    def gating(r, qt, kt):
        r0 = r * 64
        kview = qk[r0:r0 + 64, kt, :].rearrange("p (n j) -> p n j", j=256)
        A("dve", lambda e: e.reduce_sum(out=km[r0:r0 + 64, :], in_=kview, axis=AX.X),
          r=[("qk", kt, tt) for tt in range(4)], w=["km"])
        kmv = km[r0:r0 + 64, :]
        A("dve", lambda e: e.tensor_tensor(out=kdiff[r0:r0 + 64, :].rearrange("p (a b) -> p a b", b=8),
                                           in0=bcast_ap(kmv, [[0, 8], [1, 8]]), in1=bcast_ap(kmv, [[1, 8], [0, 8]]), op=ALU.subtract),
          r=["km"], w=["kdiff"])
        for blk in range(8):
            b = misc()
            mm(ps[b][0:64, 0:256], kdiff[r0:r0 + 64, :], qk[r0:r0 + 64, qt, blk * 256:(blk + 1) * 256], True, True,
               ["kdiff", ("qk", qt, blk // 2)], [("ps", b)])
            A("dve", lambda e, b=b: e.tensor_single_scalar(out=ind[0:64, :], in_=ps[b][0:64, 0:256], scalar=0.0, op=ALU.is_gt),
              r=[("ps", b)], w=["ind"])
            b2 = misc()
            if r == 0:
                mm(ps[b2][0:72, 0:256], cnte[:, blk * 72:(blk + 1) * 72], ind[:], True, True, ["ind", "cnt"], [("ps", b2)])
                m0, m1 = 64, 72
            else:
                mm(ps[b2][0:8, 0:256], cnto[:, blk * 8:(blk + 1) * 8], ind[:], True, True, ["ind", "cnt"], [("ps", b2)])
                m0, m1 = 0, 8
            A("dve", lambda e, b2=b2, m0=m0, m1=m1, blk=blk: e.tensor_scalar(
                out=qk[m0:m1, qt, blk * 256:(blk + 1) * 256], in0=ps[b2][m0:m1, 0:256], scalar1=2.5, scalar2=NEG,
                op0=ALU.is_ge, op1=ALU.mult), r=[("ps", b2)], w=[("qk", qt, blk // 2)])

    grp_i = [0]
    e_i = [0]
    lt_i = [0]
    bias_n = [0]

    def attention(l, subs, finalize, sub_outer=False):
        tiles = []
        if sub_outer:
            order = [(j, si) for si in range(len(subs)) for j in range(4)]
        else:
            order = [(j, si) for j in range(4) for si in range(len(subs))]
        for (j, si) in order:
            n = 4 * j + 4
            for i in range(n):
                tiles.append((j, si, i, n))
        cur = {"hd": None, "slot": 0}
        state = {}
        LOOK = 3

        def c0_of(j, i):
            d = i - 4 * j
            return 128 * d if d > 0 else 0

        def emit_S(t):
            j, si, i, n = tiles[t]
            sdef = subs[si]
            sbank = t % 3
            qt, q0, q1 = sdef["q"]
            kt, k0, k1 = sdef["k"]
            c0 = c0_of(j, i)
            mm(ps[sbank][:, c0:512], qk[k0:k1, kt, i * 128:(i + 1) * 128], qk[q0:q1, qt, j * 512 + c0:(j + 1) * 512], True, True,
               [("qk", kt, i // 4), ("qk", qt, j)], [("ps", sbank)])

        def emit_rest(t):
            j, si, i, n = tiles[t]
            sdef = subs[si]
            sbank = t % 3
            if i == 0:
                state[(j, si)] = grp_i[0] % 2
                grp_i[0] += 1
            gset = state[(j, si)]
            ob, zb = 3 + 2 * gset, 4 + 2 * gset
            hd = sdef["bias_head"]
            if cur["hd"] != hd:
                assert bias_seq[bias_n[0]] == hd, (bias_seq[bias_n[0]], hd)
                cur["hd"] = hd
                cur["slot"] = bias_n[0] % 2
                bias_n[0] += 1
                bias_prefetch(bias_n[0] + 1)
            bsl = cur["slot"]
            o = 512 * j - 128 * i
            c0 = c0_of(j, i)
            eb = e_i[0] % 3
            e_i[0] += 1
            if o >= 917:
                A("act", lambda e, sbank=sbank, eb=eb, bsl=bsl: e.activation(out=bf[:, eb, :], in_=ps[sbank][:], func=AF.Exp, scale=SCALE,
                                                                             bias=bias[:, bsl, TW - 1:TW]),
                  r=[("ps", sbank), ("bias", bsl)], w=[("bf", eb)])
            else:
                lt = lt_i[0] % 3
                lt_i[0] += 1
                tc0 = o + c0 + TOFF
                A("dve", lambda e, sbank=sbank, lt=lt, tc0=tc0, c0=c0, bsl=bsl: e.scalar_tensor_tensor(
                    out=ft[:, lt, c0:512], in0=ps[sbank][:, c0:512], scalar=SCALE, in1=bias[:, bsl, tc0:tc0 + 512 - c0], op0=ALU.mult, op1=ALU.add),
                  r=[("ps", sbank), ("bias", bsl)], w=[("f", lt)])
                A("act", lambda e, lt=lt, eb=eb, c0=c0: e.activation(out=bf[:, eb, c0:512], in_=ft[:, lt, c0:512], func=AF.Exp),
                  r=[("f", lt)], w=[("bf", eb)])
            vc = sdef["vcol0"]
            mm(ps[ob][:, c0:512], V[:, 0, i, vc:vc + 128], bf[:, eb, c0:512], i == 0, i == n - 1, [("V", 0, i // 4), ("bf", eb)], [("ps", ob)])
            mm(ps[zb][:, c0:512], ones, bf[:, eb, c0:512], i == 0, i == n - 1, [("bf", eb), "consts"], [("ps", zb)])
            if i == n - 1:
                pending.append((t + 2, (j, si, ob, zb)))

        pending = []
        T = len(tiles)
        for t in range(min(LOOK, T)):
            emit_S(t)
        for t in range(T):
            emit_rest(t)
            if t + LOOK < T:
                emit_S(t + LOOK)
            while pending and pending[0][0] <= t:
                finalize(*pending.pop(0)[1])
        while pending:
            finalize(*pending.pop(0)[1])

    def layer(l):
        norm_to_hT(l, GC_LN1)
        units = []
        for h in range(4):
            units.append(("d", h))
            units.append(("m", h))
        slots = {}
        slots[0] = load_w(win_d[l, 0])
        slots[1] = load_w(win_d[l, 4])
        for (kind, h) in units:
            if kind == "d":
                bias_seq.append(h)
            else:
                bias_seq.extend([4 + 2 * h, 4 + 2 * h + 1])
        bias_prefetch(bias_n[0] + 2)
        for ui, (kind, h) in enumerate(units):
            slot = slots[ui]
            if kind == "d":
                A("pool", lambda e: e.memset(qk[0:32, 3, :], 0.0), w=[("qk", 3, tt) for tt in range(4)])
                proj_qk_unit(l, slot, [(0, 0, [(0, 64, 0), (64, 128, 3)]), (1, 1, [(0, 128, 1)])])
                proj_v(slot)
            else:
                proj_qk_unit(l, slot, [(0, 2, [(0, 64, 2), (64, 128, 3)]), (1, 3, [(0, 64, 4), (64, 128, 5)])])
                proj_v(slot)
            nxt = ui + 2
            if nxt < 8:
                k2, h2 = units[nxt]
                slots[nxt] = load_w(win_d[l, h2 if k2 == "d" else 4 + h2])
            elif nxt == 8:
                slots[8] = load_w(wout_d[l, 0])
            elif nxt == 9:
                slots[9] = load_w(wout_d[l, 1])

            def recip(zb, dst, r0=0, r1=128):
                A("act", lambda e: e.activation(out=ft[r0:r1, dst, :], in_=ps[zb][r0:r1, :], func=AF.Ln), r=[("ps", zb)], w=[("f", dst)])
                A("act", lambda e: e.activation(out=ft[r0:r1, dst, :], in_=ft[r0:r1, dst, :], func=AF.Exp, scale=-1.0), r=[("f", dst)], w=[("f", dst)])

            if kind == "d":
                subs = [dict(q=(0 if m == 0 else 3, 0, 128), k=(1, 0, 128), vcol0=0, bias_head=h) for m in range(2)]

                def fin(j, si, ob, zb, h=h):
                    cs = slice(j * 512, (j + 1) * 512)
                    recip(zb, 3)
                    if si == 0:
                        A("dve", lambda e: e.tensor_tensor(out=ft[:, 4, :], in0=ps[ob][:], in1=ft[:, 3, :], op=ALU.mult),
                          r=[("ps", ob), ("f", 3)], w=[("f", 4)])
                    else:
                        A("dve", lambda e: e.tensor_tensor(out=ft[:, 3, :], in0=ps[ob][:], in1=ft[:, 3, :], op=ALU.mult),
                          r=[("ps", ob), ("f", 3)], w=[("f", 3)])
                        A("dve", lambda e: e.scalar_tensor_tensor(out=ft[:, 4, :], in0=ft[:, 3, :], scalar=small[:, NLAM + l:NLAM + l + 1],
                                                                  in1=ft[:, 4, :], op0=ALU.mult, op1=ALU.add),
                          r=[("f", 3), ("f", 4), "small"], w=[("f", 4)])
                        A("act", lambda e: e.activation(out=bf[:, 3, :], in_=ft[:, 4, :], func=AF.Square), r=[("f", 4)], w=[("bf", 3)])
                        b2 = 7
                        mm(ps[b2][:], ones, bf[:, 3, :], True, True, [("bf", 3), "consts"], [("ps", b2)])
                        A("act", lambda e: e.activation(out=ft[:, 5, :], in_=ps[b2][:], func=AF.Ln, scale=1.0 / 128, bias=eps_col),
                          r=[("ps", b2), "small"], w=[("f", 5)])
                        A("act", lambda e: e.activation(out=ft[:, 5, :], in_=ft[:, 5, :], func=AF.Exp, scale=-0.5), r=[("f", 5)], w=[("f", 5)])
                        A("dve", lambda e: e.scalar_tensor_tensor(out=yT[:, h, cs], in0=ft[:, 4, :], scalar=small[:, GSUB + l:GSUB + l + 1],
                                                                  in1=ft[:, 5, :], op0=ALU.mult, op1=ALU.mult),
                          r=[("f", 4), ("f", 5), "small"], w=[("yT", h)])
                attention(l, subs, fin)
            else:
                gating(0, 2, 4)
                gating(1, 3, 5)
                subs = [dict(q=(2 + r, 0, 128), k=(4 + r, 0, 128), vcol0=0, bias_head=4 + 2 * h + r) for r in range(2)]

                def fin(j, si, ob, zb, h=h):
                    cs = slice(j * 512, (j + 1) * 512)
                    r0 = si * 64
                    recip(zb, 3, r0, r0 + 64)
                    A("dve", lambda e: e.tensor_tensor(out=yT[r0:r0 + 64, 4 + h, cs], in0=ps[ob][r0:r0 + 64, :], in1=ft[r0:r0 + 64, 3, :],
                                                       op=ALU.mult), r=[("ps", ob), ("f", 3)], w=[("yT", 4 + h)])
                attention(l, subs, fin, sub_outer=True)
        P.barrier()
        wd = bias[:].rearrange("p a b -> p (a b)").bitcast(BF16)
        fslots = {}

        def load_wup(j):
            k = wslot_i[0] % 3
            wslot_i[0] += 1
            A("pool", lambda e: e.dma_start(out=wsl[:, k, 0:2048], in_=wffn_d[l, j][:, 0:2048]), w=[("w", k)], dma=True)
            fslots[j] = k

        def load_wdown(j):
            sl = j % 5
            A("pool", lambda e: e.dma_start(out=wd[:, sl * 1024:(sl + 1) * 1024], in_=wffn_d[l, j][:, 2048:3072]), w=[("wd", sl)], dma=True)

        slots[10] = load_w(wout_d[l, 2])
        load_wdown(0)
        for s3 in range(3):
            slot = slots[8 + s3]
            ns = 384 if s3 < 2 else 256
            d0 = s3 * 384
            for i in range(NT):
                b = misc()
                for fc in range(8):
                    mm(ps[b][:, 0:ns], yT[:, fc, i * 128:(i + 1) * 128], wsl[:, slot, fc * 384:fc * 384 + ns], fc == 0, fc == 7,
                       [("yT", fc), ("w", slot)], [("ps", b)])
                A("dve", lambda e, b=b, i=i, d0=d0, ns=ns: e.tensor_tensor(out=x[:, i, d0:d0 + ns], in0=ps[b][:, 0:ns], in1=x[:, i, d0:d0 + ns],
                                                                           op=ALU.add), r=[("ps", b), ("x", i)], w=[("x", i)])
            load_wup(s3)
        norm_to_hT(l, GC_LN2)
        qkf = qk[:].rearrange("p a b -> p (a b)").bitcast(F32)
        U = [qkf[:, 0:2050], qkf[:, 2052:4102]]
        for br in range(2):
            A("dve", lambda e, br=br: e.memset(U[br][:, 0:2], 0.0), w=[("U", br, 0)])
        GRP = 4
        ubanks = [0, 1, 2, 3, 6, 7]
        ub = [0]
        fset = [0]
        dbank = [0]
        pend = []

        def tail(j, tt, tg, tu):
            A("act", lambda e: e.activation(out=ft[:, tg, :], in_=ft[:, tg, :], func=AF.Silu), r=[("f", tg)], w=[("f", tg)])
            A("dve", lambda e: e.tensor_tensor(out=yT[:, j % 8, tt * 512:(tt + 1) * 512], in0=ft[:, tg, :], in1=ft[:, tu, :], op=ALU.mult),
              r=[("f", tg), ("f", tu)], w=[("yT", j % 8)])

        def down_proj(js):
            for i in range(NT):
                for dh in range(2):
                    b = 4 + dbank[0] % 2
                    dbank[0] += 1
                    for n_, jj in enumerate(js):
                        mm(ps[b][:], yT[:, jj % 8, i * 128:(i + 1) * 128], wd[:, (jj % 5) * 1024 + dh * 512:(jj % 5) * 1024 + (dh + 1) * 512],
                           n_ == 0, n_ == len(js) - 1, [("yT", jj % 8), ("wd", jj % 5)], [("ps", b)])
                    A("dve", lambda e, b=b, i=i, dh=dh: e.tensor_tensor(out=x[:, i, dh * 512:(dh + 1) * 512], in0=ps[b][:],
                                                                         in1=x[:, i, dh * 512:(dh + 1) * 512], op=ALU.add),
                      r=[("ps", b), ("x", i)], w=[("x", i)])

        for j in range(NCH):
            slot = fslots[j]
            for tt in range(4):
                sset = fset[0] % 3
                fset[0] += 1
                tl = [2 * sset, 2 * sset + 1]
                for br in range(2):
                    b = ubanks[ub[0] % 6]
                    ub[0] += 1
                    for kc in range(8):
                        mm(ps[b][:], wsl[:, slot, kc * 256 + br * 128:kc * 256 + (br + 1) * 128], hT[:, kc, tt * 512:(tt + 1) * 512],
                           kc == 0, kc == 7, [("w", slot), ("hT", kc, tt)], [("ps", b)])
                    A("act", lambda e, b=b, br=br, tt=tt: e.activation(out=U[br][:, 2 + tt * 512:2 + (tt + 1) * 512], in_=ps[b][:], func=AF.Copy),
                      r=[("ps", b)], w=[("U", br, tt)])
                    tmp = tl[br]
                    cw0, cw1, cw2 = [gc(l, GC_CW, k * 44 + br * NCH + j) for k in range(3)]
                    cbb = gc(l, GC_CB, br * NCH + j)
                    u1 = U[br][:, 1 + tt * 512:1 + (tt + 1) * 512]
                    u0 = U[br][:, tt * 512:(tt + 1) * 512]
                    A("act", lambda e, b=b, tmp=tmp, cw2=cw2, cbb=cbb: e.activation(out=ft[:, tmp, :], in_=ps[b][:], func=AF.Identity,
                                                                                   scale=cw2, bias=cbb),
                      r=[("ps", b), "gcols"], w=[("f", tmp)])
                    rk = [("U", br, tt), ("U", br, max(tt - 1, 0)), ("f", tmp), "gcols"]
                    A("dve", lambda e, tmp=tmp, u1=u1, cw1=cw1: e.scalar_tensor_tensor(
                        out=ft[:, tmp, :], in0=u1, scalar=cw1, in1=ft[:, tmp, :], op0=ALU.mult, op1=ALU.add), r=rk, w=[("f", tmp)])
                    A("dve", lambda e, tmp=tmp, u0=u0, cw0=cw0: e.scalar_tensor_tensor(
                        out=ft[:, tmp, :], in0=u0, scalar=cw0, in1=ft[:, tmp, :], op0=ALU.mult, op1=ALU.add), r=rk, w=[("f", tmp)])
                if pend:
                    tail(*pend.pop(0))
                pend.append((j, tt, tl[0], tl[1]))
                if tt == 0 and j >= 1 and (j % GRP) == 0:
                    down_proj(list(range(j - GRP, j)))
            if j + 3 < NCH:
                load_wup(j + 3)
            if j + 1 < NCH:
                load_wdown(j + 1)
        while pend:
            tail(*pend.pop(0))
        down_proj(list(range((NCH // GRP) * GRP, NCH)))
        P.barrier()
        if l + 1 < first_layer + n_layers:
            for t in (0, 2, 3, 4, 5):
                A("pool", lambda e, t=t: e.memset(qk[:, t, :], 0.0), w=[("qk", t, tt) for tt in range(4)])
            A("pool", lambda e: e.dma_start(out=qk[64:72, 4, :], in_=kind_d), w=[("qk", 4, tt) for tt in range(4)], dma=True)
            A("pool", lambda e: e.dma_start(out=qk[0:8, 5, :], in_=kind_d), w=[("qk", 5, tt) for tt in range(4)], dma=True)

    for l in range(first_layer, first_layer + n_layers):
        layer(l)

    for i in range(NT):
        A("sp", lambda e, i=i: e.dma_start(out=out_d[i * 128:(i + 1) * 128, :], in_=x[:, i, :]), r=[("x", i)], w=[("out", i)], dma=True)
    A("sp", None, r=[("out", i) for i in range(NT)])
    P.ops[-1].fn = None

    P.emit(nc, st)
    st.close()
    return nc


def rel_bucket_np(dist):
    n = np.maximum(dist, 0)
    nf = np.maximum(n, 16).astype(np.float32)
    large = 16 + (np.log(nf / np.float32(16)) / np.float32(math.log(1024 / 16)) * np.float32(16)).astype(np.int32)
    large = np.minimum(large, 31)
    return np.where(n < 16, n, large)


def host_layout(inp):
    f = lambda a: np.ascontiguousarray(np.asarray(a, dtype=np.float32))
    w_in, w_out, w_up, w_down = f(inp["w_in"]), f(inp["w_out"]), f(inp["w_up"]), f(inp["w_down"])
    L = DEPTH
    win = np.zeros((L, 8, 128, 8, 384), np.float32)
    for u in range(8):
        if u < 4:
            cols = np.concatenate([np.arange(128) + 128 * u, 512 + np.arange(128) + 128 * u, 1024 + np.arange(128) + 128 * u])
        else:
            p = u - 4
            cols = np.concatenate([1536 + np.arange(128) + 128 * p, 2048 + np.arange(128) + 128 * p, 2560 + np.arange(128) + 128 * p])
        win[:, u] = w_in[:, :, cols].reshape(L, 8, 128, 384).transpose(0, 2, 1, 3)
    wout = np.zeros((L, 3, 128, 8, 384), np.float32)
    for s3 in range(3):
        ns = 384 if s3 < 2 else 256
        wout[:, s3, :, :, :ns] = w_out[:, :, s3 * 384:s3 * 384 + ns].reshape(L, 8, 128, ns).transpose(0, 2, 1, 3)
    wffn = np.zeros((L, NCH, 128, 3072), np.float32)
    for j in range(NCH):
        cols = np.concatenate([np.arange(128) + 128 * j, DFF + np.arange(128) + 128 * j])
        wffn[:, j, :, :2048] = w_up[:, :, cols].reshape(L, 8, 128, 256).transpose(0, 2, 1, 3).reshape(L, 128, 2048)
        wffn[:, j, :, 2048:] = w_down[:, j * 128:(j + 1) * 128, :]
    rb = f(inp["rel_bias"])
    pp = np.arange(128)[:, None]
    cc = np.arange(TW)[None, :]
    dist = cc - pp - TOFF
    bidx = rel_bucket_np(dist)
    biasT = np.empty((12, 128, TW), np.float32)
    for h in range(12):
        biasT[h] = np.where(dist >= 0, rb[bidx, h], np.float32(NEG))
    gcols = np.zeros((128, GC), np.float32)
    ln1, ln2 = f(inp["ln_attn_g"]), f(inp["ln_ffn_g"])
    qkg, sub = f(inp["qk_norm_g"]), f(inp["diff_subln_g"])
    cw, cb = f(inp["conv_w"]), f(inp["conv_b"])
    for l in range(L):
        o = l * GC_L
        gcols[:, o + GC_LN1:o + GC_LN1 + 8] = ln1[l].reshape(8, 128).T
        gcols[:, o + GC_LN2:o + GC_LN2 + 8] = ln2[l].reshape(8, 128).T
        for k in range(4):
            gcols[:, o + GC_QK + k] = np.tile(qkg[l, k], 2)
        gcols[:, o + GC_SUB] = sub[l]
        for k in range(3):
            gcols[:, o + GC_CW + k * 44:o + GC_CW + (k + 1) * 44] = cw[l, k].reshape(44, 128).T
        gcols[:, o + GC_CB:o + GC_CB + 44] = cb[l].reshape(44, 128).T
    lamb = np.broadcast_to(f(inp["diff_lambda"]).reshape(1, L * 256), (128, L * 256)).copy()
    consts = np.zeros((128, 384), np.float32)
    consts[:, 0:128] = np.eye(128, dtype=np.float32)
    consts[:, 128:256] = 1.0
    consts[0:64, 256:320] = 1.0
    consts[64:128, 320:384] = 1.0
    cnte = np.zeros((65, 8, 72), np.float32)
    cnto = np.zeros((65, 8, 8), np.float32)
    for b in range(8):
        for n in range(8):
            c = 0.0 if n < b else (-10.0 if n == b else 10.0)
            cnte[64, b, 64 + n] = c
            cnto[64, b, n] = c
            if n < b:
                for n2 in range(b):
                    cnte[n * 8 + n2, b, 64 + n] = 1.0
                    cnto[n * 8 + n2, b, n] = 1.0
    kind = np.zeros((8, S), np.float32)
    for n in range(8):
        kind[n, n * 256:(n + 1) * 256] = 1.0
    shared = dict(win=win.reshape(L, 8, 128, 3072), wout=wout.reshape(L, 3, 128, 3072), wffn=wffn, biasT=biasT, gcols=gcols,
                  lamb=lamb, consts=consts, cnte=cnte.reshape(65, 576), cnto=cnto.reshape(65, 64), kind=kind)
    return shared


_NC_CACHE = {}


def kernel(**inputs):
    x = np.ascontiguousarray(np.asarray(inputs["x"], dtype=np.float32))
    shared = host_layout(inputs)
    if "nc" not in _NC_CACHE:
        _NC_CACHE["nc"] = build(DEPTH, 0)
    nc = _NC_CACHE["nc"]
    in_maps = [dict(shared, x=x[b]) for b in range(8)]
    res = run_bass_kernel_spmd(nc, in_maps, core_ids=list(range(8)))
    return np.stack([np.asarray(r["out"], dtype=np.float32) for r in res.results], axis=0)
```

```python
import math
from contextlib import ExitStack
import numpy as np
import concourse.bass as bass
import concourse.mybir as mybir
from concourse.bass_utils import run_bass_kernel_spmd

F32 = mybir.dt.float32
BF16 = mybir.dt.bfloat16
ALU = mybir.AluOpType
AF = mybir.ActivationFunctionType
AX = mybir.AxisListType

S = 2048
D = 1024
NT = 16
DEPTH = 4
DFF = 2816
NCH = 22
EPS = 1e-6
NEG = -30000.0
TW = 1408
TOFF = 0
SCALE = 0.125

GC_LN1 = 0
GC_LN2 = 8
GC_QK = 16
GC_SUB = 20
GC_CW = 21
GC_CB = 21 + 132
GC_L = 21 + 132 + 44
GC = GC_L * DEPTH


class Op:
    __slots__ = ("eng", "fn", "deps", "sig", "pos", "dma", "sem", "val")

    def __init__(self, eng, fn, dma):
        self.eng = eng
        self.fn = fn
        self.dma = dma
        self.deps = ()
        self.sig = dma
        self.pos = 0
        self.sem = None
        self.val = 0


class Prog:
    ENGS = ("pe", "act", "dve", "pool", "sp")

    def __init__(self):
        self.ops = []
        self.lastw = {}
        self.readers = {}
        self.cnt = {e: 0 for e in self.ENGS}
        self.last_on = {e: None for e in self.ENGS}
        self.dma_since_barrier = []

    def add(self, eng, fn, r=(), w=(), dma=False):
        op = Op(eng, fn, dma)
        idx = len(self.ops)
        op.pos = self.cnt[eng]
        self.cnt[eng] += 1
        deps = set()
        for k in r:
            lw = self.lastw.get(k)
            if lw is not None:
                deps.add(lw)
        for k in w:
            lw = self.lastw.get(k)
            if lw is not None:
                deps.add(lw)
            rd = self.readers.get(k)
            if rd:
                deps.update(rd[0].values())
                deps.update(rd[1])
        for k in r:
            rd = self.readers.setdefault(k, ({}, []))
            if dma:
                rd[1].append(idx)
            else:
                rd[0][eng] = idx
        for k in w:
            self.lastw[k] = idx
            self.readers[k] = ({}, [])
        deps.discard(idx)
        op.deps = self._filter(op, deps)
        self.ops.append(op)
        self.last_on[eng] = idx
        if dma:
            self.dma_since_barrier.append(idx)
        return idx

    def _filter(self, op, deps):
        out = []
        for d in deps:
            p = self.ops[d]
            if (not p.dma) and p.eng == op.eng and not op.dma:
                if op.eng == "pe":
                    continue
                if op.pos - p.pos > 3:
                    continue
            p.sig = True
            out.append(d)
        return tuple(out)

    def barrier(self):
        lasts = [v for v in self.last_on.values() if v is not None]
        dmas = list(self.dma_since_barrier)
        self.dma_since_barrier = []
        for e in self.ENGS:
            op = Op(e, None, False)
            op.pos = self.cnt[e]
            self.cnt[e] += 1
            deps = set(lasts) | set(dmas)
            op.deps = self._filter(op, deps)
            self.ops.append(op)
            self.last_on[e] = len(self.ops) - 1

    def emit(self, nc, stack):
        eng_sem = {e: stack.enter_context(nc.semaphore("sem_" + e)) for e in ("pe", "act", "dve", "pool")}
        NDS = 20
        dsems = {q: [stack.enter_context(nc.semaphore("dq_%s_%d" % (q, i))) for i in range(NDS)]
                 for q in ("sp", "pool", "act")}
        duse = {q: [0] * NDS for q in dsems}
        dnext = {q: 0 for q in dsems}
        ccount = {e: 0 for e in eng_sem}
        prev_wait = {}
        for i, op in enumerate(self.ops):
            if op.fn is None:
                continue
            if op.dma:
                q = op.eng
                k = dnext[q]
                dnext[q] = (k + 1) % NDS
                op.sem = dsems[q][k]
                prev_wait[i] = (op.sem, duse[q][k] * 16)
                duse[q][k] += 1
                op.val = duse[q][k] * 16
            elif op.sig:
                ccount[op.eng] += 1
                op.sem = eng_sem[op.eng]
                op.val = ccount[op.eng]
        per_eng = {e: [] for e in self.ENGS}
        for i, op in enumerate(self.ops):
            per_eng[op.eng].append(i)
        ops = self.ops

        def run(e, handle):
            seen = {}
            for i in per_eng[e]:
                op = ops[i]
                waits = {}
                for d in op.deps:
                    p = ops[d]
                    if p.sem is None:
                        continue
                    key = p.sem
                    if waits.get(key, (None, 0))[1] < p.val:
                        waits[key] = (p.sem, p.val)
                if i in prev_wait:
                    s_, v_ = prev_wait[i]
                    if v_ > 0 and waits.get(s_, (None, 0))[1] < v_:
                        waits[s_] = (s_, v_)
                for key, (s_, v_) in waits.items():
                    if seen.get(key, 0) >= v_:
                        continue
                    seen[key] = v_
                    handle.wait_ge(s_, v_)
                if op.fn is None:
                    continue
                inst = op.fn(handle)
                if op.dma:
                    inst.then_inc(op.sem, 16)
                elif op.sig:
                    inst.then_inc(op.sem, 1)

        with nc.Block() as block:
            @block.tensor
            def _(h):
                run("pe", h)

            @block.scalar
            def _(h):
                run("act", h)

            @block.vector
            def _(h):
                run("dve", h)

            @block.gpsimd
            def _(h):
                run("pool", h)

            @block.sync
            def _(h):
                run("sp", h)


def bcast_ap(ap, pattern):
    return bass.AP(tensor=ap.tensor, offset=ap.offset, ap=[list(ap.ap[0])] + [list(p) for p in pattern])


def build(n_layers=DEPTH, first_layer=0):
    nc = bass.Bass("TRN2", target_bir_lowering=False)
    dt = nc.dram_tensor
    x_d = dt("x", [S, D], F32, kind="ExternalInput").ap()
    win_d = dt("win", [DEPTH, 8, 128, 3072], F32, kind="ExternalInput").ap()
    wout_d = dt("wout", [DEPTH, 3, 128, 3072], F32, kind="ExternalInput").ap()
    wffn_d = dt("wffn", [DEPTH, NCH, 128, 3072], F32, kind="ExternalInput").ap()
    bias_d = dt("biasT", [12, 128, TW], F32, kind="ExternalInput").ap()
    gcols_d = dt("gcols", [128, GC], F32, kind="ExternalInput").ap()
    lam_d = dt("lamb", [128, DEPTH * 256], F32, kind="ExternalInput").ap()
    consts_d = dt("consts", [128, 384], F32, kind="ExternalInput").ap()
    cnte_d = dt("cnte", [65, 8 * 72], F32, kind="ExternalInput").ap()
    cnto_d = dt("cnto", [65, 8 * 8], F32, kind="ExternalInput").ap()
    kind_d = dt("kind", [8, S], F32, kind="ExternalInput").ap()
    out_d = dt("out", [S, D], F32, kind="ExternalOutput").ap()

    st = ExitStack()
    sb = lambda name, shape, dtype: st.enter_context(nc.sbuf_tensor(name, shape, dtype))
    x = sb("x_sb", [128, NT, D], F32)
    hT = sb("hT", [128, 8, S], BF16)
    yT = sb("yT", [128, 8, S], BF16)
    qk = sb("qk", [128, 6, S], BF16)
    V = sb("V", [128, 1, NT, 128], BF16)
    bias = sb("bias", [128, 2, TW], F32)
    ft = sb("ft", [128, 6, 512], F32)
    bf = sb("bf", [128, 4, 512], BF16)
    wsl = sb("wsl", [128, 3, 3072], BF16)
    gcols = sb("gcols_sb", [128, GC], F32)
    consts = sb("consts_sb", [128, 384], BF16)
    cnte = sb("cnte_sb", [65, 8 * 72], BF16)
    cnto = sb("cnto_sb", [65, 8 * 8], BF16)
    ind = sb("ind", [65, 2, 256], BF16)
    kdiff = sb("kdiff", [128, 64], BF16)
    km = sb("km", [128, 8], F32)
    small = sb("small", [128, 64], F32)
    ps = [st.enter_context(nc.psum_tensor("ps%d" % i, [128, 512], F32)) for i in range(8)]

    ident = consts[:, 0:128]
    ones = consts[:, 128:256]
    bones = consts[:, 256:384]
    SS, RSTD, EPSC, LAM, NLAM, GSUB, TMPC = 0, 16, 32, 33, 37, 41, 45
    eps_col = small[:, EPSC:EPSC + 1]

    P = Prog()

    def A(eng, fn, r=(), w=(), dma=False):
        return P.add(eng, fn, r, w, dma)

    def mm(out, lhsT, rhs, start, stop, r, w):
        A("pe", lambda e: e.matmul(out, lhsT, rhs, start=start, stop=stop), r, w)

    misc_i = [0]

    def misc():
        misc_i[0] = (misc_i[0] + 1) % 8
        return misc_i[0]

    ft_i = [0]

    def ftile():
        ft_i[0] = (ft_i[0] + 1) % 6
        return ft_i[0]

    bf_i = [0]

    def bftile():
        bf_i[0] = (bf_i[0] + 1) % 4
        return bf_i[0]

    bias_seq = []
    bias_loaded = [0]

    def bias_prefetch(upto):
        while bias_loaded[0] < min(upto, len(bias_seq)):
            n = bias_loaded[0]
            hd = bias_seq[n]
            A("sp", lambda e, hd=hd, n=n: e.dma_start(out=bias[:, n % 2, :], in_=bias_d[hd]), w=[("bias", n % 2)], dma=True)
            bias_loaded[0] += 1

    wslot_i = [0]

    def load_w(src):
        k = wslot_i[0] % 3
        wslot_i[0] += 1
        A("pool", lambda e: e.dma_start(out=wsl[:, k, :], in_=src), w=[("w", k)], dma=True)
        return k

    def gc(l, base, j=0):
        c = l * GC_L + base + j
        return gcols[:, c:c + 1]

    A("sp", lambda e: e.dma_start(out=gcols[:], in_=gcols_d), w=["gcols"], dma=True)
    A("sp", lambda e: e.dma_start(out=ft[:, 0:2, :].rearrange("p a b -> p (a b)"), in_=lam_d), w=[("f", 0), ("f", 1)], dma=True)
    A("pool", lambda e: e.dma_start(out=consts[:], in_=consts_d), w=["consts"], dma=True)
    A("pool", lambda e: e.dma_start(out=cnte[:], in_=cnte_d), w=["cnt"], dma=True)
    A("pool", lambda e: e.dma_start(out=cnto[:], in_=cnto_d), w=["cnt"], dma=True)
    for i in range(NT):
        A("sp", lambda e, i=i: e.dma_start(out=x[:, i, :], in_=x_d[i * 128:(i + 1) * 128, :]), w=[("x", i)], dma=True)
    for t in (0, 2, 3, 4, 5):
        A("pool", lambda e, t=t: e.memset(qk[:, t, :], 0.0), w=[("qk", t, tt) for tt in range(4)])
    A("pool", lambda e: e.dma_start(out=qk[64:72, 4, :], in_=kind_d), w=[("qk", 4, tt) for tt in range(4)], dma=True)
    A("pool", lambda e: e.dma_start(out=qk[0:8, 5, :], in_=kind_d), w=[("qk", 5, tt) for tt in range(4)], dma=True)
    A("dve", lambda e: e.memset(small[:], 0.0), w=["small"])
    A("dve", lambda e: e.memset(eps_col, EPS), w=["small"])
    A("dve", lambda e: e.memset(ind[:], 1.0), w=[("ind", 0), ("ind", 1)])
    lamt = ft[:, 0:2, :].rearrange("p a b -> p (a b)")
    for l in range(DEPTH):
        lam_init = 0.8 - 0.6 * math.exp(-0.3 * l)
        b0 = l * 256
        for t in range(2):
            A("dve", lambda e, b0=b0, t=t: e.tensor_tensor(out=ft[:, 2, t * 64:(t + 1) * 64], in0=lamt[:, b0 + t * 128:b0 + t * 128 + 64],
                                                          in1=lamt[:, b0 + t * 128 + 64:b0 + t * 128 + 128], op=ALU.mult),
              r=[("f", 0), ("f", 1)], w=[("f", 2)])
            A("dve", lambda e, t=t: e.reduce_sum(out=small[:, TMPC + t:TMPC + t + 1], in_=ft[:, 2, t * 64:(t + 1) * 64], axis=AX.X),
              r=[("f", 2)], w=["small"])
        A("act", lambda e: e.activation(out=small[:, TMPC + 2:TMPC + 4], in_=small[:, TMPC:TMPC + 2], func=AF.Exp), r=["small"], w=["small"])
        A("dve", lambda e, l=l: e.tensor_tensor(out=small[:, LAM + l:LAM + l + 1], in0=small[:, TMPC + 2:TMPC + 3],
                                                in1=small[:, TMPC + 3:TMPC + 4], op=ALU.subtract), r=["small"], w=["small"])
        A("dve", lambda e, l=l, li=lam_init: e.tensor_scalar(out=small[:, NLAM + l:NLAM + l + 1], in0=small[:, LAM + l:LAM + l + 1],
                                                             scalar1=li, scalar2=-1.0, op0=ALU.add, op1=ALU.mult), r=["small"], w=["small"])
        A("dve", lambda e, l=l, li=lam_init: e.tensor_scalar(out=small[:, GSUB + l:GSUB + l + 1], in0=gc(l, GC_SUB),
                                                             scalar1=1.0 - li, scalar2=0.0, op0=ALU.mult, op1=ALU.add),
          r=["small", "gcols"], w=["small"])

    def norm_to_hT(l, gbase):
        junk = ft[:, 4:6, :].rearrange("p a b -> p (a b)")
        A("dve", lambda e: e.memset(small[:, SS:SS + 16], 0.0), w=[("ss", i) for i in range(NT)])
        for i in range(NT):
            A("act", lambda e, i=i: e.activation(out=junk, in_=x[:, i, :], func=AF.Square, accum_out=small[:, SS + i:SS + i + 1]),
              r=[("x", i)], w=[("f", 4), ("f", 5), ("ss", i)])
        A("act", lambda e: e.activation(out=small[:, RSTD:RSTD + 16], in_=small[:, SS:SS + 16], func=AF.Ln, scale=1.0 / D, bias=eps_col),
          r=[("ss", i) for i in range(NT)] + ["small"], w=["rstd"])
        A("act", lambda e: e.activation(out=small[:, RSTD:RSTD + 16], in_=small[:, RSTD:RSTD + 16], func=AF.Exp, scale=-0.5),
          r=["rstd"], w=["rstd"])
        hb = yT[:].rearrange("p c s -> p (c s)")
        for i in range(NT):
            A("dve", lambda e, i=i: e.tensor_scalar(out=hb[:, i * D:(i + 1) * D], in0=x[:, i, :], scalar1=small[:, RSTD + i:RSTD + i + 1],
                                                    scalar2=0.0, op0=ALU.mult, op1=ALU.add),
              r=[("x", i), "rstd"], w=[("yT", i // 2)])
        for c in range(8):
            for half in range(2):
                b = misc()
                pst = ps[b][:].bitcast(BF16)
                for ii in range(8):
                    i = half * 8 + ii
                    A("pe", lambda e, i=i, ii=ii, c=c, pst=pst: e.transpose(out=pst[:, ii * 128:(ii + 1) * 128],
                                                                            in_=hb[:, i * D + c * 128:i * D + (c + 1) * 128], identity=ident),
                      r=[("yT", i // 2), "consts"], w=[("ps", b)])
                A("act", lambda e, c=c, half=half, pst=pst: e.activation(out=hT[:, c, half * 1024:(half + 1) * 1024], in_=pst[:, 0:1024],
                                                                         func=AF.Identity, scale=gc(l, gbase, c)),
                  r=[("ps", b), "gcols"], w=[("hT", c, half * 2), ("hT", c, half * 2 + 1)])

    def proj_qk_unit(l, slot, specs):
        chunks = [(g, gq_idx, dests, tt) for (g, gq_idx, dests) in specs for tt in range(4)]
        st_ = {}

        def emit_proj(ci):
            g, gq_idx, dests, tt = chunks[ci]
            b = misc()
            for kc in range(8):
                mm(ps[b][:], wsl[:, slot, kc * 384 + g * 128:kc * 384 + (g + 1) * 128], hT[:, kc, tt * 512:(tt + 1) * 512],
                   kc == 0, kc == 7, [("w", slot), ("hT", kc, tt)], [("ps", b)])
            sq = bftile()
            A("act", lambda e, b=b, sq=sq: e.activation(out=bf[:, sq, :], in_=ps[b][:], func=AF.Square), r=[("ps", b)], w=[("bf", sq)])
            st_[ci] = (b, sq)

        def emit_chain(ci):
            g, gq_idx, dests, tt = chunks[ci]
            b, sq = st_[ci]
            b2 = misc()
            mm(ps[b2][:], bones, bf[:, sq, :], True, True, [("bf", sq), "consts"], [("ps", b2)])
            rs = ftile()
            A("act", lambda e, b2=b2, rs=rs: e.activation(out=ft[:, rs, :], in_=ps[b2][:], func=AF.Ln, scale=1.0 / 64, bias=eps_col),
              r=[("ps", b2), "small"], w=[("f", rs)])
            A("act", lambda e, rs=rs: e.activation(out=ft[:, rs, :], in_=ft[:, rs, :], func=AF.Exp, scale=-0.5), r=[("f", rs)], w=[("f", rs)])
            for (r0, r1, t) in dests:
                A("dve", lambda e, b=b, r0=r0, r1=r1, t=t, tt=tt, rs=rs, gq_idx=gq_idx: e.scalar_tensor_tensor(
                    out=qk[r0:r1, t, tt * 512:(tt + 1) * 512], in0=ps[b][r0:r1, :], scalar=gc(l, GC_QK, gq_idx)[r0:r1, :],
                    in1=ft[r0:r1, rs, :], op0=ALU.mult, op1=ALU.mult),
                  r=[("ps", b), ("f", rs), "gcols"], w=[("qk", t, tt)])

        n = len(chunks)
        emit_proj(0)
        for ci in range(n):
            if ci + 1 < n:
                emit_proj(ci + 1)
            emit_chain(ci)

    def proj_v(slot, vb=0):
        for i4 in range(4):
            b = misc()
            for ii in range(4):
                i = i4 * 4 + ii
                for kc in range(8):
                    mm(ps[b][:, ii * 128:(ii + 1) * 128], hT[:, kc, i * 128:(i + 1) * 128], wsl[:, slot, kc * 384 + 256:kc * 384 + 384],
                       kc == 0, kc == 7, [("w", slot), ("hT", kc, i // 4)], [("ps", b)])
            A("act", lambda e, b=b, i4=i4, vb=vb: e.activation(out=V[:, vb, i4 * 4:(i4 + 1) * 4, :].rearrange("p a b -> p (a b)"),
                                                               in_=ps[b][:], func=AF.Copy),
              r=[("ps", b)], w=[("V", vb, i4)])

    def gating(r, qt, kt):
        r0 = r * 64
        kview = qk[r0:r0 + 64, kt, :].rearrange("p (n j) -> p n j", j=256)
        A("dve", lambda e: e.reduce_sum(out=km[r0:r0 + 64, :], in_=kview, axis=AX.X),
          r=[("qk", kt, tt) for tt in range(4)], w=["km"])
        kmv = km[r0:r0 + 64, :]
        A("dve", lambda e: e.tensor_tensor(out=kdiff[r0:r0 + 64, :].rearrange("p (a b) -> p a b", b=8),
                                           in0=bcast_ap(kmv, [[0, 8], [1, 8]]), in1=bcast_ap(kmv, [[1, 8], [0, 8]]), op=ALU.subtract),
          r=["km"], w=["kdiff"])
        for b0 in range(0, 8, 2):
            stg = {}
            for blk in (b0, b0 + 1):
                b = misc()
                ib = blk % 2
                mm(ps[b][0:64, 0:256], kdiff[r0:r0 + 64, :], qk[r0:r0 + 64, qt, blk * 256:(blk + 1) * 256], True, True,
                   ["kdiff", ("qk", qt, blk // 2)], [("ps", b)])
                A("dve", lambda e, b=b, ib=ib: e.tensor_single_scalar(out=ind[0:64, ib, :], in_=ps[b][0:64, 0:256], scalar=0.0, op=ALU.is_gt),
                  r=[("ps", b)], w=[("ind", ib)])
            for blk in (b0, b0 + 1):
                ib = blk % 2
                b2 = misc()
                if r == 0:
                    mm(ps[b2][0:72, 0:256], cnte[:, blk * 72:(blk + 1) * 72], ind[:, ib, :], True, True, [("ind", ib), "cnt"], [("ps", b2)])
                    m0, m1 = 64, 72
                else:
                    mm(ps[b2][0:8, 0:256], cnto[:, blk * 8:(blk + 1) * 8], ind[:, ib, :], True, True, [("ind", ib), "cnt"], [("ps", b2)])
                    m0, m1 = 0, 8
                A("dve", lambda e, b2=b2, m0=m0, m1=m1, blk=blk: e.tensor_scalar(
                    out=qk[m0:m1, qt, blk * 256:(blk + 1) * 256], in0=ps[b2][m0:m1, 0:256], scalar1=2.5, scalar2=NEG,
                    op0=ALU.is_ge, op1=ALU.mult), r=[("ps", b2)], w=[("qk", qt, blk // 2)])

    grp_i = [0]
    e_i = [0]
    lt_i = [0]
    bias_n = [0]

    def attention(l, subs, finalize, fin_act, sub_outer=False):
        tiles = []
        if sub_outer:
            order = [(j, si) for si in range(len(subs)) for j in range(4)]
        else:
            order = [(j, si) for j in range(4) for si in range(len(subs))]
        for (j, si) in order:
            n = 4 * j + 4
            for i in range(n):
                tiles.append((j, si, i, n))
        cur = {"hd": None, "slot": 0}
        state = {}
        LOOK = 3

        def c0_of(j, i):
            d = i - 4 * j
            return 128 * d if d > 0 else 0

        def emit_S(t):
            j, si, i, n = tiles[t]
            sdef = subs[si]
            sbank = t % 3
            qt, q0, q1 = sdef["q"]
            kt, k0, k1 = sdef["k"]
            c0 = c0_of(j, i)
            mm(ps[sbank][:, c0:512], qk[k0:k1, kt, i * 128:(i + 1) * 128], qk[q0:q1, qt, j * 512 + c0:(j + 1) * 512], True, True,
               [("qk", kt, i // 4), ("qk", qt, j)], [("ps", sbank)])

        def emit_rest(t):
            j, si, i, n = tiles[t]
            sdef = subs[si]
            sbank = t % 3
            if i == 0:
                state[(j, si)] = grp_i[0] % 2
                grp_i[0] += 1
            gset = state[(j, si)]
            ob, zb = 3 + 2 * gset, 4 + 2 * gset
            hd = sdef["bias_head"]
            if cur["hd"] != hd:
                assert bias_seq[bias_n[0]] == hd, (bias_seq[bias_n[0]], hd)
                cur["hd"] = hd
                cur["slot"] = bias_n[0] % 2
                bias_n[0] += 1
                bias_prefetch(bias_n[0] + 1)
            bsl = cur["slot"]
            o = 512 * j - 128 * i
            c0 = c0_of(j, i)
            eb = e_i[0] % 3
            e_i[0] += 1
            if o >= 917:
                A("act", lambda e, sbank=sbank, eb=eb, bsl=bsl: e.activation(out=bf[:, eb, :], in_=ps[sbank][:], func=AF.Exp, scale=SCALE,
                                                                             bias=bias[:, bsl, TW - 1:TW]),
                  r=[("ps", sbank), ("bias", bsl)], w=[("bf", eb)])
            else:
                lt = lt_i[0] % 3
                lt_i[0] += 1
                tc0 = o + c0 + TOFF
                A("dve", lambda e, sbank=sbank, lt=lt, tc0=tc0, c0=c0, bsl=bsl: e.scalar_tensor_tensor(
                    out=ft[:, lt, c0:512], in0=ps[sbank][:, c0:512], scalar=SCALE, in1=bias[:, bsl, tc0:tc0 + 512 - c0], op0=ALU.mult, op1=ALU.add),
                  r=[("ps", sbank), ("bias", bsl)], w=[("f", lt)])
                A("act", lambda e, lt=lt, eb=eb, c0=c0: e.activation(out=bf[:, eb, c0:512], in_=ft[:, lt, c0:512], func=AF.Exp),
                  r=[("f", lt)], w=[("bf", eb)])
            vc = sdef["vcol0"]
            mm(ps[ob][:, c0:512], V[:, 0, i, vc:vc + 128], bf[:, eb, c0:512], i == 0, i == n - 1, [("V", 0, i // 4), ("bf", eb)], [("ps", ob)])
            mm(ps[zb][:, c0:512], ones, bf[:, eb, c0:512], i == 0, i == n - 1, [("bf", eb), "consts"], [("ps", zb)])
            if i == n - 1:
                pending_a.append((t + 1, (j, si, ob, zb)))
                pending.append((t + 3, (j, si, ob, zb)))

        pending = []
        pending_a = []
        T = len(tiles)
        for t in range(min(LOOK, T)):
            emit_S(t)
        for t in range(T):
            emit_rest(t)
            if t + LOOK < T:
                emit_S(t + LOOK)
            while pending_a and pending_a[0][0] <= t:
                fin_act(*pending_a.pop(0)[1])
            while pending and pending[0][0] <= t:
                finalize(*pending.pop(0)[1])
        while pending_a:
            fin_act(*pending_a.pop(0)[1])
        while pending:
            finalize(*pending.pop(0)[1])

    def layer(l):
        norm_to_hT(l, GC_LN1)
        units = []
        for h in range(4):
            units.append(("d", h))
            units.append(("m", h))
        slots = {}
        slots[0] = load_w(win_d[l, 0])
        slots[1] = load_w(win_d[l, 4])
        for (kind, h) in units:
            if kind == "d":
                bias_seq.append(h)
            else:
                bias_seq.extend([4 + 2 * h, 4 + 2 * h + 1])
        bias_prefetch(bias_n[0] + 2)
        for ui, (kind, h) in enumerate(units):
            slot = slots[ui]
            if kind == "d":
                A("pool", lambda e: e.memset(qk[0:32, 3, :], 0.0), w=[("qk", 3, tt) for tt in range(4)])
                proj_qk_unit(l, slot, [(0, 0, [(0, 64, 0), (64, 128, 3)]), (1, 1, [(0, 128, 1)])])
                proj_v(slot)
            else:
                proj_qk_unit(l, slot, [(0, 2, [(0, 64, 2), (64, 128, 3)]), (1, 3, [(0, 64, 4), (64, 128, 5)])])
                proj_v(slot)
            nxt = ui + 2
            if nxt < 8:
                k2, h2 = units[nxt]
                slots[nxt] = load_w(win_d[l, h2 if k2 == "d" else 4 + h2])
            elif nxt == 8:
                slots[8] = load_w(wout_d[l, 0])
            elif nxt == 9:
                slots[9] = load_w(wout_d[l, 1])

            def recip(zb, dst, r0=0, r1=128):
                A("act", lambda e: e.activation(out=ft[r0:r1, dst, :], in_=ps[zb][r0:r1, :], func=AF.Ln), r=[("ps", zb)], w=[("f", dst)])
                A("act", lambda e: e.activation(out=ft[r0:r1, dst, :], in_=ft[r0:r1, dst, :], func=AF.Exp, scale=-1.0), r=[("f", dst)], w=[("f", dst)])

            if kind == "d":
                subs = [dict(q=(0 if m == 0 else 3, 0, 128), k=(1, 0, 128), vcol0=0, bias_head=h) for m in range(2)]

                def fin_a(j, si, ob, zb):
                    recip(zb, 3)

                def fin(j, si, ob, zb, h=h):
                    cs = slice(j * 512, (j + 1) * 512)
                    if si == 0:
                        A("dve", lambda e: e.tensor_tensor(out=ft[:, 4, :], in0=ps[ob][:], in1=ft[:, 3, :], op=ALU.mult),
                          r=[("ps", ob), ("f", 3)], w=[("f", 4)])
                    else:
                        A("dve", lambda e: e.tensor_tensor(out=ft[:, 3, :], in0=ps[ob][:], in1=ft[:, 3, :], op=ALU.mult),
                          r=[("ps", ob), ("f", 3)], w=[("f", 3)])
                        A("dve", lambda e: e.scalar_tensor_tensor(out=ft[:, 4, :], in0=ft[:, 3, :], scalar=small[:, NLAM + l:NLAM + l + 1],
                                                                  in1=ft[:, 4, :], op0=ALU.mult, op1=ALU.add),
                          r=[("f", 3), ("f", 4), "small"], w=[("f", 4)])
                        A("act", lambda e: e.activation(out=bf[:, 3, :], in_=ft[:, 4, :], func=AF.Square), r=[("f", 4)], w=[("bf", 3)])
                        b2 = 7
                        mm(ps[b2][:], ones, bf[:, 3, :], True, True, [("bf", 3), "consts"], [("ps", b2)])
                        A("act", lambda e: e.activation(out=ft[:, 5, :], in_=ps[b2][:], func=AF.Ln, scale=1.0 / 128, bias=eps_col),
                          r=[("ps", b2), "small"], w=[("f", 5)])
                        A("act", lambda e: e.activation(out=ft[:, 5, :], in_=ft[:, 5, :], func=AF.Exp, scale=-0.5), r=[("f", 5)], w=[("f", 5)])
                        A("dve", lambda e: e.scalar_tensor_tensor(out=yT[:, h, cs], in0=ft[:, 4, :], scalar=small[:, GSUB + l:GSUB + l + 1],
                                                                  in1=ft[:, 5, :], op0=ALU.mult, op1=ALU.mult),
                          r=[("f", 4), ("f", 5), "small"], w=[("yT", h)])
                attention(l, subs, fin, fin_a)
            else:
                gating(0, 2, 4)
                gating(1, 3, 5)
                subs = [dict(q=(2 + r, 0, 128), k=(4 + r, 0, 128), vcol0=0, bias_head=4 + 2 * h + r) for r in range(2)]

                def fin_a(j, si, ob, zb):
                    recip(zb, 3, si * 64, si * 64 + 64)

                def fin(j, si, ob, zb, h=h):
                    cs = slice(j * 512, (j + 1) * 512)
                    r0 = si * 64
                    A("dve", lambda e: e.tensor_tensor(out=yT[r0:r0 + 64, 4 + h, cs], in0=ps[ob][r0:r0 + 64, :], in1=ft[r0:r0 + 64, 3, :],
                                                       op=ALU.mult), r=[("ps", ob), ("f", 3)], w=[("yT", 4 + h)])
                attention(l, subs, fin, fin_a, sub_outer=True)
        ALLQK = [("qk", t, tt) for t in range(6) for tt in range(4)]
        ALLU = [("U", br, tt) for br in range(2) for tt in range(4)]
        ALLWD = [("wd", k5) for k5 in range(5)]
        A("dve", lambda e: e.memset(bias[:, 0, 0:2], 0.0), w=[("bias", 0), ("bias", 1)] + ALLWD)
        wd = bias[:].rearrange("p a b -> p (a b)").bitcast(BF16)
        fslots = {}

        def load_wup(j):
            k = wslot_i[0] % 3
            wslot_i[0] += 1
            A("pool", lambda e: e.dma_start(out=wsl[:, k, 0:2048], in_=wffn_d[l, j][:, 0:2048]), w=[("w", k)], dma=True)
            fslots[j] = k

        def load_wdown(j):
            sl = j % 5
            A("pool", lambda e: e.dma_start(out=wd[:, sl * 1024:(sl + 1) * 1024], in_=wffn_d[l, j][:, 2048:3072]), w=[("wd", sl)], dma=True)

        slots[10] = load_w(wout_d[l, 2])
        load_wdown(0)
        for s3 in range(3):
            slot = slots[8 + s3]
            ns = 384 if s3 < 2 else 256
            d0 = s3 * 384
            for i in range(NT):
                b = misc()
                for fc in range(8):
                    mm(ps[b][:, 0:ns], yT[:, fc, i * 128:(i + 1) * 128], wsl[:, slot, fc * 384:fc * 384 + ns], fc == 0, fc == 7,
                       [("yT", fc), ("w", slot)], [("ps", b)])
                A("dve", lambda e, b=b, i=i, d0=d0, ns=ns: e.tensor_tensor(out=x[:, i, d0:d0 + ns], in0=ps[b][:, 0:ns], in1=x[:, i, d0:d0 + ns],
                                                                           op=ALU.add), r=[("ps", b), ("x", i)], w=[("x", i)])
            load_wup(s3)
        norm_to_hT(l, GC_LN2)
        qkf = qk[:].rearrange("p a b -> p (a b)").bitcast(F32)
        U = [qkf[:, 0:2050], qkf[:, 2052:4102]]
        A("dve", lambda e: e.memset(qkf[:, 0:4104], 0.0), w=ALLQK + ALLU)
        GRP = 4
        ubanks = [0, 1, 2, 3, 6, 7]
        ub = [0]
        fset = [0]
        dbank = [0]
        pend = []

        def tail(j, tt, tg, tu):
            A("act", lambda e: e.activation(out=ft[:, tg, :], in_=ft[:, tg, :], func=AF.Silu), r=[("f", tg)], w=[("f", tg)])
            A("dve", lambda e: e.tensor_tensor(out=yT[:, j % 8, tt * 512:(tt + 1) * 512], in0=ft[:, tg, :], in1=ft[:, tu, :], op=ALU.mult),
              r=[("f", tg), ("f", tu)], w=[("yT", j % 8)])

        def down_proj(js):
            for i in range(NT):
                for dh in range(2):
                    b = 4 + dbank[0] % 2
                    dbank[0] += 1
                    for n_, jj in enumerate(js):
                        mm(ps[b][:], yT[:, jj % 8, i * 128:(i + 1) * 128], wd[:, (jj % 5) * 1024 + dh * 512:(jj % 5) * 1024 + (dh + 1) * 512],
                           n_ == 0, n_ == len(js) - 1, [("yT", jj % 8), ("wd", jj % 5)], [("ps", b)])
                    A("dve", lambda e, b=b, i=i, dh=dh: e.tensor_tensor(out=x[:, i, dh * 512:(dh + 1) * 512], in0=ps[b][:],
                                                                         in1=x[:, i, dh * 512:(dh + 1) * 512], op=ALU.add),
                      r=[("ps", b), ("x", i)], w=[("x", i)])

        for j in range(NCH):
            slot = fslots[j]
            for tt in range(4):
                sset = fset[0] % 3
                fset[0] += 1
                tl = [2 * sset, 2 * sset + 1]
                for br in range(2):
                    b = ubanks[ub[0] % 6]
                    ub[0] += 1
                    for kc in range(8):
                        mm(ps[b][:], wsl[:, slot, kc * 256 + br * 128:kc * 256 + (br + 1) * 128], hT[:, kc, tt * 512:(tt + 1) * 512],
                           kc == 0, kc == 7, [("w", slot), ("hT", kc, tt)], [("ps", b)])
                    A("act", lambda e, b=b, br=br, tt=tt: e.activation(out=U[br][:, 2 + tt * 512:2 + (tt + 1) * 512], in_=ps[b][:], func=AF.Copy),
                      r=[("ps", b)], w=[("U", br, tt)])
                    tmp = tl[br]
                    cw0, cw1, cw2 = [gc(l, GC_CW, k * 44 + br * NCH + j) for k in range(3)]
                    cbb = gc(l, GC_CB, br * NCH + j)
                    u1 = U[br][:, 1 + tt * 512:1 + (tt + 1) * 512]
                    u0 = U[br][:, tt * 512:(tt + 1) * 512]
                    A("act", lambda e, b=b, tmp=tmp, cw2=cw2, cbb=cbb: e.activation(out=ft[:, tmp, :], in_=ps[b][:], func=AF.Identity,
                                                                                   scale=cw2, bias=cbb),
                      r=[("ps", b), "gcols"], w=[("f", tmp)])
                    rk = [("U", br, tt), ("U", br, max(tt - 1, 0)), ("f", tmp), "gcols"]
                    A("dve", lambda e, tmp=tmp, u1=u1, cw1=cw1: e.scalar_tensor_tensor(
                        out=ft[:, tmp, :], in0=u1, scalar=cw1, in1=ft[:, tmp, :], op0=ALU.mult, op1=ALU.add), r=rk, w=[("f", tmp)])
                    A("dve", lambda e, tmp=tmp, u0=u0, cw0=cw0: e.scalar_tensor_tensor(
                        out=ft[:, tmp, :], in0=u0, scalar=cw0, in1=ft[:, tmp, :], op0=ALU.mult, op1=ALU.add), r=rk, w=[("f", tmp)])
                if pend:
                    tail(*pend.pop(0))
                pend.append((j, tt, tl[0], tl[1]))
                if tt == 0 and j >= 1 and (j % GRP) == 0:
                    down_proj(list(range(j - GRP, j)))
            if j + 3 < NCH:
                load_wup(j + 3)
            if j + 1 < NCH:
                load_wdown(j + 1)
        while pend:
            tail(*pend.pop(0))
        down_proj(list(range((NCH // GRP) * GRP, NCH)))
        A("dve", lambda e: e.memset(bias[:, 0, 0:2], 0.0), w=[("bias", 0), ("bias", 1)] + ALLWD)
        if l + 1 < first_layer + n_layers:
            A("pool", lambda e: e.memset(qk[:].rearrange("p a b -> p (a b)"), 0.0), w=ALLQK + ALLU)
            A("pool", lambda e: e.dma_start(out=qk[64:72, 4, :], in_=kind_d), w=[("qk", 4, tt) for tt in range(4)], dma=True)
            A("pool", lambda e: e.dma_start(out=qk[0:8, 5, :], in_=kind_d), w=[("qk", 5, tt) for tt in range(4)], dma=True)

    for l in range(first_layer, first_layer + n_layers):
        layer(l)

    for i in range(NT):
        A("sp", lambda e, i=i: e.dma_start(out=out_d[i * 128:(i + 1) * 128, :], in_=x[:, i, :]), r=[("x", i)], w=[("out", i)], dma=True)
    A("sp", None, r=[("out", i) for i in range(NT)])
    P.ops[-1].fn = None

    P.emit(nc, st)
    st.close()
    return nc


def rel_bucket_np(dist):
    n = np.maximum(dist, 0)
    nf = np.maximum(n, 16).astype(np.float32)
    large = 16 + (np.log(nf / np.float32(16)) / np.float32(math.log(1024 / 16)) * np.float32(16)).astype(np.int32)
    large = np.minimum(large, 31)
    return np.where(n < 16, n, large)


def host_layout(inp):
    f = lambda a: np.ascontiguousarray(np.asarray(a, dtype=np.float32))
    w_in, w_out, w_up, w_down = f(inp["w_in"]), f(inp["w_out"]), f(inp["w_up"]), f(inp["w_down"])
    L = DEPTH
    win = np.zeros((L, 8, 128, 8, 384), np.float32)
    for u in range(8):
        if u < 4:
            cols = np.concatenate([np.arange(128) + 128 * u, 512 + np.arange(128) + 128 * u, 1024 + np.arange(128) + 128 * u])
        else:
            p = u - 4
            cols = np.concatenate([1536 + np.arange(128) + 128 * p, 2048 + np.arange(128) + 128 * p, 2560 + np.arange(128) + 128 * p])
        win[:, u] = w_in[:, :, cols].reshape(L, 8, 128, 384).transpose(0, 2, 1, 3)
    wout = np.zeros((L, 3, 128, 8, 384), np.float32)
    for s3 in range(3):
        ns = 384 if s3 < 2 else 256
        wout[:, s3, :, :, :ns] = w_out[:, :, s3 * 384:s3 * 384 + ns].reshape(L, 8, 128, ns).transpose(0, 2, 1, 3)
    wffn = np.zeros((L, NCH, 128, 3072), np.float32)
    for j in range(NCH):
        cols = np.concatenate([np.arange(128) + 128 * j, DFF + np.arange(128) + 128 * j])
        wffn[:, j, :, :2048] = w_up[:, :, cols].reshape(L, 8, 128, 256).transpose(0, 2, 1, 3).reshape(L, 128, 2048)
        wffn[:, j, :, 2048:] = w_down[:, j * 128:(j + 1) * 128, :]
    rb = f(inp["rel_bias"])
    pp = np.arange(128)[:, None]
    cc = np.arange(TW)[None, :]
    dist = cc - pp - TOFF
    bidx = rel_bucket_np(dist)
    biasT = np.empty((12, 128, TW), np.float32)
    for h in range(12):
        biasT[h] = np.where(dist >= 0, rb[bidx, h], np.float32(NEG))
    gcols = np.zeros((128, GC), np.float32)
    ln1, ln2 = f(inp["ln_attn_g"]), f(inp["ln_ffn_g"])
    qkg, sub = f(inp["qk_norm_g"]), f(inp["diff_subln_g"])
    cw, cb = f(inp["conv_w"]), f(inp["conv_b"])
    for l in range(L):
        o = l * GC_L
        gcols[:, o + GC_LN1:o + GC_LN1 + 8] = ln1[l].reshape(8, 128).T
        gcols[:, o + GC_LN2:o + GC_LN2 + 8] = ln2[l].reshape(8, 128).T
        for k in range(4):
            gcols[:, o + GC_QK + k] = np.tile(qkg[l, k], 2)
        gcols[:, o + GC_SUB] = sub[l]
        for k in range(3):
            gcols[:, o + GC_CW + k * 44:o + GC_CW + (k + 1) * 44] = cw[l, k].reshape(44, 128).T
        gcols[:, o + GC_CB:o + GC_CB + 44] = cb[l].reshape(44, 128).T
    lamb = np.broadcast_to(f(inp["diff_lambda"]).reshape(1, L * 256), (128, L * 256)).copy()
    consts = np.zeros((128, 384), np.float32)
    consts[:, 0:128] = np.eye(128, dtype=np.float32)
    consts[:, 128:256] = 1.0
    consts[0:64, 256:320] = 1.0
    consts[64:128, 320:384] = 1.0
    cnte = np.zeros((65, 8, 72), np.float32)
    cnto = np.zeros((65, 8, 8), np.float32)
    for b in range(8):
        for n in range(8):
            c = 0.0 if n < b else (-10.0 if n == b else 10.0)
            cnte[64, b, 64 + n] = c
            cnto[64, b, n] = c
            if n < b:
                for n2 in range(b):
                    cnte[n * 8 + n2, b, 64 + n] = 1.0
                    cnto[n * 8 + n2, b, n] = 1.0
    kind = np.zeros((8, S), np.float32)
    for n in range(8):
        kind[n, n * 256:(n + 1) * 256] = 1.0
    shared = dict(win=win.reshape(L, 8, 128, 3072), wout=wout.reshape(L, 3, 128, 3072), wffn=wffn, biasT=biasT, gcols=gcols,
                  lamb=lamb, consts=consts, cnte=cnte.reshape(65, 576), cnto=cnto.reshape(65, 64), kind=kind)
    return shared


_NC_CACHE = {}


def kernel(**inputs):
    x = np.ascontiguousarray(np.asarray(inputs["x"], dtype=np.float32))
    shared = host_layout(inputs)
    if "nc" not in _NC_CACHE:
        _NC_CACHE["nc"] = build(DEPTH, 0)
    nc = _NC_CACHE["nc"]
    in_maps = [dict(shared, x=x[b]) for b in range(8)]
    res = run_bass_kernel_spmd(nc, in_maps, core_ids=list(range(8)))
    return np.stack([np.asarray(r["out"], dtype=np.float32) for r in res.results], axis=0)
```

```python
import math
from contextlib import ExitStack
import numpy as np
import concourse.bass as bass
import concourse.mybir as mybir
from concourse.bass_utils import run_bass_kernel_spmd

F32 = mybir.dt.float32
BF16 = mybir.dt.bfloat16
ALU = mybir.AluOpType
AF = mybir.ActivationFunctionType
AX = mybir.AxisListType

S = 2048
D = 1024
NT = 16
DEPTH = 4
DFF = 2816
NCH = 22
EPS = 1e-6
NEG = -30000.0
TW = 1408
TOFF = 0
SCALE = 0.125

GC_LN1 = 0
GC_LN2 = 8
GC_QK = 16
GC_SUB = 20
GC_CW = 21
GC_CB = 21 + 132
GC_L = 21 + 132 + 44
GC = GC_L * DEPTH


class Op:
    __slots__ = ("eng", "fn", "deps", "sig", "pos", "dma", "sem", "val")

    def __init__(self, eng, fn, dma):
        self.eng = eng
        self.fn = fn
        self.dma = dma
        self.deps = ()
        self.sig = dma
        self.pos = 0
        self.sem = None
        self.val = 0


class Prog:
    ENGS = ("pe", "act", "dve", "pool", "sp")

    def __init__(self):
        self.ops = []
        self.lastw = {}
        self.readers = {}
        self.cnt = {e: 0 for e in self.ENGS}
        self.last_on = {e: None for e in self.ENGS}
        self.dma_since_barrier = []

    def add(self, eng, fn, r=(), w=(), dma=False):
        op = Op(eng, fn, dma)
        idx = len(self.ops)
        op.pos = self.cnt[eng]
        self.cnt[eng] += 1
        deps = set()
        for k in r:
            lw = self.lastw.get(k)
            if lw is not None:
                deps.add(lw)
        for k in w:
            lw = self.lastw.get(k)
            if lw is not None:
                deps.add(lw)
            rd = self.readers.get(k)
            if rd:
                deps.update(rd[0].values())
                deps.update(rd[1])
        for k in r:
            rd = self.readers.setdefault(k, ({}, []))
            if dma:
                rd[1].append(idx)
            else:
                rd[0][eng] = idx
        for k in w:
            self.lastw[k] = idx
            self.readers[k] = ({}, [])
        deps.discard(idx)
        op.deps = self._filter(op, deps)
        self.ops.append(op)
        self.last_on[eng] = idx
        if dma:
            self.dma_since_barrier.append(idx)
        return idx

    def _filter(self, op, deps):
        out = []
        for d in deps:
            p = self.ops[d]
            if (not p.dma) and p.eng == op.eng and not op.dma:
                if op.eng == "pe":
                    continue
                if op.pos - p.pos > 3:
                    continue
            p.sig = True
            out.append(d)
        return tuple(out)

    def barrier(self):
        lasts = [v for v in self.last_on.values() if v is not None]
        dmas = list(self.dma_since_barrier)
        self.dma_since_barrier = []
        for e in self.ENGS:
            op = Op(e, None, False)
            op.pos = self.cnt[e]
            self.cnt[e] += 1
            deps = set(lasts) | set(dmas)
            op.deps = self._filter(op, deps)
            self.ops.append(op)
            self.last_on[e] = len(self.ops) - 1

    def emit(self, nc, stack):
        eng_sem = {e: stack.enter_context(nc.semaphore("sem_" + e)) for e in ("pe", "act", "dve", "pool")}
        NDS = 20
        dsems = {q: [stack.enter_context(nc.semaphore("dq_%s_%d" % (q, i))) for i in range(NDS)]
                 for q in ("sp", "pool", "act")}
        duse = {q: [0] * NDS for q in dsems}
        dnext = {q: 0 for q in dsems}
        ccount = {e: 0 for e in eng_sem}
        prev_wait = {}
        for i, op in enumerate(self.ops):
            if op.fn is None:
                continue
            if op.dma:
                q = op.eng
                k = dnext[q]
                dnext[q] = (k + 1) % NDS
                op.sem = dsems[q][k]
                prev_wait[i] = (op.sem, duse[q][k] * 16)
                duse[q][k] += 1
                op.val = duse[q][k] * 16
            elif op.sig:
                ccount[op.eng] += 1
                op.sem = eng_sem[op.eng]
                op.val = ccount[op.eng]
        per_eng = {e: [] for e in self.ENGS}
        for i, op in enumerate(self.ops):
            per_eng[op.eng].append(i)
        ops = self.ops

        def run(e, handle):
            seen = {}
            for i in per_eng[e]:
                op = ops[i]
                waits = {}
                for d in op.deps:
                    p = ops[d]
                    if p.sem is None:
                        continue
                    key = p.sem
                    if waits.get(key, (None, 0))[1] < p.val:
                        waits[key] = (p.sem, p.val)
                if i in prev_wait:
                    s_, v_ = prev_wait[i]
                    if v_ > 0 and waits.get(s_, (None, 0))[1] < v_:
                        waits[s_] = (s_, v_)
                for key, (s_, v_) in waits.items():
                    if seen.get(key, 0) >= v_:
                        continue
                    seen[key] = v_
                    handle.wait_ge(s_, v_)
                if op.fn is None:
                    continue
                inst = op.fn(handle)
                if op.dma:
                    inst.then_inc(op.sem, 16)
                elif op.sig:
                    inst.then_inc(op.sem, 1)

        with nc.Block() as block:
            @block.tensor
            def _(h):
                run("pe", h)

            @block.scalar
            def _(h):
                run("act", h)

            @block.vector
            def _(h):
                run("dve", h)

            @block.gpsimd
            def _(h):
                run("pool", h)

            @block.sync
            def _(h):
                run("sp", h)


def bcast_ap(ap, pattern):
    return bass.AP(tensor=ap.tensor, offset=ap.offset, ap=[list(ap.ap[0])] + [list(p) for p in pattern])


def build(n_layers=DEPTH, first_layer=0):
    nc = bass.Bass("TRN2", target_bir_lowering=False)
    dt = nc.dram_tensor
    x_d = dt("x", [S, D], F32, kind="ExternalInput").ap()
    win_d = dt("win", [DEPTH, 8, 128, 3072], F32, kind="ExternalInput").ap()
    wout_d = dt("wout", [DEPTH, 3, 128, 3072], F32, kind="ExternalInput").ap()
    wffn_d = dt("wffn", [DEPTH, NCH, 128, 3072], F32, kind="ExternalInput").ap()
    bias_d = dt("biasT", [12, 128, TW], F32, kind="ExternalInput").ap()
    gcols_d = dt("gcols", [128, GC], F32, kind="ExternalInput").ap()
    lam_d = dt("lamb", [128, DEPTH * 256], F32, kind="ExternalInput").ap()
    consts_d = dt("consts", [128, 384], F32, kind="ExternalInput").ap()
    cnte_d = dt("cnte", [128, 8 * 72], F32, kind="ExternalInput").ap()
    cnto_d = dt("cnto", [128, 8 * 8], F32, kind="ExternalInput").ap()
    kind_d = dt("kind", [8, S], F32, kind="ExternalInput").ap()
    out_d = dt("out", [S, D], F32, kind="ExternalOutput").ap()

    st = ExitStack()
    sb = lambda name, shape, dtype: st.enter_context(nc.sbuf_tensor(name, shape, dtype))
    x = sb("x_sb", [128, NT, D], F32)
    hT = sb("hT", [128, 8, S], BF16)
    yT = sb("yT", [128, 8, S], BF16)
    qk = sb("qk", [128, 6, S], BF16)
    V = sb("V", [128, 1, NT, 128], BF16)
    bias = sb("bias", [128, 2, TW], F32)
    ft = sb("ft", [128, 6, 512], F32)
    bf = sb("bf", [128, 4, 512], BF16)
    wsl = sb("wsl", [128, 3, 3072], BF16)
    gcols = sb("gcols_sb", [128, GC], F32)
    consts = sb("consts_sb", [128, 384], BF16)
    cnte = sb("cnte_sb", [128, 8 * 72], BF16)
    cnto = sb("cnto_sb", [128, 8 * 8], BF16)
    ind = sb("ind", [128, 2, 256], BF16)
    kdiff = sb("kdiff", [128, 2, 64], BF16)
    km = sb("km", [128, 8], F32)
    small = sb("small", [128, 64], F32)
    ps = [st.enter_context(nc.psum_tensor("ps%d" % i, [128, 512], F32)) for i in range(8)]

    ident = consts[:, 0:128]
    ones = consts[:, 128:256]
    bones = consts[:, 256:384]
    SS, RSTD, EPSC, LAM, NLAM, GSUB, TMPC = 0, 16, 32, 33, 37, 41, 45
    eps_col = small[:, EPSC:EPSC + 1]

    P = Prog()

    def A(eng, fn, r=(), w=(), dma=False):
        return P.add(eng, fn, r, w, dma)

    def mm(out, lhsT, rhs, start, stop, r, w):
        A("pe", lambda e: e.matmul(out, lhsT, rhs, start=start, stop=stop), r, w)

    misc_i = [0]

    def misc():
        misc_i[0] = (misc_i[0] + 1) % 8
        return misc_i[0]

    ft_i = [0]

    def ftile():
        ft_i[0] = (ft_i[0] + 1) % 6
        return ft_i[0]

    bf_i = [0]

    def bftile():
        bf_i[0] = (bf_i[0] + 1) % 4
        return bf_i[0]

    bias_seq = []
    bias_loaded = [0]

    def bias_prefetch(upto):
        while bias_loaded[0] < min(upto, len(bias_seq)):
            n = bias_loaded[0]
            hd = bias_seq[n]
            A("sp", lambda e, hd=hd, n=n: e.dma_start(out=bias[:, n % 2, :], in_=bias_d[hd]), w=[("bias", n % 2)], dma=True)
            bias_loaded[0] += 1

    wslot_i = [0]

    def load_w(src):
        k = wslot_i[0] % 3
        wslot_i[0] += 1
        A("pool", lambda e: e.dma_start(out=wsl[:, k, :], in_=src), w=[("w", k)], dma=True)
        return k

    def gc(l, base, j=0):
        c = l * GC_L + base + j
        return gcols[:, c:c + 1]

    A("sp", lambda e: e.dma_start(out=gcols[:], in_=gcols_d), w=["gcols"], dma=True)
    A("sp", lambda e: e.dma_start(out=ft[:, 0:2, :].rearrange("p a b -> p (a b)"), in_=lam_d), w=[("f", 0), ("f", 1)], dma=True)
    A("pool", lambda e: e.dma_start(out=consts[:], in_=consts_d), w=["consts"], dma=True)
    A("pool", lambda e: e.dma_start(out=cnte[:], in_=cnte_d), w=["cnt"], dma=True)
    A("pool", lambda e: e.dma_start(out=cnto[:], in_=cnto_d), w=["cnt"], dma=True)
    for i in range(NT):
        A("sp", lambda e, i=i: e.dma_start(out=x[:, i, :], in_=x_d[i * 128:(i + 1) * 128, :]), w=[("x", i)], dma=True)
    for t in (0, 2, 3, 4, 5):
        A("pool", lambda e, t=t: e.memset(qk[:, t, :], 0.0), w=[("qk", t, tt) for tt in range(4)])
    A("pool", lambda e: e.dma_start(out=qk[64:72, 4, :], in_=kind_d), w=[("qk", 4, tt) for tt in range(4)], dma=True)
    A("pool", lambda e: e.dma_start(out=qk[0:8, 5, :], in_=kind_d), w=[("qk", 5, tt) for tt in range(4)], dma=True)
    A("dve", lambda e: e.memset(small[:], 0.0), w=["small"])
    A("dve", lambda e: e.memset(eps_col, EPS), w=["small"])
    A("dve", lambda e: e.memset(ind[:], 0.0), w=[("ind", 0), ("ind", 1)])
    A("dve", lambda e: e.memset(ind[64:65, :, :], 1.0), w=[("ind", 0), ("ind", 1)])
    A("dve", lambda e: e.memset(kdiff[:], 0.0), w=["kdiff"])
    lamt = ft[:, 0:2, :].rearrange("p a b -> p (a b)")
    for l in range(DEPTH):
        lam_init = 0.8 - 0.6 * math.exp(-0.3 * l)
        b0 = l * 256
        for t in range(2):
            A("dve", lambda e, b0=b0, t=t: e.tensor_tensor(out=ft[:, 2, t * 64:(t + 1) * 64], in0=lamt[:, b0 + t * 128:b0 + t * 128 + 64],
                                                          in1=lamt[:, b0 + t * 128 + 64:b0 + t * 128 + 128], op=ALU.mult),
              r=[("f", 0), ("f", 1)], w=[("f", 2)])
            A("dve", lambda e, t=t: e.reduce_sum(out=small[:, TMPC + t:TMPC + t + 1], in_=ft[:, 2, t * 64:(t + 1) * 64], axis=AX.X),
              r=[("f", 2)], w=["small"])
        A("act", lambda e: e.activation(out=small[:, TMPC + 2:TMPC + 4], in_=small[:, TMPC:TMPC + 2], func=AF.Exp), r=["small"], w=["small"])
        A("dve", lambda e, l=l: e.tensor_tensor(out=small[:, LAM + l:LAM + l + 1], in0=small[:, TMPC + 2:TMPC + 3],
                                                in1=small[:, TMPC + 3:TMPC + 4], op=ALU.subtract), r=["small"], w=["small"])
        A("dve", lambda e, l=l, li=lam_init: e.tensor_scalar(out=small[:, NLAM + l:NLAM + l + 1], in0=small[:, LAM + l:LAM + l + 1],
                                                             scalar1=li, scalar2=-1.0, op0=ALU.add, op1=ALU.mult), r=["small"], w=["small"])
        A("dve", lambda e, l=l, li=lam_init: e.tensor_scalar(out=small[:, GSUB + l:GSUB + l + 1], in0=gc(l, GC_SUB),
                                                             scalar1=1.0 - li, scalar2=0.0, op0=ALU.mult, op1=ALU.add),
          r=["small", "gcols"], w=["small"])

    def norm_to_hT(l, gbase):
        junk = ft[:, 4:6, :].rearrange("p a b -> p (a b)")
        A("dve", lambda e: e.memset(small[:, SS:SS + 16], 0.0), w=[("ss", i) for i in range(NT)])
        for i in range(NT):
            A("act", lambda e, i=i: e.activation(out=junk, in_=x[:, i, :], func=AF.Square, accum_out=small[:, SS + i:SS + i + 1]),
              r=[("x", i)], w=[("f", 4), ("f", 5), ("ss", i)])
        A("act", lambda e: e.activation(out=small[:, RSTD:RSTD + 16], in_=small[:, SS:SS + 16], func=AF.Ln, scale=1.0 / D, bias=eps_col),
          r=[("ss", i) for i in range(NT)] + ["small"], w=["rstd"])
        A("act", lambda e: e.activation(out=small[:, RSTD:RSTD + 16], in_=small[:, RSTD:RSTD + 16], func=AF.Exp, scale=-0.5),
          r=["rstd"], w=["rstd"])
        hb = yT[:].rearrange("p c s -> p (c s)")
        for i in range(NT):
            A("dve", lambda e, i=i: e.tensor_scalar(out=hb[:, i * D:(i + 1) * D], in0=x[:, i, :], scalar1=small[:, RSTD + i:RSTD + i + 1],
                                                    scalar2=0.0, op0=ALU.mult, op1=ALU.add),
              r=[("x", i), "rstd"], w=[("yT", i // 2)])
        for half in range(2):
            for c in range(8):
                b = misc()
                pst = ps[b][:].bitcast(BF16)
                for ii in range(8):
                    i = half * 8 + ii
                    A("pe", lambda e, i=i, ii=ii, c=c, pst=pst: e.transpose(out=pst[:, ii * 128:(ii + 1) * 128],
                                                                            in_=hb[:, i * D + c * 128:i * D + (c + 1) * 128], identity=ident),
                      r=[("yT", i // 2), "consts"], w=[("ps", b)])
                A("act", lambda e, c=c, half=half, pst=pst: e.activation(out=hT[:, c, half * 1024:(half + 1) * 1024], in_=pst[:, 0:1024],
                                                                         func=AF.Identity, scale=gc(l, gbase, c)),
                  r=[("ps", b), "gcols"], w=[("hT", c, half * 2), ("hT", c, half * 2 + 1)])

    def proj_qk_unit(l, slot, specs):
        chunks = [(g, gq_idx, dests, tt) for (g, gq_idx, dests) in specs for tt in range(4)]
        st_ = {}

        def emit_proj(ci):
            g, gq_idx, dests, tt = chunks[ci]
            b = misc()
            for kc in range(8):
                mm(ps[b][:], wsl[:, slot, kc * 384 + g * 128:kc * 384 + (g + 1) * 128], hT[:, kc, tt * 512:(tt + 1) * 512],
                   kc == 0, kc == 7, [("w", slot), ("hT", kc, tt)], [("ps", b)])
            sq = bftile()
            A("act", lambda e, b=b, sq=sq: e.activation(out=bf[:, sq, :], in_=ps[b][:], func=AF.Square), r=[("ps", b)], w=[("bf", sq)])
            st_[ci] = (b, sq)

        def emit_chain(ci):
            g, gq_idx, dests, tt = chunks[ci]
            b, sq = st_[ci]
            b2 = misc()
            mm(ps[b2][:], bones, bf[:, sq, :], True, True, [("bf", sq), "consts"], [("ps", b2)])
            rs = ftile()
            A("act", lambda e, b2=b2, rs=rs: e.activation(out=ft[:, rs, :], in_=ps[b2][:], func=AF.Ln, scale=1.0 / 64, bias=eps_col),
              r=[("ps", b2), "small"], w=[("f", rs)])
            A("act", lambda e, rs=rs: e.activation(out=ft[:, rs, :], in_=ft[:, rs, :], func=AF.Exp, scale=-0.5), r=[("f", rs)], w=[("f", rs)])
            for (r0, r1, t) in dests:
                A("dve", lambda e, b=b, r0=r0, r1=r1, t=t, tt=tt, rs=rs, gq_idx=gq_idx: e.scalar_tensor_tensor(
                    out=qk[r0:r1, t, tt * 512:(tt + 1) * 512], in0=ps[b][r0:r1, :], scalar=gc(l, GC_QK, gq_idx)[r0:r1, :],
                    in1=ft[r0:r1, rs, :], op0=ALU.mult, op1=ALU.mult),
                  r=[("ps", b), ("f", rs), "gcols"], w=[("qk", t, tt)])

        n = len(chunks)
        emit_proj(0)
        for ci in range(n):
            if ci + 1 < n:
                emit_proj(ci + 1)
            emit_chain(ci)

    def proj_v(slot, vb=0):
        for i4 in range(4):
            b = misc()
            for ii in range(4):
                i = i4 * 4 + ii
                for kc in range(8):
                    mm(ps[b][:, ii * 128:(ii + 1) * 128], hT[:, kc, i * 128:(i + 1) * 128], wsl[:, slot, kc * 384 + 256:kc * 384 + 384],
                       kc == 0, kc == 7, [("w", slot), ("hT", kc, i // 4)], [("ps", b)])
            A("act", lambda e, b=b, i4=i4, vb=vb: e.activation(out=V[:, vb, i4 * 4:(i4 + 1) * 4, :].rearrange("p a b -> p (a b)"),
                                                               in_=ps[b][:], func=AF.Copy),
              r=[("ps", b)], w=[("V", vb, i4)])

    def gating(r, qt, kt):
        r0 = r * 64
        kview = qk[r0:r0 + 64, kt, :].rearrange("p (n j) -> p n j", j=256)
        A("dve", lambda e: e.reduce_sum(out=km[r0:r0 + 64, :], in_=kview, axis=AX.X),
          r=[("qk", kt, tt) for tt in range(4)], w=["km"])
        kmv = km[r0:r0 + 64, :]
        A("dve", lambda e: e.tensor_tensor(out=kdiff[r0:r0 + 64, r, :].rearrange("p (a b) -> p a b", b=8),
                                           in0=bcast_ap(kmv, [[0, 8], [1, 8]]), in1=bcast_ap(kmv, [[1, 8], [0, 8]]), op=ALU.subtract),
          r=["km"], w=["kdiff"])
        for b0 in range(0, 8, 2):
            stg = {}
            for blk in (b0, b0 + 1):
                b = misc()
                ib = blk % 2
                mm(ps[b][0:64, 0:256], kdiff[:, r, :], qk[:, qt, blk * 256:(blk + 1) * 256], True, True,
                   ["kdiff", ("qk", qt, blk // 2)], [("ps", b)])
                A("dve", lambda e, b=b, ib=ib: e.tensor_single_scalar(out=ind[0:64, ib, :], in_=ps[b][0:64, 0:256], scalar=0.0, op=ALU.is_gt),
                  r=[("ps", b)], w=[("ind", ib)])
            for blk in (b0, b0 + 1):
                ib = blk % 2
                b2 = misc()
                if r == 0:
                    mm(ps[b2][0:72, 0:256], cnte[:, blk * 72:(blk + 1) * 72], ind[:, ib, :], True, True, [("ind", ib), "cnt"], [("ps", b2)])
                    m0, m1 = 64, 72
                else:
                    mm(ps[b2][0:8, 0:256], cnto[:, blk * 8:(blk + 1) * 8], ind[:, ib, :], True, True, [("ind", ib), "cnt"], [("ps", b2)])
                    m0, m1 = 0, 8
                A("dve", lambda e, b2=b2, m0=m0, m1=m1, blk=blk: e.tensor_scalar(
                    out=qk[m0:m1, qt, blk * 256:(blk + 1) * 256], in0=ps[b2][m0:m1, 0:256], scalar1=2.5, scalar2=NEG,
                    op0=ALU.is_ge, op1=ALU.mult), r=[("ps", b2)], w=[("qk", qt, blk // 2)])

    grp_i = [0]
    e_i = [0]
    lt_i = [0]
    bias_n = [0]

    def attention(l, subs, finalize, fin_act, sub_outer=False):
        tiles = []
        if sub_outer:
            order = [(j, si) for si in range(len(subs)) for j in range(4)]
        else:
            order = [(j, si) for j in range(4) for si in range(len(subs))]
        for (j, si) in order:
            n = 4 * j + 4
            for i in range(n):
                tiles.append((j, si, i, n))
        cur = {"hd": None, "slot": 0}
        state = {}
        LOOK = 3

        def c0_of(j, i):
            d = i - 4 * j
            return 128 * d if d > 0 else 0

        def emit_S(t):
            j, si, i, n = tiles[t]
            sdef = subs[si]
            sbank = t % 3
            qt, q0, q1 = sdef["q"]
            kt, k0, k1 = sdef["k"]
            c0 = c0_of(j, i)
            mm(ps[sbank][:, c0:512], qk[k0:k1, kt, i * 128:(i + 1) * 128], qk[q0:q1, qt, j * 512 + c0:(j + 1) * 512], True, True,
               [("qk", kt, i // 4), ("qk", qt, j)], [("ps", sbank)])

        def emit_rest(t):
            j, si, i, n = tiles[t]
            sdef = subs[si]
            sbank = t % 3
            if i == 0:
                state[(j, si)] = grp_i[0] % 2
                grp_i[0] += 1
            gset = state[(j, si)]
            ob, zb = 3 + 2 * gset, 4 + 2 * gset
            hd = sdef["bias_head"]
            if cur["hd"] != hd:
                assert bias_seq[bias_n[0]] == hd, (bias_seq[bias_n[0]], hd)
                cur["hd"] = hd
                cur["slot"] = bias_n[0] % 2
                bias_n[0] += 1
                bias_prefetch(bias_n[0] + 1)
            bsl = cur["slot"]
            o = 512 * j - 128 * i
            c0 = c0_of(j, i)
            eb = e_i[0] % 3
            e_i[0] += 1
            if o >= 917:
                A("act", lambda e, sbank=sbank, eb=eb, bsl=bsl: e.activation(out=bf[:, eb, :], in_=ps[sbank][:], func=AF.Exp, scale=SCALE,
                                                                             bias=bias[:, bsl, TW - 1:TW]),
                  r=[("ps", sbank), ("bias", bsl)], w=[("bf", eb)])
            else:
                lt = lt_i[0] % 3
                lt_i[0] += 1
                tc0 = o + c0 + TOFF
                A("dve", lambda e, sbank=sbank, lt=lt, tc0=tc0, c0=c0, bsl=bsl: e.scalar_tensor_tensor(
                    out=ft[:, lt, c0:512], in0=ps[sbank][:, c0:512], scalar=SCALE, in1=bias[:, bsl, tc0:tc0 + 512 - c0], op0=ALU.mult, op1=ALU.add),
                  r=[("ps", sbank), ("bias", bsl)], w=[("f", lt)])
                A("act", lambda e, lt=lt, eb=eb, c0=c0: e.activation(out=bf[:, eb, c0:512], in_=ft[:, lt, c0:512], func=AF.Exp),
                  r=[("f", lt)], w=[("bf", eb)])
            tinfo[t] = (ob, zb, c0, eb)

        def emit_pe(t):
            j, si, i, n = tiles[t]
            sdef = subs[si]
            ob, zb, c0, eb = tinfo.pop(t)
            vc = sdef["vcol0"]
            mm(ps[ob][:, c0:512], V[:, 0, i, vc:vc + 128], bf[:, eb, c0:512], i == 0, i == n - 1, [("V", 0, i // 4), ("bf", eb)], [("ps", ob)])
            mm(ps[zb][:, c0:512], ones, bf[:, eb, c0:512], i == 0, i == n - 1, [("bf", eb), "consts"], [("ps", zb)])
            if i == n - 1:
                pending_a.append((t + 1, (j, si, ob, zb)))
                pending.append((t + 3, (j, si, ob, zb)))

        pending = []
        pending_a = []
        tinfo = {}
        T = len(tiles)
        for t in range(min(LOOK, T)):
            emit_S(t)
        for t in range(T):
            emit_rest(t)
            if t + LOOK < T:
                emit_S(t + LOOK)
            emit_pe(t)
            while pending_a and pending_a[0][0] <= t:
                fin_act(*pending_a.pop(0)[1])
            while pending and pending[0][0] <= t:
                finalize(*pending.pop(0)[1])
        while pending_a:
            fin_act(*pending_a.pop(0)[1])
        while pending:
            finalize(*pending.pop(0)[1])

    def layer(l):
        norm_to_hT(l, GC_LN1)
        units = []
        for h in range(4):
            units.append(("d", h))
            units.append(("m", h))
        slots = {}
        slots[0] = load_w(win_d[l, 0])
        slots[1] = load_w(win_d[l, 4])
        for (kind, h) in units:
            if kind == "d":
                bias_seq.append(h)
            else:
                bias_seq.extend([4 + 2 * h, 4 + 2 * h + 1])
        bias_prefetch(bias_n[0] + 2)
        for ui, (kind, h) in enumerate(units):
            slot = slots[ui]
            if kind == "d":
                A("pool", lambda e: e.memset(qk[0:32, 3, :], 0.0), w=[("qk", 3, tt) for tt in range(4)])
                proj_qk_unit(l, slot, [(0, 0, [(0, 64, 0), (64, 128, 3)]), (1, 1, [(0, 128, 1)])])
                proj_v(slot)
            else:
                proj_qk_unit(l, slot, [(0, 2, [(0, 64, 2), (64, 128, 3)]), (1, 3, [(0, 64, 4), (64, 128, 5)])])
                proj_v(slot)
            nxt = ui + 2
            if nxt < 8:
                k2, h2 = units[nxt]
                slots[nxt] = load_w(win_d[l, h2 if k2 == "d" else 4 + h2])
            elif nxt == 8:
                slots[8] = load_w(wout_d[l, 0])
            elif nxt == 9:
                slots[9] = load_w(wout_d[l, 1])

            def recip(zb, dst, r0=0, r1=128):
                A("act", lambda e: e.activation(out=ft[r0:r1, dst, :], in_=ps[zb][r0:r1, :], func=AF.Ln), r=[("ps", zb)], w=[("f", dst)])
                A("act", lambda e: e.activation(out=ft[r0:r1, dst, :], in_=ft[r0:r1, dst, :], func=AF.Exp, scale=-1.0), r=[("f", dst)], w=[("f", dst)])

            if kind == "d":
                subs = [dict(q=(0 if m == 0 else 3, 0, 128), k=(1, 0, 128), vcol0=0, bias_head=h) for m in range(2)]

                def fin_a(j, si, ob, zb):
                    recip(zb, 3)

                def fin(j, si, ob, zb, h=h):
                    cs = slice(j * 512, (j + 1) * 512)
                    if si == 0:
                        A("dve", lambda e: e.tensor_tensor(out=ft[:, 4, :], in0=ps[ob][:], in1=ft[:, 3, :], op=ALU.mult),
                          r=[("ps", ob), ("f", 3)], w=[("f", 4)])
                    else:
                        A("dve", lambda e: e.tensor_tensor(out=ft[:, 3, :], in0=ps[ob][:], in1=ft[:, 3, :], op=ALU.mult),
                          r=[("ps", ob), ("f", 3)], w=[("f", 3)])
                        A("dve", lambda e: e.scalar_tensor_tensor(out=ft[:, 4, :], in0=ft[:, 3, :], scalar=small[:, NLAM + l:NLAM + l + 1],
                                                                  in1=ft[:, 4, :], op0=ALU.mult, op1=ALU.add),
                          r=[("f", 3), ("f", 4), "small"], w=[("f", 4)])
                        A("act", lambda e: e.activation(out=bf[:, 3, :], in_=ft[:, 4, :], func=AF.Square), r=[("f", 4)], w=[("bf", 3)])
                        b2 = 7
                        mm(ps[b2][:], ones, bf[:, 3, :], True, True, [("bf", 3), "consts"], [("ps", b2)])
                        A("act", lambda e: e.activation(out=ft[:, 5, :], in_=ps[b2][:], func=AF.Ln, scale=1.0 / 128, bias=eps_col),
                          r=[("ps", b2), "small"], w=[("f", 5)])
                        A("act", lambda e: e.activation(out=ft[:, 5, :], in_=ft[:, 5, :], func=AF.Exp, scale=-0.5), r=[("f", 5)], w=[("f", 5)])
                        A("dve", lambda e: e.scalar_tensor_tensor(out=yT[:, h, cs], in0=ft[:, 4, :], scalar=small[:, GSUB + l:GSUB + l + 1],
                                                                  in1=ft[:, 5, :], op0=ALU.mult, op1=ALU.mult),
                          r=[("f", 4), ("f", 5), "small"], w=[("yT", h)])
                attention(l, subs, fin, fin_a)
            else:
                gating(0, 2, 4)
                gating(1, 3, 5)
                subs = [dict(q=(2 + r, 0, 128), k=(4 + r, 0, 128), vcol0=0, bias_head=4 + 2 * h + r) for r in range(2)]

                def fin_a(j, si, ob, zb):
                    recip(zb, 3, si * 64, si * 64 + 64)

                def fin(j, si, ob, zb, h=h):
                    cs = slice(j * 512, (j + 1) * 512)
                    r0 = si * 64
                    A("dve", lambda e: e.tensor_tensor(out=yT[r0:r0 + 64, 4 + h, cs], in0=ps[ob][r0:r0 + 64, :], in1=ft[r0:r0 + 64, 3, :],
                                                       op=ALU.mult), r=[("ps", ob), ("f", 3)], w=[("yT", 4 + h)])
                attention(l, subs, fin, fin_a, sub_outer=True)
        ALLQK = [("qk", t, tt) for t in range(6) for tt in range(4)]
        ALLU = [("U", br, tt) for br in range(2) for tt in range(4)]
        ALLWD = [("wd", k5) for k5 in range(5)]
        A("dve", lambda e: e.memset(bias[:, 0, 0:2], 0.0), w=[("bias", 0), ("bias", 1)] + ALLWD)
        wd = bias[:].rearrange("p a b -> p (a b)").bitcast(BF16)
        fslots = {}

        def load_wup(j):
            k = wslot_i[0] % 3
            wslot_i[0] += 1
            A("pool", lambda e: e.dma_start(out=wsl[:, k, 0:2048], in_=wffn_d[l, j][:, 0:2048]), w=[("w", k)], dma=True)
            fslots[j] = k

        def load_wdown(j):
            sl = j % 5
            A("pool", lambda e: e.dma_start(out=wd[:, sl * 1024:(sl + 1) * 1024], in_=wffn_d[l, j][:, 2048:3072]), w=[("wd", sl)], dma=True)

        slots[10] = load_w(wout_d[l, 2])
        load_wdown(0)
        for s3 in range(3):
            slot = slots[8 + s3]
            ns = 384 if s3 < 2 else 256
            d0 = s3 * 384
            for i in range(NT):
                b = misc()
                for fc in range(8):
                    mm(ps[b][:, 0:ns], yT[:, fc, i * 128:(i + 1) * 128], wsl[:, slot, fc * 384:fc * 384 + ns], fc == 0, fc == 7,
                       [("yT", fc), ("w", slot)], [("ps", b)])
                A("dve", lambda e, b=b, i=i, d0=d0, ns=ns: e.tensor_tensor(out=x[:, i, d0:d0 + ns], in0=ps[b][:, 0:ns], in1=x[:, i, d0:d0 + ns],
                                                                           op=ALU.add), r=[("ps", b), ("x", i)], w=[("x", i)])
            load_wup(s3)
        norm_to_hT(l, GC_LN2)
        qkf = qk[:].rearrange("p a b -> p (a b)").bitcast(F32)
        U = [qkf[:, 0:2050], qkf[:, 2052:4102]]
        A("dve", lambda e: e.memset(qkf[:, 0:4104], 0.0), w=ALLQK + ALLU)
        GRP = 4
        ubanks = [0, 1, 2, 3, 6, 7]
        ub = [0]
        fset = [0]
        dbank = [0]
        pend = []

        def tail(j, tt, tg, tu):
            A("act", lambda e: e.activation(out=ft[:, tg, :], in_=ft[:, tg, :], func=AF.Silu), r=[("f", tg)], w=[("f", tg)])
            A("dve", lambda e: e.tensor_tensor(out=yT[:, j % 8, tt * 512:(tt + 1) * 512], in0=ft[:, tg, :], in1=ft[:, tu, :], op=ALU.mult),
              r=[("f", tg), ("f", tu)], w=[("yT", j % 8)])

        def down_proj(js):
            for i in range(NT):
                for dh in range(2):
                    b = 4 + dbank[0] % 2
                    dbank[0] += 1
                    for n_, jj in enumerate(js):
                        mm(ps[b][:], yT[:, jj % 8, i * 128:(i + 1) * 128], wd[:, (jj % 5) * 1024 + dh * 512:(jj % 5) * 1024 + (dh + 1) * 512],
                           n_ == 0, n_ == len(js) - 1, [("yT", jj % 8), ("wd", jj % 5)], [("ps", b)])
                    A("dve", lambda e, b=b, i=i, dh=dh: e.tensor_tensor(out=x[:, i, dh * 512:(dh + 1) * 512], in0=ps[b][:],
                                                                         in1=x[:, i, dh * 512:(dh + 1) * 512], op=ALU.add),
                      r=[("ps", b), ("x", i)], w=[("x", i)])

        for j in range(NCH):
            slot = fslots[j]
            for tt in range(4):
                sset = fset[0] % 3
                fset[0] += 1
                tl = [2 * sset, 2 * sset + 1]
                for br in range(2):
                    b = ubanks[ub[0] % 6]
                    ub[0] += 1
                    for kc in range(8):
                        mm(ps[b][:], wsl[:, slot, kc * 256 + br * 128:kc * 256 + (br + 1) * 128], hT[:, kc, tt * 512:(tt + 1) * 512],
                           kc == 0, kc == 7, [("w", slot), ("hT", kc, tt)], [("ps", b)])
                    A("act", lambda e, b=b, br=br, tt=tt: e.activation(out=U[br][:, 2 + tt * 512:2 + (tt + 1) * 512], in_=ps[b][:], func=AF.Copy),
                      r=[("ps", b)], w=[("U", br, tt)])
                    tmp = tl[br]
                    cw0, cw1, cw2 = [gc(l, GC_CW, k * 44 + br * NCH + j) for k in range(3)]
                    cbb = gc(l, GC_CB, br * NCH + j)
                    u1 = U[br][:, 1 + tt * 512:1 + (tt + 1) * 512]
                    u0 = U[br][:, tt * 512:(tt + 1) * 512]
                    A("act", lambda e, b=b, tmp=tmp, cw2=cw2, cbb=cbb: e.activation(out=ft[:, tmp, :], in_=ps[b][:], func=AF.Identity,
                                                                                   scale=cw2, bias=cbb),
                      r=[("ps", b), "gcols"], w=[("f", tmp)])
                    rk = [("U", br, tt), ("U", br, max(tt - 1, 0)), ("f", tmp), "gcols"]
                    A("dve", lambda e, tmp=tmp, u1=u1, cw1=cw1: e.scalar_tensor_tensor(
                        out=ft[:, tmp, :], in0=u1, scalar=cw1, in1=ft[:, tmp, :], op0=ALU.mult, op1=ALU.add), r=rk, w=[("f", tmp)])
                    A("dve", lambda e, tmp=tmp, u0=u0, cw0=cw0: e.scalar_tensor_tensor(
                        out=ft[:, tmp, :], in0=u0, scalar=cw0, in1=ft[:, tmp, :], op0=ALU.mult, op1=ALU.add), r=rk, w=[("f", tmp)])
                if pend:
                    tail(*pend.pop(0))
                pend.append((j, tt, tl[0], tl[1]))
                if tt == 0 and j >= 1 and (j % GRP) == 0:
                    down_proj(list(range(j - GRP, j)))
            if j + 3 < NCH:
                load_wup(j + 3)
            if j + 1 < NCH:
                load_wdown(j + 1)
        while pend:
            tail(*pend.pop(0))
        down_proj(list(range((NCH // GRP) * GRP, NCH)))
        A("dve", lambda e: e.memset(bias[:, 0, 0:2], 0.0), w=[("bias", 0), ("bias", 1)] + ALLWD)
        if l + 1 < first_layer + n_layers:
            A("pool", lambda e: e.memset(qk[:].rearrange("p a b -> p (a b)"), 0.0), w=ALLQK + ALLU)
            A("pool", lambda e: e.dma_start(out=qk[64:72, 4, :], in_=kind_d), w=[("qk", 4, tt) for tt in range(4)], dma=True)
            A("pool", lambda e: e.dma_start(out=qk[0:8, 5, :], in_=kind_d), w=[("qk", 5, tt) for tt in range(4)], dma=True)

    for l in range(first_layer, first_layer + n_layers):
        layer(l)

    for i in range(NT):
        A("sp", lambda e, i=i: e.dma_start(out=out_d[i * 128:(i + 1) * 128, :], in_=x[:, i, :]), r=[("x", i)], w=[("out", i)], dma=True)
    A("sp", None, r=[("out", i) for i in range(NT)])
    P.ops[-1].fn = None

    P.emit(nc, st)
    st.close()
    return nc


def rel_bucket_np(dist):
    n = np.maximum(dist, 0)
    nf = np.maximum(n, 16).astype(np.float32)
    large = 16 + (np.log(nf / np.float32(16)) / np.float32(math.log(1024 / 16)) * np.float32(16)).astype(np.int32)
    large = np.minimum(large, 31)
    return np.where(n < 16, n, large)


def host_layout(inp):
    f = lambda a: np.ascontiguousarray(np.asarray(a, dtype=np.float32))
    w_in, w_out, w_up, w_down = f(inp["w_in"]), f(inp["w_out"]), f(inp["w_up"]), f(inp["w_down"])
    L = DEPTH
    win = np.zeros((L, 8, 128, 8, 384), np.float32)
    for u in range(8):
        if u < 4:
            cols = np.concatenate([np.arange(128) + 128 * u, 512 + np.arange(128) + 128 * u, 1024 + np.arange(128) + 128 * u])
        else:
            p = u - 4
            cols = np.concatenate([1536 + np.arange(128) + 128 * p, 2048 + np.arange(128) + 128 * p, 2560 + np.arange(128) + 128 * p])
        win[:, u] = w_in[:, :, cols].reshape(L, 8, 128, 384).transpose(0, 2, 1, 3)
    wout = np.zeros((L, 3, 128, 8, 384), np.float32)
    for s3 in range(3):
        ns = 384 if s3 < 2 else 256
        wout[:, s3, :, :, :ns] = w_out[:, :, s3 * 384:s3 * 384 + ns].reshape(L, 8, 128, ns).transpose(0, 2, 1, 3)
    wffn = np.zeros((L, NCH, 128, 3072), np.float32)
    for j in range(NCH):
        cols = np.concatenate([np.arange(128) + 128 * j, DFF + np.arange(128) + 128 * j])
        wffn[:, j, :, :2048] = w_up[:, :, cols].reshape(L, 8, 128, 256).transpose(0, 2, 1, 3).reshape(L, 128, 2048)
        wffn[:, j, :, 2048:] = w_down[:, j * 128:(j + 1) * 128, :]
    rb = f(inp["rel_bias"])
    pp = np.arange(128)[:, None]
    cc = np.arange(TW)[None, :]
    dist = cc - pp - TOFF
    bidx = rel_bucket_np(dist)
    biasT = np.empty((12, 128, TW), np.float32)
    for h in range(12):
        biasT[h] = np.where(dist >= 0, rb[bidx, h], np.float32(NEG))
    gcols = np.zeros((128, GC), np.float32)
    ln1, ln2 = f(inp["ln_attn_g"]), f(inp["ln_ffn_g"])
    qkg, sub = f(inp["qk_norm_g"]), f(inp["diff_subln_g"])
    cw, cb = f(inp["conv_w"]), f(inp["conv_b"])
    for l in range(L):
        o = l * GC_L
        gcols[:, o + GC_LN1:o + GC_LN1 + 8] = ln1[l].reshape(8, 128).T
        gcols[:, o + GC_LN2:o + GC_LN2 + 8] = ln2[l].reshape(8, 128).T
        for k in range(4):
            gcols[:, o + GC_QK + k] = np.tile(qkg[l, k], 2)
        gcols[:, o + GC_SUB] = sub[l]
        for k in range(3):
            gcols[:, o + GC_CW + k * 44:o + GC_CW + (k + 1) * 44] = cw[l, k].reshape(44, 128).T
        gcols[:, o + GC_CB:o + GC_CB + 44] = cb[l].reshape(44, 128).T
    lamb = np.broadcast_to(f(inp["diff_lambda"]).reshape(1, L * 256), (128, L * 256)).copy()
    consts = np.zeros((128, 384), np.float32)
    consts[:, 0:128] = np.eye(128, dtype=np.float32)
    consts[:, 128:256] = 1.0
    consts[0:64, 256:320] = 1.0
    consts[64:128, 320:384] = 1.0
    cnte = np.zeros((128, 8, 72), np.float32)
    cnto = np.zeros((128, 8, 8), np.float32)
    for b in range(8):
        for n in range(8):
            c = 0.0 if n < b else (-10.0 if n == b else 10.0)
            cnte[64, b, 64 + n] = c
            cnto[64, b, n] = c
            if n < b:
                for n2 in range(b):
                    cnte[n * 8 + n2, b, 64 + n] = 1.0
                    cnto[n * 8 + n2, b, n] = 1.0
    kind = np.zeros((8, S), np.float32)
    for n in range(8):
        kind[n, n * 256:(n + 1) * 256] = 1.0
    shared = dict(win=win.reshape(L, 8, 128, 3072), wout=wout.reshape(L, 3, 128, 3072), wffn=wffn, biasT=biasT, gcols=gcols,
                  lamb=lamb, consts=consts, cnte=cnte.reshape(128, 576), cnto=cnto.reshape(128, 64), kind=kind)
    return shared


_NC_CACHE = {}


def kernel(**inputs):
    x = np.ascontiguousarray(np.asarray(inputs["x"], dtype=np.float32))
    shared = host_layout(inputs)
    if "nc" not in _NC_CACHE:
        _NC_CACHE["nc"] = build(DEPTH, 0)
    nc = _NC_CACHE["nc"]
    in_maps = [dict(shared, x=x[b]) for b in range(8)]
    res = run_bass_kernel_spmd(nc, in_maps, core_ids=list(range(8)))
    return np.stack([np.asarray(r["out"], dtype=np.float32) for r in res.results], axis=0)
```

```python
import math
from contextlib import ExitStack
import numpy as np
import concourse.bass as bass
import concourse.mybir as mybir
from concourse.bass_utils import run_bass_kernel_spmd

F32 = mybir.dt.float32
BF16 = mybir.dt.bfloat16
ALU = mybir.AluOpType
AF = mybir.ActivationFunctionType
AX = mybir.AxisListType

S = 2048
D = 1024
NT = 16
DEPTH = 4
DFF = 2816
NCH = 22
EPS = 1e-6
NEG = -30000.0
TW = 1408
TOFF = 0
SCALE = 0.125

GC_LN1 = 0
GC_LN2 = 8
GC_QK = 16
GC_SUB = 20
GC_CW = 21
GC_CB = 21 + 132
GC_L = 21 + 132 + 44
GC = GC_L * DEPTH


class Op:
    __slots__ = ("eng", "fn", "deps", "sig", "pos", "dma", "sem", "val")

    def __init__(self, eng, fn, dma):
        self.eng = eng
        self.fn = fn
        self.dma = dma
        self.deps = ()
        self.sig = dma
        self.pos = 0
        self.sem = None
        self.val = 0


class Prog:
    ENGS = ("pe", "act", "dve", "pool", "sp")

    def __init__(self):
        self.ops = []
        self.lastw = {}
        self.readers = {}
        self.cnt = {e: 0 for e in self.ENGS}
        self.last_on = {e: None for e in self.ENGS}
        self.dma_since_barrier = []

    def add(self, eng, fn, r=(), w=(), dma=False):
        op = Op(eng, fn, dma)
        idx = len(self.ops)
        op.pos = self.cnt[eng]
        self.cnt[eng] += 1
        deps = set()
        for k in r:
            lw = self.lastw.get(k)
            if lw is not None:
                deps.add(lw)
        for k in w:
            lw = self.lastw.get(k)
            if lw is not None:
                deps.add(lw)
            rd = self.readers.get(k)
            if rd:
                deps.update(rd[0].values())
                deps.update(rd[1])
        for k in r:
            rd = self.readers.setdefault(k, ({}, []))
            if dma:
                rd[1].append(idx)
            else:
                rd[0][eng] = idx
        for k in w:
            self.lastw[k] = idx
            self.readers[k] = ({}, [])
        deps.discard(idx)
        op.deps = self._filter(op, deps)
        self.ops.append(op)
        self.last_on[eng] = idx
        if dma:
            self.dma_since_barrier.append(idx)
        return idx

    def _filter(self, op, deps):
        out = []
        for d in deps:
            p = self.ops[d]
            if (not p.dma) and p.eng == op.eng and not op.dma:
                if op.eng == "pe":
                    continue
                if op.pos - p.pos > 3:
                    continue
            p.sig = True
            out.append(d)
        return tuple(out)

    def barrier(self):
        lasts = [v for v in self.last_on.values() if v is not None]
        dmas = list(self.dma_since_barrier)
        self.dma_since_barrier = []
        for e in self.ENGS:
            op = Op(e, None, False)
            op.pos = self.cnt[e]
            self.cnt[e] += 1
            deps = set(lasts) | set(dmas)
            op.deps = self._filter(op, deps)
            self.ops.append(op)
            self.last_on[e] = len(self.ops) - 1

    def emit(self, nc, stack):
        eng_sem = {e: stack.enter_context(nc.semaphore("sem_" + e)) for e in ("pe", "act", "dve", "pool")}
        NDS = 20
        dsems = {q: [stack.enter_context(nc.semaphore("dq_%s_%d" % (q, i))) for i in range(NDS)]
                 for q in ("sp", "pool", "act")}
        duse = {q: [0] * NDS for q in dsems}
        dnext = {q: 0 for q in dsems}
        ccount = {e: 0 for e in eng_sem}
        prev_wait = {}
        for i, op in enumerate(self.ops):
            if op.fn is None:
                continue
            if op.dma:
                q = op.eng
                k = dnext[q]
                dnext[q] = (k + 1) % NDS
                op.sem = dsems[q][k]
                prev_wait[i] = (op.sem, duse[q][k] * 16)
                duse[q][k] += 1
                op.val = duse[q][k] * 16
            elif op.sig:
                ccount[op.eng] += 1
                op.sem = eng_sem[op.eng]
                op.val = ccount[op.eng]
        per_eng = {e: [] for e in self.ENGS}
        for i, op in enumerate(self.ops):
            per_eng[op.eng].append(i)
        ops = self.ops

        def run(e, handle):
            seen = {}
            for i in per_eng[e]:
                op = ops[i]
                waits = {}
                for d in op.deps:
                    p = ops[d]
                    if p.sem is None:
                        continue
                    key = p.sem
                    if waits.get(key, (None, 0))[1] < p.val:
                        waits[key] = (p.sem, p.val)
                if i in prev_wait:
                    s_, v_ = prev_wait[i]
                    if v_ > 0 and waits.get(s_, (None, 0))[1] < v_:
                        waits[s_] = (s_, v_)
                for key, (s_, v_) in waits.items():
                    if seen.get(key, 0) >= v_:
                        continue
                    seen[key] = v_
                    handle.wait_ge(s_, v_)
                if op.fn is None:
                    continue
                inst = op.fn(handle)
                if op.dma:
                    inst.then_inc(op.sem, 16)
                elif op.sig:
                    inst.then_inc(op.sem, 1)

        with nc.Block() as block:
            @block.tensor
            def _(h):
                run("pe", h)

            @block.scalar
            def _(h):
                run("act", h)

            @block.vector
            def _(h):
                run("dve", h)

            @block.gpsimd
            def _(h):
                run("pool", h)

            @block.sync
            def _(h):
                run("sp", h)


def bcast_ap(ap, pattern):
    return bass.AP(tensor=ap.tensor, offset=ap.offset, ap=[list(ap.ap[0])] + [list(p) for p in pattern])


def build(n_layers=DEPTH, first_layer=0):
    nc = bass.Bass("TRN2", target_bir_lowering=False)
    dt = nc.dram_tensor
    x_d = dt("x", [S, D], F32, kind="ExternalInput").ap()
    win_d = dt("win", [DEPTH, 8, 128, 3072], F32, kind="ExternalInput").ap()
    wout_d = dt("wout", [DEPTH, 3, 128, 3072], F32, kind="ExternalInput").ap()
    wffn_d = dt("wffn", [DEPTH, NCH, 128, 3072], F32, kind="ExternalInput").ap()
    bias_d = dt("biasT", [12, 128, TW], F32, kind="ExternalInput").ap()
    gcols_d = dt("gcols", [128, GC], F32, kind="ExternalInput").ap()
    lam_d = dt("lamb", [128, DEPTH * 256], F32, kind="ExternalInput").ap()
    consts_d = dt("consts", [128, 384], F32, kind="ExternalInput").ap()
    cnte_d = dt("cnte", [128, 8 * 72], F32, kind="ExternalInput").ap()
    cnto_d = dt("cnto", [128, 8 * 8], F32, kind="ExternalInput").ap()
    kind_d = dt("kind", [8, S], F32, kind="ExternalInput").ap()
    out_d = dt("out", [S, D], F32, kind="ExternalOutput").ap()

    st = ExitStack()
    sb = lambda name, shape, dtype: st.enter_context(nc.sbuf_tensor(name, shape, dtype))
    x = sb("x_sb", [128, NT, D], F32)
    hT = sb("hT", [128, 8, S], BF16)
    yT = sb("yT", [128, 8, S], BF16)
    qk = sb("qk", [128, 6, S], BF16)
    V = sb("V", [128, 1, NT, 128], BF16)
    bias = sb("bias", [128, 2, TW], F32)
    ft = sb("ft", [128, 6, 512], F32)
    bf = sb("bf", [128, 4, 512], BF16)
    wsl = sb("wsl", [128, 3, 3072], BF16)
    gcols = sb("gcols_sb", [128, GC], F32)
    consts = sb("consts_sb", [128, 384], BF16)
    cnte = sb("cnte_sb", [128, 8 * 72], BF16)
    cnto = sb("cnto_sb", [128, 8 * 8], BF16)
    ind = sb("ind", [128, 2, 256], BF16)
    kdiff = sb("kdiff", [128, 2, 64], BF16)
    km = sb("km", [128, 8], F32)
    small = sb("small", [128, 64], F32)
    ps = [st.enter_context(nc.psum_tensor("ps%d" % i, [128, 512], F32)) for i in range(8)]

    ident = consts[:, 0:128]
    ones = consts[:, 128:256]
    bones = consts[:, 256:384]
    SS, RSTD, EPSC, LAM, NLAM, GSUB, TMPC = 0, 16, 32, 33, 37, 41, 45
    eps_col = small[:, EPSC:EPSC + 1]

    P = Prog()

    def A(eng, fn, r=(), w=(), dma=False):
        return P.add(eng, fn, r, w, dma)

    def mm(out, lhsT, rhs, start, stop, r, w):
        A("pe", lambda e: e.matmul(out, lhsT, rhs, start=start, stop=stop), r, w)

    misc_i = [0]

    def misc():
        misc_i[0] = (misc_i[0] + 1) % 8
        return misc_i[0]

    ft_i = [0]

    def ftile():
        ft_i[0] = (ft_i[0] + 1) % 6
        return ft_i[0]

    bf_i = [0]

    def bftile():
        bf_i[0] = (bf_i[0] + 1) % 4
        return bf_i[0]

    bias_seq = []
    bias_loaded = [0]

    def bias_prefetch(upto):
        while bias_loaded[0] < min(upto, len(bias_seq)):
            n = bias_loaded[0]
            hd = bias_seq[n]
            A("sp", lambda e, hd=hd, n=n: e.dma_start(out=bias[:, n % 2, :], in_=bias_d[hd]), w=[("bias", n % 2)], dma=True)
            bias_loaded[0] += 1

    wslot_i = [0]

    def load_w(src):
        k = wslot_i[0] % 3
        wslot_i[0] += 1
        A("pool", lambda e: e.dma_start(out=wsl[:, k, :], in_=src), w=[("w", k)], dma=True)
        return k

    def gc(l, base, j=0):
        c = l * GC_L + base + j
        return gcols[:, c:c + 1]

    A("sp", lambda e: e.dma_start(out=gcols[:], in_=gcols_d), w=["gcols"], dma=True)
    A("sp", lambda e: e.dma_start(out=ft[:, 0:2, :].rearrange("p a b -> p (a b)"), in_=lam_d), w=[("f", 0), ("f", 1)], dma=True)
    A("pool", lambda e: e.dma_start(out=consts[:], in_=consts_d), w=["consts"], dma=True)
    A("pool", lambda e: e.dma_start(out=cnte[:], in_=cnte_d), w=["cnt"], dma=True)
    A("pool", lambda e: e.dma_start(out=cnto[:], in_=cnto_d), w=["cnt"], dma=True)
    for i in range(NT):
        A("sp", lambda e, i=i: e.dma_start(out=x[:, i, :], in_=x_d[i * 128:(i + 1) * 128, :]), w=[("x", i)], dma=True)
    for t in (0, 2, 3, 4, 5):
        A("pool", lambda e, t=t: e.memset(qk[:, t, :], 0.0), w=[("qk", t, tt) for tt in range(4)])
    A("pool", lambda e: e.dma_start(out=qk[64:72, 4, :], in_=kind_d), w=[("qk", 4, tt) for tt in range(4)], dma=True)
    A("pool", lambda e: e.dma_start(out=qk[0:8, 5, :], in_=kind_d), w=[("qk", 5, tt) for tt in range(4)], dma=True)
    A("dve", lambda e: e.memset(small[:], 0.0), w=["small"])
    A("dve", lambda e: e.memset(eps_col, EPS), w=["small"])
    A("dve", lambda e: e.memset(ind[:], 0.0), w=[("ind", 0), ("ind", 1)])
    A("dve", lambda e: e.memset(ind[64:65, :, :], 1.0), w=[("ind", 0), ("ind", 1)])
    A("dve", lambda e: e.memset(kdiff[:], 0.0), w=["kdiff"])
    lamt = ft[:, 0:2, :].rearrange("p a b -> p (a b)")
    for l in range(DEPTH):
        lam_init = 0.8 - 0.6 * math.exp(-0.3 * l)
        b0 = l * 256
        for t in range(2):
            A("dve", lambda e, b0=b0, t=t: e.tensor_tensor(out=ft[:, 2, t * 64:(t + 1) * 64], in0=lamt[:, b0 + t * 128:b0 + t * 128 + 64],
                                                          in1=lamt[:, b0 + t * 128 + 64:b0 + t * 128 + 128], op=ALU.mult),
              r=[("f", 0), ("f", 1)], w=[("f", 2)])
            A("dve", lambda e, t=t: e.reduce_sum(out=small[:, TMPC + t:TMPC + t + 1], in_=ft[:, 2, t * 64:(t + 1) * 64], axis=AX.X),
              r=[("f", 2)], w=["small"])
        A("act", lambda e: e.activation(out=small[:, TMPC + 2:TMPC + 4], in_=small[:, TMPC:TMPC + 2], func=AF.Exp), r=["small"], w=["small"])
        A("dve", lambda e, l=l: e.tensor_tensor(out=small[:, LAM + l:LAM + l + 1], in0=small[:, TMPC + 2:TMPC + 3],
                                                in1=small[:, TMPC + 3:TMPC + 4], op=ALU.subtract), r=["small"], w=["small"])
        A("dve", lambda e, l=l, li=lam_init: e.tensor_scalar(out=small[:, NLAM + l:NLAM + l + 1], in0=small[:, LAM + l:LAM + l + 1],
                                                             scalar1=li, scalar2=-1.0, op0=ALU.add, op1=ALU.mult), r=["small"], w=["small"])
        A("dve", lambda e, l=l, li=lam_init: e.tensor_scalar(out=small[:, GSUB + l:GSUB + l + 1], in0=gc(l, GC_SUB),
                                                             scalar1=1.0 - li, scalar2=0.0, op0=ALU.mult, op1=ALU.add),
          r=["small", "gcols"], w=["small"])

    def norm_to_hT(l, gbase):
        junk = ft[:, 4:6, :].rearrange("p a b -> p (a b)")
        A("dve", lambda e: e.memset(small[:, SS:SS + 16], 0.0), w=[("ss", i) for i in range(NT)])
        hb = yT[:].rearrange("p c s -> p (c s)")
        for g4 in range(4):
            for i in range(g4 * 4, g4 * 4 + 4):
                A("act", lambda e, i=i: e.activation(out=junk, in_=x[:, i, :], func=AF.Square, accum_out=small[:, SS + i:SS + i + 1]),
                  r=[("x", i)], w=[("f", 4), ("f", 5), ("ss", i)])
            c0_, c1_ = RSTD + g4 * 4, RSTD + g4 * 4 + 4
            A("act", lambda e, g4=g4, c0_=c0_, c1_=c1_: e.activation(out=small[:, c0_:c1_], in_=small[:, SS + g4 * 4:SS + g4 * 4 + 4],
                                                                 func=AF.Ln, scale=1.0 / D, bias=eps_col),
              r=[("ss", i) for i in range(g4 * 4, g4 * 4 + 4)] + ["small"], w=[("rstd", g4)])
            A("act", lambda e, c0_=c0_, c1_=c1_: e.activation(out=small[:, c0_:c1_], in_=small[:, c0_:c1_], func=AF.Exp, scale=-0.5),
              r=[("rstd", g4)], w=[("rstd", g4)])
            for i in range(g4 * 4, g4 * 4 + 4):
                A("dve", lambda e, i=i: e.tensor_scalar(out=hb[:, i * D:(i + 1) * D], in0=x[:, i, :], scalar1=small[:, RSTD + i:RSTD + i + 1],
                                                        scalar2=0.0, op0=ALU.mult, op1=ALU.add),
                  r=[("x", i), ("rstd", g4)], w=[("yT", i // 2)])
        for half in range(2):
            for c in range(8):
                b = misc()
                pst = ps[b][:].bitcast(BF16)
                for ii in range(8):
                    i = half * 8 + ii
                    A("pe", lambda e, i=i, ii=ii, c=c, pst=pst: e.transpose(out=pst[:, ii * 128:(ii + 1) * 128],
                                                                            in_=hb[:, i * D + c * 128:i * D + (c + 1) * 128], identity=ident),
                      r=[("yT", i // 2), "consts"], w=[("ps", b)])
                A("act", lambda e, c=c, half=half, pst=pst: e.activation(out=hT[:, c, half * 1024:(half + 1) * 1024], in_=pst[:, 0:1024],
                                                                         func=AF.Identity, scale=gc(l, gbase, c)),
                  r=[("ps", b), "gcols"], w=[("hT", c, half * 2), ("hT", c, half * 2 + 1)])

    def proj_qk_unit(l, slot, specs):
        chunks = [(g, gq_idx, dests, tt) for (g, gq_idx, dests) in specs for tt in range(4)]
        st_ = {}

        def emit_proj(ci):
            g, gq_idx, dests, tt = chunks[ci]
            b = misc()
            for kc in range(8):
                mm(ps[b][:], wsl[:, slot, kc * 384 + g * 128:kc * 384 + (g + 1) * 128], hT[:, kc, tt * 512:(tt + 1) * 512],
                   kc == 0, kc == 7, [("w", slot), ("hT", kc, tt)], [("ps", b)])
            sq = bftile()
            A("act", lambda e, b=b, sq=sq: e.activation(out=bf[:, sq, :], in_=ps[b][:], func=AF.Square), r=[("ps", b)], w=[("bf", sq)])
            st_[ci] = (b, sq)

        def emit_chain(ci):
            g, gq_idx, dests, tt = chunks[ci]
            b, sq = st_[ci]
            b2 = misc()
            mm(ps[b2][:], bones, bf[:, sq, :], True, True, [("bf", sq), "consts"], [("ps", b2)])
            rs = ftile()
            A("act", lambda e, b2=b2, rs=rs: e.activation(out=ft[:, rs, :], in_=ps[b2][:], func=AF.Ln, scale=1.0 / 64, bias=eps_col),
              r=[("ps", b2), "small"], w=[("f", rs)])
            A("act", lambda e, rs=rs: e.activation(out=ft[:, rs, :], in_=ft[:, rs, :], func=AF.Exp, scale=-0.5), r=[("f", rs)], w=[("f", rs)])
            for (r0, r1, t) in dests:
                A("dve", lambda e, b=b, r0=r0, r1=r1, t=t, tt=tt, rs=rs, gq_idx=gq_idx: e.scalar_tensor_tensor(
                    out=qk[r0:r1, t, tt * 512:(tt + 1) * 512], in0=ps[b][r0:r1, :], scalar=gc(l, GC_QK, gq_idx)[r0:r1, :],
                    in1=ft[r0:r1, rs, :], op0=ALU.mult, op1=ALU.mult),
                  r=[("ps", b), ("f", rs), "gcols"], w=[("qk", t, tt)])

        n = len(chunks)
        emit_proj(0)
        for ci in range(n):
            if ci + 1 < n:
                emit_proj(ci + 1)
            emit_chain(ci)

    def proj_v(slot, vb=0):
        for i4 in range(4):
            b = misc()
            for ii in range(4):
                i = i4 * 4 + ii
                for kc in range(8):
                    mm(ps[b][:, ii * 128:(ii + 1) * 128], hT[:, kc, i * 128:(i + 1) * 128], wsl[:, slot, kc * 384 + 256:kc * 384 + 384],
                       kc == 0, kc == 7, [("w", slot), ("hT", kc, i // 4)], [("ps", b)])
            A("act", lambda e, b=b, i4=i4, vb=vb: e.activation(out=V[:, vb, i4 * 4:(i4 + 1) * 4, :].rearrange("p a b -> p (a b)"),
                                                               in_=ps[b][:], func=AF.Copy),
              r=[("ps", b)], w=[("V", vb, i4)])

    def gating(r, qt, kt):
        r0 = r * 64
        kview = qk[r0:r0 + 64, kt, :].rearrange("p (n j) -> p n j", j=256)
        A("dve", lambda e: e.reduce_sum(out=km[r0:r0 + 64, :], in_=kview, axis=AX.X),
          r=[("qk", kt, tt) for tt in range(4)], w=["km"])
        kmv = km[r0:r0 + 64, :]
        A("dve", lambda e: e.tensor_tensor(out=kdiff[r0:r0 + 64, r, :].rearrange("p (a b) -> p a b", b=8),
                                           in0=bcast_ap(kmv, [[0, 8], [1, 8]]), in1=bcast_ap(kmv, [[1, 8], [0, 8]]), op=ALU.subtract),
          r=["km"], w=["kdiff"])
        for b0 in range(0, 8, 2):
            stg = {}
            for blk in (b0, b0 + 1):
                b = misc()
                ib = blk % 2
                mm(ps[b][0:64, 0:256], kdiff[:, r, :], qk[:, qt, blk * 256:(blk + 1) * 256], True, True,
                   ["kdiff", ("qk", qt, blk // 2)], [("ps", b)])
                A("dve", lambda e, b=b, ib=ib: e.tensor_single_scalar(out=ind[0:64, ib, :], in_=ps[b][0:64, 0:256], scalar=0.0, op=ALU.is_gt),
                  r=[("ps", b)], w=[("ind", ib)])
            for blk in (b0, b0 + 1):
                ib = blk % 2
                b2 = misc()
                if r == 0:
                    mm(ps[b2][0:72, 0:256], cnte[:, blk * 72:(blk + 1) * 72], ind[:, ib, :], True, True, [("ind", ib), "cnt"], [("ps", b2)])
                    m0, m1 = 64, 72
                else:
                    mm(ps[b2][0:8, 0:256], cnto[:, blk * 8:(blk + 1) * 8], ind[:, ib, :], True, True, [("ind", ib), "cnt"], [("ps", b2)])
                    m0, m1 = 0, 8
                A("dve", lambda e, b2=b2, m0=m0, m1=m1, blk=blk: e.tensor_scalar(
                    out=qk[m0:m1, qt, blk * 256:(blk + 1) * 256], in0=ps[b2][m0:m1, 0:256], scalar1=2.5, scalar2=NEG,
                    op0=ALU.is_ge, op1=ALU.mult), r=[("ps", b2)], w=[("qk", qt, blk // 2)])

    grp_i = [0]
    e_i = [0]
    lt_i = [0]
    bias_n = [0]

    def attention(l, subs, finalize, fin_act, fin_c=None, sub_outer=False):
        tiles = []
        if sub_outer:
            order = [(j, si) for si in range(len(subs)) for j in range(4)]
        else:
            order = [(j, si) for j in range(4) for si in range(len(subs))]
        for (j, si) in order:
            n = 4 * j + 4
            for i in range(n):
                tiles.append((j, si, i, n))
        cur = {"hd": None, "slot": 0}
        state = {}
        LOOK = 3

        def c0_of(j, i):
            d = i - 4 * j
            return 128 * d if d > 0 else 0

        def emit_S(t):
            j, si, i, n = tiles[t]
            sdef = subs[si]
            sbank = t % 3
            qt, q0, q1 = sdef["q"]
            kt, k0, k1 = sdef["k"]
            c0 = c0_of(j, i)
            mm(ps[sbank][:, c0:512], qk[k0:k1, kt, i * 128:(i + 1) * 128], qk[q0:q1, qt, j * 512 + c0:(j + 1) * 512], True, True,
               [("qk", kt, i // 4), ("qk", qt, j)], [("ps", sbank)])

        def emit_rest(t):
            j, si, i, n = tiles[t]
            sdef = subs[si]
            sbank = t % 3
            if i == 0:
                state[(j, si)] = grp_i[0] % 2
                grp_i[0] += 1
            gset = state[(j, si)]
            ob, zb = 3 + 2 * gset, 4 + 2 * gset
            hd = sdef["bias_head"]
            if cur["hd"] != hd:
                assert bias_seq[bias_n[0]] == hd, (bias_seq[bias_n[0]], hd)
                cur["hd"] = hd
                cur["slot"] = bias_n[0] % 2
                bias_n[0] += 1
                bias_prefetch(bias_n[0] + 1)
            bsl = cur["slot"]
            o = 512 * j - 128 * i
            c0 = c0_of(j, i)
            eb = e_i[0] % 3
            e_i[0] += 1
            if o >= 917:
                A("act", lambda e, sbank=sbank, eb=eb, bsl=bsl: e.activation(out=bf[:, eb, :], in_=ps[sbank][:], func=AF.Exp, scale=SCALE,
                                                                             bias=bias[:, bsl, TW - 1:TW]),
                  r=[("ps", sbank), ("bias", bsl)], w=[("bf", eb)])
            else:
                lt = lt_i[0] % 3
                lt_i[0] += 1
                tc0 = o + c0 + TOFF
                A("dve", lambda e, sbank=sbank, lt=lt, tc0=tc0, c0=c0, bsl=bsl: e.scalar_tensor_tensor(
                    out=ft[:, lt, c0:512], in0=ps[sbank][:, c0:512], scalar=SCALE, in1=bias[:, bsl, tc0:tc0 + 512 - c0], op0=ALU.mult, op1=ALU.add),
                  r=[("ps", sbank), ("bias", bsl)], w=[("f", lt)])
                A("act", lambda e, lt=lt, eb=eb, c0=c0: e.activation(out=bf[:, eb, c0:512], in_=ft[:, lt, c0:512], func=AF.Exp),
                  r=[("f", lt)], w=[("bf", eb)])
            tinfo[t] = (ob, zb, c0, eb)

        def emit_pe(t):
            j, si, i, n = tiles[t]
            sdef = subs[si]
            ob, zb, c0, eb = tinfo.pop(t)
            vc = sdef["vcol0"]
            mm(ps[ob][:, c0:512], V[:, 0, i, vc:vc + 128], bf[:, eb, c0:512], i == 0, i == n - 1, [("V", 0, i // 4), ("bf", eb)], [("ps", ob)])
            mm(ps[zb][:, c0:512], ones, bf[:, eb, c0:512], i == 0, i == n - 1, [("bf", eb), "consts"], [("ps", zb)])
            if i == n - 1:
                pending_a.append((t + 1, (j, si, ob, zb)))
                pending.append((t + 3, (j, si, ob, zb)))
                if fin_c is not None:
                    pending_c.append((t + 5, (j, si, ob, zb)))

        pending = []
        pending_a = []
        pending_c = []
        tinfo = {}
        T = len(tiles)
        for t in range(min(LOOK, T)):
            emit_S(t)
        for t in range(T):
            emit_rest(t)
            if t + LOOK < T:
                emit_S(t + LOOK)
            emit_pe(t)
            while pending_a and pending_a[0][0] <= t:
                fin_act(*pending_a.pop(0)[1])
            while pending and pending[0][0] <= t:
                finalize(*pending.pop(0)[1])
            while pending_c and pending_c[0][0] <= t:
                fin_c(*pending_c.pop(0)[1])
        while pending_a:
            fin_act(*pending_a.pop(0)[1])
        while pending:
            finalize(*pending.pop(0)[1])
        while pending_c:
            fin_c(*pending_c.pop(0)[1])

    def layer(l):
        norm_to_hT(l, GC_LN1)
        units = []
        for h in range(4):
            units.append(("d", h))
            units.append(("m", h))
        slots = {}
        slots[0] = load_w(win_d[l, 0])
        slots[1] = load_w(win_d[l, 4])
        for (kind, h) in units:
            if kind == "d":
                bias_seq.append(h)
            else:
                bias_seq.extend([4 + 2 * h, 4 + 2 * h + 1])
        bias_prefetch(bias_n[0] + 2)
        for ui, (kind, h) in enumerate(units):
            slot = slots[ui]
            if kind == "d":
                A("pool", lambda e: e.memset(qk[0:32, 3, :], 0.0), w=[("qk", 3, tt) for tt in range(4)])
                proj_qk_unit(l, slot, [(0, 0, [(0, 64, 0), (64, 128, 3)]), (1, 1, [(0, 128, 1)])])
                proj_v(slot)
            else:
                proj_qk_unit(l, slot, [(0, 2, [(0, 64, 2), (64, 128, 3)]), (1, 3, [(0, 64, 4), (64, 128, 5)])])
                proj_v(slot)
            nxt = ui + 2
            if nxt < 8:
                k2, h2 = units[nxt]
                slots[nxt] = load_w(win_d[l, h2 if k2 == "d" else 4 + h2])
            elif nxt == 8:
                slots[8] = load_w(wout_d[l, 0])
            elif nxt == 9:
                slots[9] = load_w(wout_d[l, 1])
                slots[10] = load_w(wout_d[l, 2])

            def recip(zb, dst, r0=0, r1=128):
                A("act", lambda e: e.activation(out=ft[r0:r1, dst, :], in_=ps[zb][r0:r1, :], func=AF.Ln), r=[("ps", zb)], w=[("f", dst)])
                A("act", lambda e: e.activation(out=ft[r0:r1, dst, :], in_=ft[r0:r1, dst, :], func=AF.Exp, scale=-1.0), r=[("f", dst)], w=[("f", dst)])

            if kind == "d":
                subs = [dict(q=(0 if m == 0 else 3, 0, 128), k=(1, 0, 128), vcol0=0, bias_head=h) for m in range(2)]

                def fin_a(j, si, ob, zb):
                    recip(zb, 3)

                def fin(j, si, ob, zb, h=h):
                    cs = slice(j * 512, (j + 1) * 512)
                    if si == 0:
                        A("dve", lambda e: e.tensor_tensor(out=ft[:, 4, :], in0=ps[ob][:], in1=ft[:, 3, :], op=ALU.mult),
                          r=[("ps", ob), ("f", 3)], w=[("f", 4)])
                    else:
                        A("dve", lambda e: e.tensor_tensor(out=ft[:, 3, :], in0=ps[ob][:], in1=ft[:, 3, :], op=ALU.mult),
                          r=[("ps", ob), ("f", 3)], w=[("f", 3)])
                        A("dve", lambda e: e.scalar_tensor_tensor(out=ft[:, 4, :], in0=ft[:, 3, :], scalar=small[:, NLAM + l:NLAM + l + 1],
                                                                  in1=ft[:, 4, :], op0=ALU.mult, op1=ALU.add),
                          r=[("f", 3), ("f", 4), "small"], w=[("f", 4)])
                        A("act", lambda e: e.activation(out=bf[:, 3, :], in_=ft[:, 4, :], func=AF.Square), r=[("f", 4)], w=[("bf", 3)])
                        b2 = 7
                        mm(ps[b2][:], ones, bf[:, 3, :], True, True, [("bf", 3), "consts"], [("ps", b2)])

                def fin_cc(j, si, ob, zb, h=h):
                    cs = slice(j * 512, (j + 1) * 512)
                    b2 = 7
                    if si == 1:
                        A("act", lambda e: e.activation(out=ft[:, 5, :], in_=ps[b2][:], func=AF.Ln, scale=1.0 / 128, bias=eps_col),
                          r=[("ps", b2), "small"], w=[("f", 5)])
                        A("act", lambda e: e.activation(out=ft[:, 5, :], in_=ft[:, 5, :], func=AF.Exp, scale=-0.5), r=[("f", 5)], w=[("f", 5)])
                        A("dve", lambda e: e.scalar_tensor_tensor(out=yT[:, h, cs], in0=ft[:, 4, :], scalar=small[:, GSUB + l:GSUB + l + 1],
                                                                  in1=ft[:, 5, :], op0=ALU.mult, op1=ALU.mult),
                          r=[("f", 4), ("f", 5), "small"], w=[("yT", h)])
                attention(l, subs, fin, fin_a, fin_cc)
            else:
                gating(0, 2, 4)
                gating(1, 3, 5)
                subs = [dict(q=(2 + r, 0, 128), k=(4 + r, 0, 128), vcol0=0, bias_head=4 + 2 * h + r) for r in range(2)]

                def fin_a(j, si, ob, zb):
                    recip(zb, 3, si * 64, si * 64 + 64)

                def fin(j, si, ob, zb, h=h):
                    cs = slice(j * 512, (j + 1) * 512)
                    r0 = si * 64
                    A("dve", lambda e: e.tensor_tensor(out=yT[r0:r0 + 64, 4 + h, cs], in0=ps[ob][r0:r0 + 64, :], in1=ft[r0:r0 + 64, 3, :],
                                                       op=ALU.mult), r=[("ps", ob), ("f", 3)], w=[("yT", 4 + h)])
                attention(l, subs, fin, fin_a, sub_outer=True)
        ALLQK = [("qk", t, tt) for t in range(6) for tt in range(4)]
        ALLU = [("U", br, tt) for br in range(2) for tt in range(4)]
        ALLWD = [("wd", k5) for k5 in range(5)]
        A("dve", lambda e: e.memset(bias[:, 0, 0:2], 0.0), w=[("bias", 0), ("bias", 1)] + ALLWD)
        wd = bias[:].rearrange("p a b -> p (a b)").bitcast(BF16)
        fslots = {}

        def load_wup(j):
            k = wslot_i[0] % 3
            wslot_i[0] += 1
            A("pool", lambda e: e.dma_start(out=wsl[:, k, 0:2048], in_=wffn_d[l, j][:, 0:2048]), w=[("w", k)], dma=True)
            fslots[j] = k

        def load_wdown(j):
            sl = j % 5
            A("pool", lambda e: e.dma_start(out=wd[:, sl * 1024:(sl + 1) * 1024], in_=wffn_d[l, j][:, 2048:3072]), w=[("wd", sl)], dma=True)

        load_wdown(0)
        for i in range(NT):
            for s3 in range(3):
                slot = slots[8 + s3]
                ns = 384 if s3 < 2 else 256
                d0 = s3 * 384
                b = misc()
                for fc in range(8):
                    mm(ps[b][:, 0:ns], yT[:, fc, i * 128:(i + 1) * 128], wsl[:, slot, fc * 384:fc * 384 + ns], fc == 0, fc == 7,
                       [("yT", fc), ("w", slot)], [("ps", b)])
                A("dve", lambda e, b=b, i=i, d0=d0, ns=ns: e.tensor_tensor(out=x[:, i, d0:d0 + ns], in0=ps[b][:, 0:ns], in1=x[:, i, d0:d0 + ns],
                                                                           op=ALU.add), r=[("ps", b), ("x", i)], w=[("x", i)])
        for s3 in range(3):
            load_wup(s3)
        norm_to_hT(l, GC_LN2)
        qkf = qk[:].rearrange("p a b -> p (a b)").bitcast(F32)
        U = [qkf[:, 0:2050], qkf[:, 2052:4102]]
        A("dve", lambda e: e.memset(qkf[:, 0:4104], 0.0), w=ALLQK + ALLU)
        GRP = 4
        ubanks = [0, 1, 2, 3, 6, 7]
        ub = [0]
        fset = [0]
        dbank = [0]
        pend = []

        def tail(j, tt, tg, tu):
            A("act", lambda e: e.activation(out=ft[:, tg, :], in_=ft[:, tg, :], func=AF.Silu), r=[("f", tg)], w=[("f", tg)])
            A("dve", lambda e: e.tensor_tensor(out=yT[:, j % 8, tt * 512:(tt + 1) * 512], in0=ft[:, tg, :], in1=ft[:, tu, :], op=ALU.mult),
              r=[("f", tg), ("f", tu)], w=[("yT", j % 8)])

        def down_proj(js):
            for i in range(NT):
                for dh in range(2):
                    b = 4 + dbank[0] % 2
                    dbank[0] += 1
                    for n_, jj in enumerate(js):
                        mm(ps[b][:], yT[:, jj % 8, i * 128:(i + 1) * 128], wd[:, (jj % 5) * 1024 + dh * 512:(jj % 5) * 1024 + (dh + 1) * 512],
                           n_ == 0, n_ == len(js) - 1, [("yT", jj % 8), ("wd", jj % 5)], [("ps", b)])
                    A("dve", lambda e, b=b, i=i, dh=dh: e.tensor_tensor(out=x[:, i, dh * 512:(dh + 1) * 512], in0=ps[b][:],
                                                                         in1=x[:, i, dh * 512:(dh + 1) * 512], op=ALU.add),
                      r=[("ps", b), ("x", i)], w=[("x", i)])

        for j in range(NCH):
            slot = fslots[j]
            for tt in range(4):
                sset = fset[0] % 3
                fset[0] += 1
                tl = [2 * sset, 2 * sset + 1]
                for br in range(2):
                    b = ubanks[ub[0] % 6]
                    ub[0] += 1
                    for kc in range(8):
                        mm(ps[b][:], wsl[:, slot, kc * 256 + br * 128:kc * 256 + (br + 1) * 128], hT[:, kc, tt * 512:(tt + 1) * 512],
                           kc == 0, kc == 7, [("w", slot), ("hT", kc, tt)], [("ps", b)])
                    A("act", lambda e, b=b, br=br, tt=tt: e.activation(out=U[br][:, 2 + tt * 512:2 + (tt + 1) * 512], in_=ps[b][:], func=AF.Copy),
                      r=[("ps", b)], w=[("U", br, tt)])
                    tmp = tl[br]
                    cw0, cw1, cw2 = [gc(l, GC_CW, k * 44 + br * NCH + j) for k in range(3)]
                    cbb = gc(l, GC_CB, br * NCH + j)
                    u1 = U[br][:, 1 + tt * 512:1 + (tt + 1) * 512]
                    u0 = U[br][:, tt * 512:(tt + 1) * 512]
                    A("act", lambda e, b=b, tmp=tmp, cw2=cw2, cbb=cbb: e.activation(out=ft[:, tmp, :], in_=ps[b][:], func=AF.Identity,
                                                                                   scale=cw2, bias=cbb),
                      r=[("ps", b), "gcols"], w=[("f", tmp)])
                    rk = [("U", br, tt), ("U", br, max(tt - 1, 0)), ("f", tmp), "gcols"]
                    A("dve", lambda e, tmp=tmp, u1=u1, cw1=cw1: e.scalar_tensor_tensor(
                        out=ft[:, tmp, :], in0=u1, scalar=cw1, in1=ft[:, tmp, :], op0=ALU.mult, op1=ALU.add), r=rk, w=[("f", tmp)])
                    A("dve", lambda e, tmp=tmp, u0=u0, cw0=cw0: e.scalar_tensor_tensor(
                        out=ft[:, tmp, :], in0=u0, scalar=cw0, in1=ft[:, tmp, :], op0=ALU.mult, op1=ALU.add), r=rk, w=[("f", tmp)])
                if pend:
                    tail(*pend.pop(0))
                pend.append((j, tt, tl[0], tl[1]))
                if tt == 0 and j >= 1 and (j % GRP) == 0:
                    down_proj(list(range(j - GRP, j)))
            if j + 3 < NCH:
                load_wup(j + 3)
            if j + 1 < NCH:
                load_wdown(j + 1)
        while pend:
            tail(*pend.pop(0))
        down_proj(list(range((NCH // GRP) * GRP, NCH)))
        A("dve", lambda e: e.memset(bias[:, 0, 0:2], 0.0), w=[("bias", 0), ("bias", 1)] + ALLWD)
        if l + 1 < first_layer + n_layers:
            A("pool", lambda e: e.memset(qk[:].rearrange("p a b -> p (a b)"), 0.0), w=ALLQK + ALLU)
            A("pool", lambda e: e.dma_start(out=qk[64:72, 4, :], in_=kind_d), w=[("qk", 4, tt) for tt in range(4)], dma=True)
            A("pool", lambda e: e.dma_start(out=qk[0:8, 5, :], in_=kind_d), w=[("qk", 5, tt) for tt in range(4)], dma=True)

    for l in range(first_layer, first_layer + n_layers):
        layer(l)

    for i in range(NT):
        A("sp", lambda e, i=i: e.dma_start(out=out_d[i * 128:(i + 1) * 128, :], in_=x[:, i, :]), r=[("x", i)], w=[("out", i)], dma=True)
    A("sp", None, r=[("out", i) for i in range(NT)])
    P.ops[-1].fn = None

    P.emit(nc, st)
    st.close()
    return nc


def rel_bucket_np(dist):
    n = np.maximum(dist, 0)
    nf = np.maximum(n, 16).astype(np.float32)
    large = 16 + (np.log(nf / np.float32(16)) / np.float32(math.log(1024 / 16)) * np.float32(16)).astype(np.int32)
    large = np.minimum(large, 31)
    return np.where(n < 16, n, large)


def host_layout(inp):
    f = lambda a: np.ascontiguousarray(np.asarray(a, dtype=np.float32))
    w_in, w_out, w_up, w_down = f(inp["w_in"]), f(inp["w_out"]), f(inp["w_up"]), f(inp["w_down"])
    L = DEPTH
    win = np.zeros((L, 8, 128, 8, 384), np.float32)
    for u in range(8):
        if u < 4:
            cols = np.concatenate([np.arange(128) + 128 * u, 512 + np.arange(128) + 128 * u, 1024 + np.arange(128) + 128 * u])
        else:
            p = u - 4
            cols = np.concatenate([1536 + np.arange(128) + 128 * p, 2048 + np.arange(128) + 128 * p, 2560 + np.arange(128) + 128 * p])
        win[:, u] = w_in[:, :, cols].reshape(L, 8, 128, 384).transpose(0, 2, 1, 3)
    wout = np.zeros((L, 3, 128, 8, 384), np.float32)
    for s3 in range(3):
        ns = 384 if s3 < 2 else 256
        wout[:, s3, :, :, :ns] = w_out[:, :, s3 * 384:s3 * 384 + ns].reshape(L, 8, 128, ns).transpose(0, 2, 1, 3)
    wffn = np.zeros((L, NCH, 128, 3072), np.float32)
    for j in range(NCH):
        cols = np.concatenate([np.arange(128) + 128 * j, DFF + np.arange(128) + 128 * j])
        wffn[:, j, :, :2048] = w_up[:, :, cols].reshape(L, 8, 128, 256).transpose(0, 2, 1, 3).reshape(L, 128, 2048)
        wffn[:, j, :, 2048:] = w_down[:, j * 128:(j + 1) * 128, :]
    rb = f(inp["rel_bias"])
    pp = np.arange(128)[:, None]
    cc = np.arange(TW)[None, :]
    dist = cc - pp - TOFF
    bidx = rel_bucket_np(dist)
    biasT = np.empty((12, 128, TW), np.float32)
    for h in range(12):
        biasT[h] = np.where(dist >= 0, rb[bidx, h], np.float32(NEG))
    gcols = np.zeros((128, GC), np.float32)
    ln1, ln2 = f(inp["ln_attn_g"]), f(inp["ln_ffn_g"])
    qkg, sub = f(inp["qk_norm_g"]), f(inp["diff_subln_g"])
    cw, cb = f(inp["conv_w"]), f(inp["conv_b"])
    for l in range(L):
        o = l * GC_L
        gcols[:, o + GC_LN1:o + GC_LN1 + 8] = ln1[l].reshape(8, 128).T
        gcols[:, o + GC_LN2:o + GC_LN2 + 8] = ln2[l].reshape(8, 128).T
        for k in range(4):
            gcols[:, o + GC_QK + k] = np.tile(qkg[l, k], 2)
        gcols[:, o + GC_SUB] = sub[l]
        for k in range(3):
            gcols[:, o + GC_CW + k * 44:o + GC_CW + (k + 1) * 44] = cw[l, k].reshape(44, 128).T
        gcols[:, o + GC_CB:o + GC_CB + 44] = cb[l].reshape(44, 128).T
    lamb = np.broadcast_to(f(inp["diff_lambda"]).reshape(1, L * 256), (128, L * 256)).copy()
    consts = np.zeros((128, 384), np.float32)
    consts[:, 0:128] = np.eye(128, dtype=np.float32)
    consts[:, 128:256] = 1.0
    consts[0:64, 256:320] = 1.0
    consts[64:128, 320:384] = 1.0
    cnte = np.zeros((128, 8, 72), np.float32)
    cnto = np.zeros((128, 8, 8), np.float32)
    for b in range(8):
        for n in range(8):
            c = 0.0 if n < b else (-10.0 if n == b else 10.0)
            cnte[64, b, 64 + n] = c
            cnto[64, b, n] = c
            if n < b:
                for n2 in range(b):
                    cnte[n * 8 + n2, b, 64 + n] = 1.0
                    cnto[n * 8 + n2, b, n] = 1.0
    kind = np.zeros((8, S), np.float32)
    for n in range(8):
        kind[n, n * 256:(n + 1) * 256] = 1.0
    shared = dict(win=win.reshape(L, 8, 128, 3072), wout=wout.reshape(L, 3, 128, 3072), wffn=wffn, biasT=biasT, gcols=gcols,
                  lamb=lamb, consts=consts, cnte=cnte.reshape(128, 576), cnto=cnto.reshape(128, 64), kind=kind)
    return shared


_NC_CACHE = {}


def kernel(**inputs):
    x = np.ascontiguousarray(np.asarray(inputs["x"], dtype=np.float32))
    shared = host_layout(inputs)
    if "nc" not in _NC_CACHE:
        _NC_CACHE["nc"] = build(DEPTH, 0)
    nc = _NC_CACHE["nc"]
    in_maps = [dict(shared, x=x[b]) for b in range(8)]
    res = run_bass_kernel_spmd(nc, in_maps, core_ids=list(range(8)))
    return np.stack([np.asarray(r["out"], dtype=np.float32) for r in res.results], axis=0)
```

```python
import math
from contextlib import ExitStack
import numpy as np
import concourse.bass as bass
import concourse.mybir as mybir
from concourse.bass_utils import run_bass_kernel_spmd

F32 = mybir.dt.float32
BF16 = mybir.dt.bfloat16
ALU = mybir.AluOpType
AF = mybir.ActivationFunctionType
AX = mybir.AxisListType

S = 2048
D = 1024
NT = 16
DEPTH = 4
DFF = 2816
NCH = 22
EPS = 1e-6
NEG = -30000.0
TW = 1408
TOFF = 0
SCALE = 0.125

GC_LN1 = 0
GC_LN2 = 8
GC_QK = 16
GC_SUB = 20
GC_CW = 21
GC_CB = 21 + 132
GC_L = 21 + 132 + 44
GC = GC_L * DEPTH


class Op:
    __slots__ = ("eng", "fn", "deps", "sig", "pos", "dma", "sem", "val")

    def __init__(self, eng, fn, dma):
        self.eng = eng
        self.fn = fn
        self.dma = dma
        self.deps = ()
        self.sig = dma
        self.pos = 0
        self.sem = None
        self.val = 0


class Prog:
    ENGS = ("pe", "act", "dve", "pool", "sp")

    def __init__(self):
        self.ops = []
        self.lastw = {}
        self.readers = {}
        self.cnt = {e: 0 for e in self.ENGS}
        self.last_on = {e: None for e in self.ENGS}
        self.dma_since_barrier = []

    def add(self, eng, fn, r=(), w=(), dma=False):
        op = Op(eng, fn, dma)
        idx = len(self.ops)
        op.pos = self.cnt[eng]
        self.cnt[eng] += 1
        deps = set()
        for k in r:
            lw = self.lastw.get(k)
            if lw is not None:
                deps.add(lw)
        for k in w:
            lw = self.lastw.get(k)
            if lw is not None:
                deps.add(lw)
            rd = self.readers.get(k)
            if rd:
                deps.update(rd[0].values())
                deps.update(rd[1])
        for k in r:
            rd = self.readers.setdefault(k, ({}, []))
            if dma:
                rd[1].append(idx)
            else:
                rd[0][eng] = idx
        for k in w:
            self.lastw[k] = idx
            self.readers[k] = ({}, [])
        deps.discard(idx)
        op.deps = self._filter(op, deps)
        self.ops.append(op)
        self.last_on[eng] = idx
        if dma:
            self.dma_since_barrier.append(idx)
        return idx

    def _filter(self, op, deps):
        out = []
        for d in deps:
            p = self.ops[d]
            if (not p.dma) and p.eng == op.eng and not op.dma:
                if op.eng == "pe":
                    continue
                if op.pos - p.pos > 3:
                    continue
            p.sig = True
            out.append(d)
        return tuple(out)

    def barrier(self):
        lasts = [v for v in self.last_on.values() if v is not None]
        dmas = list(self.dma_since_barrier)
        self.dma_since_barrier = []
        for e in self.ENGS:
            op = Op(e, None, False)
            op.pos = self.cnt[e]
            self.cnt[e] += 1
            deps = set(lasts) | set(dmas)
            op.deps = self._filter(op, deps)
            self.ops.append(op)
            self.last_on[e] = len(self.ops) - 1

    def emit(self, nc, stack):
        eng_sem = {e: stack.enter_context(nc.semaphore("sem_" + e)) for e in ("pe", "act", "dve", "pool")}
        NDS = 20
        dsems = {q: [stack.enter_context(nc.semaphore("dq_%s_%d" % (q, i))) for i in range(NDS)]
                 for q in ("sp", "pool", "act")}
        duse = {q: [0] * NDS for q in dsems}
        dnext = {q: 0 for q in dsems}
        ccount = {e: 0 for e in eng_sem}
        prev_wait = {}
        for i, op in enumerate(self.ops):
            if op.fn is None:
                continue
            if op.dma:
                q = op.eng
                k = dnext[q]
                dnext[q] = (k + 1) % NDS
                op.sem = dsems[q][k]
                prev_wait[i] = (op.sem, duse[q][k] * 16)
                duse[q][k] += 1
                op.val = duse[q][k] * 16
            elif op.sig:
                ccount[op.eng] += 1
                op.sem = eng_sem[op.eng]
                op.val = ccount[op.eng]
        per_eng = {e: [] for e in self.ENGS}
        for i, op in enumerate(self.ops):
            per_eng[op.eng].append(i)
        ops = self.ops

        def run(e, handle):
            seen = {}
            for i in per_eng[e]:
                op = ops[i]
                waits = {}
                for d in op.deps:
                    p = ops[d]
                    if p.sem is None:
                        continue
                    key = p.sem
                    if waits.get(key, (None, 0))[1] < p.val:
                        waits[key] = (p.sem, p.val)
                if i in prev_wait:
                    s_, v_ = prev_wait[i]
                    if v_ > 0 and waits.get(s_, (None, 0))[1] < v_:
                        waits[s_] = (s_, v_)
                for key, (s_, v_) in waits.items():
                    if seen.get(key, 0) >= v_:
                        continue
                    seen[key] = v_
                    handle.wait_ge(s_, v_)
                if op.fn is None:
                    continue
                inst = op.fn(handle)
                if op.dma:
                    inst.then_inc(op.sem, 16)
                elif op.sig:
                    inst.then_inc(op.sem, 1)

        with nc.Block() as block:
            @block.tensor
            def _(h):
                run("pe", h)

            @block.scalar
            def _(h):
                run("act", h)

            @block.vector
            def _(h):
                run("dve", h)

            @block.gpsimd
            def _(h):
                run("pool", h)

            @block.sync
            def _(h):
                run("sp", h)


def bcast_ap(ap, pattern):
    return bass.AP(tensor=ap.tensor, offset=ap.offset, ap=[list(ap.ap[0])] + [list(p) for p in pattern])


def build(n_layers=DEPTH, first_layer=0):
    nc = bass.Bass("TRN2", target_bir_lowering=False)
    dt = nc.dram_tensor
    x_d = dt("x", [S, D], F32, kind="ExternalInput").ap()
    win_d = dt("win", [DEPTH, 8, 128, 3072], F32, kind="ExternalInput").ap()
    wout_d = dt("wout", [DEPTH, 3, 128, 3072], F32, kind="ExternalInput").ap()
    wffn_d = dt("wffn", [DEPTH, NCH, 128, 3072], F32, kind="ExternalInput").ap()
    bias_d = dt("biasT", [12, 128, TW], F32, kind="ExternalInput").ap()
    gcols_d = dt("gcols", [128, GC], F32, kind="ExternalInput").ap()
    lam_d = dt("lamb", [128, DEPTH * 256], F32, kind="ExternalInput").ap()
    consts_d = dt("consts", [128, 384], F32, kind="ExternalInput").ap()
    cnte_d = dt("cnte", [128, 8 * 72], F32, kind="ExternalInput").ap()
    cnto_d = dt("cnto", [128, 8 * 8], F32, kind="ExternalInput").ap()
    kind_d = dt("kind", [8, S], F32, kind="ExternalInput").ap()
    out_d = dt("out", [S, D], F32, kind="ExternalOutput").ap()

    st = ExitStack()
    sb = lambda name, shape, dtype: st.enter_context(nc.sbuf_tensor(name, shape, dtype))
    x = sb("x_sb", [128, NT, D], F32)
    hT = sb("hT", [128, 8, S], BF16)
    yT = sb("yT", [128, 8, S], BF16)
    qk = sb("qk", [128, 6, S], BF16)
    V = sb("V", [128, 1, NT, 128], BF16)
    bias = sb("bias", [128, 2, TW], F32)
    ft = sb("ft", [128, 6, 512], F32)
    bf = sb("bf", [128, 4, 512], BF16)
    wsl = sb("wsl", [128, 3, 3072], BF16)
    gcols = sb("gcols_sb", [128, GC], F32)
    consts = sb("consts_sb", [128, 384], BF16)
    cnte = sb("cnte_sb", [128, 8 * 72], BF16)
    cnto = sb("cnto_sb", [128, 8 * 8], BF16)
    ind = sb("ind", [128, 2, 256], BF16)
    kdiff = sb("kdiff", [128, 2, 64], BF16)
    km = sb("km", [128, 8], F32)
    small = sb("small", [128, 64], F32)
    ps = [st.enter_context(nc.psum_tensor("ps%d" % i, [128, 512], F32)) for i in range(8)]

    ident = consts[:, 0:128]
    ones = consts[:, 128:256]
    bones = consts[:, 256:384]
    SS, RSTD, EPSC, LAM, NLAM, GSUB, TMPC = 0, 16, 32, 33, 37, 41, 45
    eps_col = small[:, EPSC:EPSC + 1]

    P = Prog()

    def A(eng, fn, r=(), w=(), dma=False):
        return P.add(eng, fn, r, w, dma)

    def mm(out, lhsT, rhs, start, stop, r, w):
        A("pe", lambda e: e.matmul(out, lhsT, rhs, start=start, stop=stop), r, w)

    misc_i = [0]

    def misc():
        misc_i[0] = (misc_i[0] + 1) % 8
        return misc_i[0]

    ft_i = [0]

    def ftile():
        ft_i[0] = (ft_i[0] + 1) % 6
        return ft_i[0]

    bf_i = [0]

    def bftile():
        bf_i[0] = (bf_i[0] + 1) % 4
        return bf_i[0]

    bias_seq = []
    bias_loaded = [0]

    def bias_prefetch(upto):
        while bias_loaded[0] < min(upto, len(bias_seq)):
            n = bias_loaded[0]
            hd = bias_seq[n]
            A("sp", lambda e, hd=hd, n=n: e.dma_start(out=bias[:, n % 2, :], in_=bias_d[hd]), w=[("bias", n % 2)], dma=True)
            bias_loaded[0] += 1

    wslot_i = [0]

    def load_w(src):
        k = wslot_i[0] % 3
        wslot_i[0] += 1
        A("pool", lambda e: e.dma_start(out=wsl[:, k, :], in_=src), w=[("w", k)], dma=True)
        return k

    def gc(l, base, j=0):
        c = l * GC_L + base + j
        return gcols[:, c:c + 1]

    A("sp", lambda e: e.dma_start(out=gcols[:], in_=gcols_d), w=["gcols"], dma=True)
    A("sp", lambda e: e.dma_start(out=ft[:, 0:2, :].rearrange("p a b -> p (a b)"), in_=lam_d), w=[("f", 0), ("f", 1)], dma=True)
    A("pool", lambda e: e.dma_start(out=consts[:], in_=consts_d), w=["consts"], dma=True)
    A("pool", lambda e: e.dma_start(out=cnte[:], in_=cnte_d), w=["cnt"], dma=True)
    A("pool", lambda e: e.dma_start(out=cnto[:], in_=cnto_d), w=["cnt"], dma=True)
    for i in range(NT):
        A("sp", lambda e, i=i: e.dma_start(out=x[:, i, :], in_=x_d[i * 128:(i + 1) * 128, :]), w=[("x", i)], dma=True)
    for t in (0, 2, 3, 4, 5):
        A("pool", lambda e, t=t: e.memset(qk[:, t, :], 0.0), w=[("qk", t, tt) for tt in range(4)])
    A("pool", lambda e: e.dma_start(out=qk[64:72, 4, :], in_=kind_d), w=[("qk", 4, tt) for tt in range(4)], dma=True)
    A("pool", lambda e: e.dma_start(out=qk[0:8, 5, :], in_=kind_d), w=[("qk", 5, tt) for tt in range(4)], dma=True)
    A("dve", lambda e: e.memset(small[:], 0.0), w=["small"])
    A("dve", lambda e: e.memset(eps_col, EPS), w=["small"])
    A("dve", lambda e: e.memset(ind[:], 0.0), w=[("ind", 0), ("ind", 1)])
    A("dve", lambda e: e.memset(ind[64:65, :, :], 1.0), w=[("ind", 0), ("ind", 1)])
    A("dve", lambda e: e.memset(kdiff[:], 0.0), w=["kdiff"])
    lamt = ft[:, 0:2, :].rearrange("p a b -> p (a b)")
    for l in range(DEPTH):
        lam_init = 0.8 - 0.6 * math.exp(-0.3 * l)
        b0 = l * 256
        for t in range(2):
            A("dve", lambda e, b0=b0, t=t: e.tensor_tensor(out=ft[:, 2, t * 64:(t + 1) * 64], in0=lamt[:, b0 + t * 128:b0 + t * 128 + 64],
                                                          in1=lamt[:, b0 + t * 128 + 64:b0 + t * 128 + 128], op=ALU.mult),
              r=[("f", 0), ("f", 1)], w=[("f", 2)])
            A("dve", lambda e, t=t: e.reduce_sum(out=small[:, TMPC + t:TMPC + t + 1], in_=ft[:, 2, t * 64:(t + 1) * 64], axis=AX.X),
              r=[("f", 2)], w=["small"])
        A("act", lambda e: e.activation(out=small[:, TMPC + 2:TMPC + 4], in_=small[:, TMPC:TMPC + 2], func=AF.Exp), r=["small"], w=["small"])
        A("dve", lambda e, l=l: e.tensor_tensor(out=small[:, LAM + l:LAM + l + 1], in0=small[:, TMPC + 2:TMPC + 3],
                                                in1=small[:, TMPC + 3:TMPC + 4], op=ALU.subtract), r=["small"], w=["small"])
        A("dve", lambda e, l=l, li=lam_init: e.tensor_scalar(out=small[:, NLAM + l:NLAM + l + 1], in0=small[:, LAM + l:LAM + l + 1],
                                                             scalar1=li, scalar2=-1.0, op0=ALU.add, op1=ALU.mult), r=["small"], w=["small"])
        A("dve", lambda e, l=l, li=lam_init: e.tensor_scalar(out=small[:, GSUB + l:GSUB + l + 1], in0=gc(l, GC_SUB),
                                                             scalar1=1.0 - li, scalar2=0.0, op0=ALU.mult, op1=ALU.add),
          r=["small", "gcols"], w=["small"])

    def norm_to_hT(l, gbase):
        junk = ft[:, 4:6, :].rearrange("p a b -> p (a b)")
        A("dve", lambda e: e.memset(small[:, SS:SS + 16], 0.0), w=[("ss", i) for i in range(NT)])
        hb = yT[:].rearrange("p c s -> p (c s)")
        for g4 in range(4):
            for i in range(g4 * 4, g4 * 4 + 4):
                A("act", lambda e, i=i: e.activation(out=junk, in_=x[:, i, :], func=AF.Square, accum_out=small[:, SS + i:SS + i + 1]),
                  r=[("x", i)], w=[("f", 4), ("f", 5), ("ss", i)])
            c0_, c1_ = RSTD + g4 * 4, RSTD + g4 * 4 + 4
            A("act", lambda e, g4=g4, c0_=c0_, c1_=c1_: e.activation(out=small[:, c0_:c1_], in_=small[:, SS + g4 * 4:SS + g4 * 4 + 4],
                                                                 func=AF.Ln, scale=1.0 / D, bias=eps_col),
              r=[("ss", i) for i in range(g4 * 4, g4 * 4 + 4)] + ["small"], w=[("rstd", g4)])
            A("act", lambda e, c0_=c0_, c1_=c1_: e.activation(out=small[:, c0_:c1_], in_=small[:, c0_:c1_], func=AF.Exp, scale=-0.5),
              r=[("rstd", g4)], w=[("rstd", g4)])
            for i in range(g4 * 4, g4 * 4 + 4):
                A("dve", lambda e, i=i: e.tensor_scalar(out=hb[:, i * D:(i + 1) * D], in0=x[:, i, :], scalar1=small[:, RSTD + i:RSTD + i + 1],
                                                        scalar2=0.0, op0=ALU.mult, op1=ALU.add),
                  r=[("x", i), ("rstd", g4)], w=[("yT", i // 2)])
        for half in range(2):
            for c in range(8):
                b = misc()
                pst = ps[b][:].bitcast(BF16)
                for ii in range(8):
                    i = half * 8 + ii
                    A("pe", lambda e, i=i, ii=ii, c=c, pst=pst: e.transpose(out=pst[:, ii * 128:(ii + 1) * 128],
                                                                            in_=hb[:, i * D + c * 128:i * D + (c + 1) * 128], identity=ident),
                      r=[("yT", i // 2), "consts"], w=[("ps", b)])
                A("act", lambda e, c=c, half=half, pst=pst: e.activation(out=hT[:, c, half * 1024:(half + 1) * 1024], in_=pst[:, 0:1024],
                                                                         func=AF.Identity, scale=gc(l, gbase, c)),
                  r=[("ps", b), "gcols"], w=[("hT", c, half * 2), ("hT", c, half * 2 + 1)])

    def proj_qk_unit(l, slot, specs):
        chunks = [(g, gq_idx, dests, tt) for (g, gq_idx, dests) in specs for tt in range(4)]
        st_ = {}
        misc_i[0] = 6

        def emit_proj(ci):
            g, gq_idx, dests, tt = chunks[ci]
            b = misc()
            for kc in range(8):
                mm(ps[b][:], wsl[:, slot, kc * 384 + g * 128:kc * 384 + (g + 1) * 128], hT[:, kc, tt * 512:(tt + 1) * 512],
                   kc == 0, kc == 7, [("w", slot), ("hT", kc, tt)], [("ps", b)])
            sq = bftile()
            A("act", lambda e, b=b, sq=sq: e.activation(out=bf[:, sq, :], in_=ps[b][:], func=AF.Square), r=[("ps", b)], w=[("bf", sq)])
            st_[ci] = (b, sq)

        def emit_chain(ci):
            g, gq_idx, dests, tt = chunks[ci]
            b, sq = st_[ci]
            b2 = misc()
            mm(ps[b2][:], bones, bf[:, sq, :], True, True, [("bf", sq), "consts"], [("ps", b2)])
            rs = ftile()
            A("act", lambda e, b2=b2, rs=rs: e.activation(out=ft[:, rs, :], in_=ps[b2][:], func=AF.Ln, scale=1.0 / 64, bias=eps_col),
              r=[("ps", b2), "small"], w=[("f", rs)])
            A("act", lambda e, rs=rs: e.activation(out=ft[:, rs, :], in_=ft[:, rs, :], func=AF.Exp, scale=-0.5), r=[("f", rs)], w=[("f", rs)])
            for (r0, r1, t) in dests:
                A("dve", lambda e, b=b, r0=r0, r1=r1, t=t, tt=tt, rs=rs, gq_idx=gq_idx: e.scalar_tensor_tensor(
                    out=qk[r0:r1, t, tt * 512:(tt + 1) * 512], in0=ps[b][r0:r1, :], scalar=gc(l, GC_QK, gq_idx)[r0:r1, :],
                    in1=ft[r0:r1, rs, :], op0=ALU.mult, op1=ALU.mult),
                  r=[("ps", b), ("f", rs), "gcols"], w=[("qk", t, tt)])

        n = len(chunks)
        emit_proj(0)
        for ci in range(n):
            if ci + 1 < n:
                emit_proj(ci + 1)
            emit_chain(ci)

    def proj_v(slot, vb=0):
        for i4 in range(4):
            b = misc()
            for ii in range(4):
                i = i4 * 4 + ii
                for kc in range(8):
                    mm(ps[b][:, ii * 128:(ii + 1) * 128], hT[:, kc, i * 128:(i + 1) * 128], wsl[:, slot, kc * 384 + 256:kc * 384 + 384],
                       kc == 0, kc == 7, [("w", slot), ("hT", kc, i // 4)], [("ps", b)])
            A("act", lambda e, b=b, i4=i4, vb=vb: e.activation(out=V[:, vb, i4 * 4:(i4 + 1) * 4, :].rearrange("p a b -> p (a b)"),
                                                               in_=ps[b][:], func=AF.Copy),
              r=[("ps", b)], w=[("V", vb, i4)])

    def gating(r, qt, kt):
        r0 = r * 64
        kview = qk[r0:r0 + 64, kt, :].rearrange("p (n j) -> p n j", j=256)
        A("dve", lambda e: e.reduce_sum(out=km[r0:r0 + 64, :], in_=kview, axis=AX.X),
          r=[("qk", kt, tt) for tt in range(4)], w=["km"])
        kmv = km[r0:r0 + 64, :]
        A("dve", lambda e: e.tensor_tensor(out=kdiff[r0:r0 + 64, r, :].rearrange("p (a b) -> p a b", b=8),
                                           in0=bcast_ap(kmv, [[0, 8], [1, 8]]), in1=bcast_ap(kmv, [[1, 8], [0, 8]]), op=ALU.subtract),
          r=["km"], w=["kdiff"])
        for b0 in range(0, 8, 2):
            stg = {}
            for blk in (b0, b0 + 1):
                b = misc()
                ib = blk % 2
                mm(ps[b][0:64, 0:256], kdiff[:, r, :], qk[:, qt, blk * 256:(blk + 1) * 256], True, True,
                   ["kdiff", ("qk", qt, blk // 2)], [("ps", b)])
                A("dve", lambda e, b=b, ib=ib: e.tensor_single_scalar(out=ind[0:64, ib, :], in_=ps[b][0:64, 0:256], scalar=0.0, op=ALU.is_gt),
                  r=[("ps", b)], w=[("ind", ib)])
            for blk in (b0, b0 + 1):
                ib = blk % 2
                b2 = misc()
                if r == 0:
                    mm(ps[b2][0:72, 0:256], cnte[:, blk * 72:(blk + 1) * 72], ind[:, ib, :], True, True, [("ind", ib), "cnt"], [("ps", b2)])
                    m0, m1 = 64, 72
                else:
                    mm(ps[b2][0:8, 0:256], cnto[:, blk * 8:(blk + 1) * 8], ind[:, ib, :], True, True, [("ind", ib), "cnt"], [("ps", b2)])
                    m0, m1 = 0, 8
                A("dve", lambda e, b2=b2, m0=m0, m1=m1, blk=blk: e.tensor_scalar(
                    out=qk[m0:m1, qt, blk * 256:(blk + 1) * 256], in0=ps[b2][m0:m1, 0:256], scalar1=2.5, scalar2=NEG,
                    op0=ALU.is_ge, op1=ALU.mult), r=[("ps", b2)], w=[("qk", qt, blk // 2)])

    grp_i = [0]
    e_i = [0]
    lt_i = [0]
    bias_n = [0]

    def attention(l, subs, finalize, fin_act, fin_c=None, sub_outer=False):
        tiles = []
        if sub_outer:
            order = [(j, si) for si in range(len(subs)) for j in range(4)]
        else:
            order = [(j, si) for j in range(4) for si in range(len(subs))]
        for (j, si) in order:
            n = 4 * j + 4
            for i in range(n):
                tiles.append((j, si, i, n))
        cur = {"hd": None, "slot": 0}
        state = {}
        LOOK = 3

        def c0_of(j, i):
            d = i - 4 * j
            return 128 * d if d > 0 else 0

        def emit_S(t):
            j, si, i, n = tiles[t]
            sdef = subs[si]
            sbank = t % 3
            qt, q0, q1 = sdef["q"]
            kt, k0, k1 = sdef["k"]
            c0 = c0_of(j, i)
            mm(ps[sbank][:, c0:512], qk[k0:k1, kt, i * 128:(i + 1) * 128], qk[q0:q1, qt, j * 512 + c0:(j + 1) * 512], True, True,
               [("qk", kt, i // 4), ("qk", qt, j)], [("ps", sbank)])

        def emit_rest(t):
            j, si, i, n = tiles[t]
            sdef = subs[si]
            sbank = t % 3
            if i == 0:
                state[(j, si)] = grp_i[0] % 2
                grp_i[0] += 1
            gset = state[(j, si)]
            ob, zb = 3 + 2 * gset, 4 + 2 * gset
            hd = sdef["bias_head"]
            if cur["hd"] != hd:
                assert bias_seq[bias_n[0]] == hd, (bias_seq[bias_n[0]], hd)
                cur["hd"] = hd
                cur["slot"] = bias_n[0] % 2
                bias_n[0] += 1
                bias_prefetch(bias_n[0] + 1)
            bsl = cur["slot"]
            o = 512 * j - 128 * i
            c0 = c0_of(j, i)
            eb = e_i[0] % 3
            e_i[0] += 1
            if o >= 917:
                A("act", lambda e, sbank=sbank, eb=eb, bsl=bsl: e.activation(out=bf[:, eb, :], in_=ps[sbank][:], func=AF.Exp, scale=SCALE,
                                                                             bias=bias[:, bsl, TW - 1:TW]),
                  r=[("ps", sbank), ("bias", bsl)], w=[("bf", eb)])
            else:
                lt = lt_i[0] % 3
                lt_i[0] += 1
                tc0 = o + c0 + TOFF
                A("dve", lambda e, sbank=sbank, lt=lt, tc0=tc0, c0=c0, bsl=bsl: e.scalar_tensor_tensor(
                    out=ft[:, lt, c0:512], in0=ps[sbank][:, c0:512], scalar=SCALE, in1=bias[:, bsl, tc0:tc0 + 512 - c0], op0=ALU.mult, op1=ALU.add),
                  r=[("ps", sbank), ("bias", bsl)], w=[("f", lt)])
                A("act", lambda e, lt=lt, eb=eb, c0=c0: e.activation(out=bf[:, eb, c0:512], in_=ft[:, lt, c0:512], func=AF.Exp),
                  r=[("f", lt)], w=[("bf", eb)])
            tinfo[t] = (ob, zb, c0, eb)

        def emit_pe(t):
            j, si, i, n = tiles[t]
            sdef = subs[si]
            ob, zb, c0, eb = tinfo.pop(t)
            vc = sdef["vcol0"]
            mm(ps[ob][:, c0:512], V[:, 0, i, vc:vc + 128], bf[:, eb, c0:512], i == 0, i == n - 1, [("V", 0, i // 4), ("bf", eb)], [("ps", ob)])
            mm(ps[zb][:, c0:512], ones, bf[:, eb, c0:512], i == 0, i == n - 1, [("bf", eb), "consts"], [("ps", zb)])
            if i == n - 1:
                pending_a.append((t + 1, (j, si, ob, zb)))
                pending.append((t + 3, (j, si, ob, zb)))
                if fin_c is not None:
                    pending_c.append((t + 5, (j, si, ob, zb)))

        pending = []
        pending_a = []
        pending_c = []
        tinfo = {}
        T = len(tiles)
        for t in range(min(LOOK, T)):
            emit_S(t)
        for t in range(T):
            emit_rest(t)
            if t + LOOK < T:
                emit_S(t + LOOK)
            emit_pe(t)
            while pending_a and pending_a[0][0] <= t:
                fin_act(*pending_a.pop(0)[1])
            while pending and pending[0][0] <= t:
                finalize(*pending.pop(0)[1])
            while pending_c and pending_c[0][0] <= t:
                fin_c(*pending_c.pop(0)[1])
        while pending_a:
            fin_act(*pending_a.pop(0)[1])
        while pending:
            finalize(*pending.pop(0)[1])
        while pending_c:
            fin_c(*pending_c.pop(0)[1])

    def layer(l):
        norm_to_hT(l, GC_LN1)
        units = []
        for h in range(4):
            units.append(("d", h))
            units.append(("m", h))
        slots = {}
        slots[0] = load_w(win_d[l, 0])
        slots[1] = load_w(win_d[l, 4])
        for (kind, h) in units:
            if kind == "d":
                bias_seq.append(h)
            else:
                bias_seq.extend([4 + 2 * h, 4 + 2 * h + 1])
        bias_prefetch(bias_n[0] + 2)
        for ui, (kind, h) in enumerate(units):
            slot = slots[ui]
            if kind == "d":
                A("pool", lambda e: e.memset(qk[0:32, 3, :], 0.0), w=[("qk", 3, tt) for tt in range(4)])
                proj_qk_unit(l, slot, [(0, 0, [(0, 64, 0), (64, 128, 3)]), (1, 1, [(0, 128, 1)])])
                proj_v(slot)
            else:
                proj_qk_unit(l, slot, [(0, 2, [(0, 64, 2), (64, 128, 3)]), (1, 3, [(0, 64, 4), (64, 128, 5)])])
                proj_v(slot)
            nxt = ui + 2
            if nxt < 8:
                k2, h2 = units[nxt]
                slots[nxt] = load_w(win_d[l, h2 if k2 == "d" else 4 + h2])
            elif nxt == 8:
                slots[8] = load_w(wout_d[l, 0])
            elif nxt == 9:
                slots[9] = load_w(wout_d[l, 1])
                slots[10] = load_w(wout_d[l, 2])

            def recip(zb, dst, r0=0, r1=128):
                A("act", lambda e: e.activation(out=ft[r0:r1, dst, :], in_=ps[zb][r0:r1, :], func=AF.Ln), r=[("ps", zb)], w=[("f", dst)])
                A("act", lambda e: e.activation(out=ft[r0:r1, dst, :], in_=ft[r0:r1, dst, :], func=AF.Exp, scale=-1.0), r=[("f", dst)], w=[("f", dst)])

            if kind == "d":
                subs = [dict(q=(0 if m == 0 else 3, 0, 128), k=(1, 0, 128), vcol0=0, bias_head=h) for m in range(2)]

                def fin_a(j, si, ob, zb):
                    recip(zb, 3)

                def fin(j, si, ob, zb, h=h):
                    cs = slice(j * 512, (j + 1) * 512)
                    if si == 0:
                        A("dve", lambda e: e.tensor_tensor(out=ft[:, 4, :], in0=ps[ob][:], in1=ft[:, 3, :], op=ALU.mult),
                          r=[("ps", ob), ("f", 3)], w=[("f", 4)])
                    else:
                        A("dve", lambda e: e.tensor_tensor(out=ft[:, 3, :], in0=ps[ob][:], in1=ft[:, 3, :], op=ALU.mult),
                          r=[("ps", ob), ("f", 3)], w=[("f", 3)])
                        A("dve", lambda e: e.scalar_tensor_tensor(out=ft[:, 4, :], in0=ft[:, 3, :], scalar=small[:, NLAM + l:NLAM + l + 1],
                                                                  in1=ft[:, 4, :], op0=ALU.mult, op1=ALU.add),
                          r=[("f", 3), ("f", 4), "small"], w=[("f", 4)])
                        A("act", lambda e: e.activation(out=bf[:, 3, :], in_=ft[:, 4, :], func=AF.Square), r=[("f", 4)], w=[("bf", 3)])
                        b2 = 7
                        mm(ps[b2][:], ones, bf[:, 3, :], True, True, [("bf", 3), "consts"], [("ps", b2)])

                def fin_cc(j, si, ob, zb, h=h):
                    cs = slice(j * 512, (j + 1) * 512)
                    b2 = 7
                    if si == 1:
                        A("act", lambda e: e.activation(out=ft[:, 5, :], in_=ps[b2][:], func=AF.Ln, scale=1.0 / 128, bias=eps_col),
                          r=[("ps", b2), "small"], w=[("f", 5)])
                        A("act", lambda e: e.activation(out=ft[:, 5, :], in_=ft[:, 5, :], func=AF.Exp, scale=-0.5), r=[("f", 5)], w=[("f", 5)])
                        A("dve", lambda e: e.scalar_tensor_tensor(out=yT[:, h, cs], in0=ft[:, 4, :], scalar=small[:, GSUB + l:GSUB + l + 1],
                                                                  in1=ft[:, 5, :], op0=ALU.mult, op1=ALU.mult),
                          r=[("f", 4), ("f", 5), "small"], w=[("yT", h)])
                attention(l, subs, fin, fin_a, fin_cc)
            else:
                gating(0, 2, 4)
                gating(1, 3, 5)
                subs = [dict(q=(2 + r, 0, 128), k=(4 + r, 0, 128), vcol0=0, bias_head=4 + 2 * h + r) for r in range(2)]

                def fin_a(j, si, ob, zb):
                    recip(zb, 3, si * 64, si * 64 + 64)

                def fin(j, si, ob, zb, h=h):
                    cs = slice(j * 512, (j + 1) * 512)
                    r0 = si * 64
                    A("dve", lambda e: e.tensor_tensor(out=yT[r0:r0 + 64, 4 + h, cs], in0=ps[ob][r0:r0 + 64, :], in1=ft[r0:r0 + 64, 3, :],
                                                       op=ALU.mult), r=[("ps", ob), ("f", 3)], w=[("yT", 4 + h)])
                attention(l, subs, fin, fin_a, sub_outer=True)
        ALLQK = [("qk", t, tt) for t in range(6) for tt in range(4)]
        ALLU = [("U", br, tt) for br in range(2) for tt in range(4)]
        ALLWD = [("wd", k5) for k5 in range(5)]
        A("dve", lambda e: e.memset(bias[:, 0, 0:2], 0.0), w=[("bias", 0), ("bias", 1)] + ALLWD)
        wd = bias[:].rearrange("p a b -> p (a b)").bitcast(BF16)
        fslots = {}

        def load_wup(j):
            k = wslot_i[0] % 3
            wslot_i[0] += 1
            A("pool", lambda e: e.dma_start(out=wsl[:, k, 0:2048], in_=wffn_d[l, j][:, 0:2048]), w=[("w", k)], dma=True)
            fslots[j] = k

        def load_wdown(j):
            sl = j % 5
            A("pool", lambda e: e.dma_start(out=wd[:, sl * 1024:(sl + 1) * 1024], in_=wffn_d[l, j][:, 2048:3072]), w=[("wd", sl)], dma=True)

        load_wdown(0)
        for i in range(NT):
            for s3 in range(3):
                slot = slots[8 + s3]
                ns = 384 if s3 < 2 else 256
                d0 = s3 * 384
                b = misc()
                for fc in range(8):
                    mm(ps[b][:, 0:ns], yT[:, fc, i * 128:(i + 1) * 128], wsl[:, slot, fc * 384:fc * 384 + ns], fc == 0, fc == 7,
                       [("yT", fc), ("w", slot)], [("ps", b)])
                A("dve", lambda e, b=b, i=i, d0=d0, ns=ns: e.tensor_tensor(out=x[:, i, d0:d0 + ns], in0=ps[b][:, 0:ns], in1=x[:, i, d0:d0 + ns],
                                                                           op=ALU.add), r=[("ps", b), ("x", i)], w=[("x", i)])
        for s3 in range(3):
            load_wup(s3)
        norm_to_hT(l, GC_LN2)
        qkf = qk[:].rearrange("p a b -> p (a b)").bitcast(F32)
        U = [qkf[:, 0:2050], qkf[:, 2052:4102]]
        A("dve", lambda e: e.memset(qkf[:, 0:4104], 0.0), w=ALLQK + ALLU)
        GRP = 4
        ubanks = [0, 1, 2, 3, 6, 7]
        ub = [0]
        fset = [0]
        dbank = [0]
        pend = []

        def tail(j, tt, tg, tu):
            A("act", lambda e: e.activation(out=ft[:, tg, :], in_=ft[:, tg, :], func=AF.Silu), r=[("f", tg)], w=[("f", tg)])
            A("dve", lambda e: e.tensor_tensor(out=yT[:, j % 8, tt * 512:(tt + 1) * 512], in0=ft[:, tg, :], in1=ft[:, tu, :], op=ALU.mult),
              r=[("f", tg), ("f", tu)], w=[("yT", j % 8)])

        def down_proj(js):
            for i in range(NT):
                for dh in range(2):
                    b = 4 + dbank[0] % 2
                    dbank[0] += 1
                    for n_, jj in enumerate(js):
                        mm(ps[b][:], yT[:, jj % 8, i * 128:(i + 1) * 128], wd[:, (jj % 5) * 1024 + dh * 512:(jj % 5) * 1024 + (dh + 1) * 512],
                           n_ == 0, n_ == len(js) - 1, [("yT", jj % 8), ("wd", jj % 5)], [("ps", b)])
                    A("dve", lambda e, b=b, i=i, dh=dh: e.tensor_tensor(out=x[:, i, dh * 512:(dh + 1) * 512], in0=ps[b][:],
                                                                         in1=x[:, i, dh * 512:(dh + 1) * 512], op=ALU.add),
                      r=[("ps", b), ("x", i)], w=[("x", i)])

        for j in range(NCH):
            slot = fslots[j]
            for tt in range(4):
                sset = fset[0] % 3
                fset[0] += 1
                tl = [2 * sset, 2 * sset + 1]
                for br in range(2):
                    b = ubanks[ub[0] % 6]
                    ub[0] += 1
                    for kc in range(8):
                        mm(ps[b][:], wsl[:, slot, kc * 256 + br * 128:kc * 256 + (br + 1) * 128], hT[:, kc, tt * 512:(tt + 1) * 512],
                           kc == 0, kc == 7, [("w", slot), ("hT", kc, tt)], [("ps", b)])
                    A("act", lambda e, b=b, br=br, tt=tt: e.activation(out=U[br][:, 2 + tt * 512:2 + (tt + 1) * 512], in_=ps[b][:], func=AF.Copy),
                      r=[("ps", b)], w=[("U", br, tt)])
                    tmp = tl[br]
                    cw0, cw1, cw2 = [gc(l, GC_CW, k * 44 + br * NCH + j) for k in range(3)]
                    cbb = gc(l, GC_CB, br * NCH + j)
                    u1 = U[br][:, 1 + tt * 512:1 + (tt + 1) * 512]
                    u0 = U[br][:, tt * 512:(tt + 1) * 512]
                    A("act", lambda e, b=b, tmp=tmp, cw2=cw2, cbb=cbb: e.activation(out=ft[:, tmp, :], in_=ps[b][:], func=AF.Identity,
                                                                                   scale=cw2, bias=cbb),
                      r=[("ps", b), "gcols"], w=[("f", tmp)])
                    rk = [("U", br, tt), ("U", br, max(tt - 1, 0)), ("f", tmp), "gcols"]
                    A("dve", lambda e, tmp=tmp, u1=u1, cw1=cw1: e.scalar_tensor_tensor(
                        out=ft[:, tmp, :], in0=u1, scalar=cw1, in1=ft[:, tmp, :], op0=ALU.mult, op1=ALU.add), r=rk, w=[("f", tmp)])
                    A("dve", lambda e, tmp=tmp, u0=u0, cw0=cw0: e.scalar_tensor_tensor(
                        out=ft[:, tmp, :], in0=u0, scalar=cw0, in1=ft[:, tmp, :], op0=ALU.mult, op1=ALU.add), r=rk, w=[("f", tmp)])
                if pend:
                    tail(*pend.pop(0))
                pend.append((j, tt, tl[0], tl[1]))
                if tt == 0 and j >= 1 and (j % GRP) == 0:
                    down_proj(list(range(j - GRP, j)))
            if j + 3 < NCH:
                load_wup(j + 3)
            if j + 1 < NCH:
                load_wdown(j + 1)
        while pend:
            tail(*pend.pop(0))
        down_proj(list(range((NCH // GRP) * GRP, NCH)))
        A("dve", lambda e: e.memset(bias[:, 0, 0:2], 0.0), w=[("bias", 0), ("bias", 1)] + ALLWD)
        if l + 1 < first_layer + n_layers:
            A("pool", lambda e: e.memset(qk[:].rearrange("p a b -> p (a b)"), 0.0), w=ALLQK + ALLU)
            A("pool", lambda e: e.dma_start(out=qk[64:72, 4, :], in_=kind_d), w=[("qk", 4, tt) for tt in range(4)], dma=True)
            A("pool", lambda e: e.dma_start(out=qk[0:8, 5, :], in_=kind_d), w=[("qk", 5, tt) for tt in range(4)], dma=True)

    for l in range(first_layer, first_layer + n_layers):
        layer(l)

    for i in range(NT):
        A("sp", lambda e, i=i: e.dma_start(out=out_d[i * 128:(i + 1) * 128, :], in_=x[:, i, :]), r=[("x", i)], w=[("out", i)], dma=True)
    A("sp", None, r=[("out", i) for i in range(NT)])
    P.ops[-1].fn = None

    P.emit(nc, st)
    st.close()
    return nc


def rel_bucket_np(dist):
    n = np.maximum(dist, 0)
    nf = np.maximum(n, 16).astype(np.float32)
    large = 16 + (np.log(nf / np.float32(16)) / np.float32(math.log(1024 / 16)) * np.float32(16)).astype(np.int32)
    large = np.minimum(large, 31)
    return np.where(n < 16, n, large)


def host_layout(inp):
    f = lambda a: np.ascontiguousarray(np.asarray(a, dtype=np.float32))
    w_in, w_out, w_up, w_down = f(inp["w_in"]), f(inp["w_out"]), f(inp["w_up"]), f(inp["w_down"])
    L = DEPTH
    win = np.zeros((L, 8, 128, 8, 384), np.float32)
    for u in range(8):
        if u < 4:
            cols = np.concatenate([np.arange(128) + 128 * u, 512 + np.arange(128) + 128 * u, 1024 + np.arange(128) + 128 * u])
        else:
            p = u - 4
            cols = np.concatenate([1536 + np.arange(128) + 128 * p, 2048 + np.arange(128) + 128 * p, 2560 + np.arange(128) + 128 * p])
        win[:, u] = w_in[:, :, cols].reshape(L, 8, 128, 384).transpose(0, 2, 1, 3)
    wout = np.zeros((L, 3, 128, 8, 384), np.float32)
    for s3 in range(3):
        ns = 384 if s3 < 2 else 256
        wout[:, s3, :, :, :ns] = w_out[:, :, s3 * 384:s3 * 384 + ns].reshape(L, 8, 128, ns).transpose(0, 2, 1, 3)
    wffn = np.zeros((L, NCH, 128, 3072), np.float32)
    for j in range(NCH):
        cols = np.concatenate([np.arange(128) + 128 * j, DFF + np.arange(128) + 128 * j])
        wffn[:, j, :, :2048] = w_up[:, :, cols].reshape(L, 8, 128, 256).transpose(0, 2, 1, 3).reshape(L, 128, 2048)
        wffn[:, j, :, 2048:] = w_down[:, j * 128:(j + 1) * 128, :]
    rb = f(inp["rel_bias"])
    pp = np.arange(128)[:, None]
    cc = np.arange(TW)[None, :]
    dist = cc - pp - TOFF
    bidx = rel_bucket_np(dist)
    biasT = np.empty((12, 128, TW), np.float32)
    for h in range(12):
        biasT[h] = np.where(dist >= 0, rb[bidx, h], np.float32(NEG))
    gcols = np.zeros((128, GC), np.float32)
    ln1, ln2 = f(inp["ln_attn_g"]), f(inp["ln_ffn_g"])
    qkg, sub = f(inp["qk_norm_g"]), f(inp["diff_subln_g"])
    cw, cb = f(inp["conv_w"]), f(inp["conv_b"])
    for l in range(L):
        o = l * GC_L
        gcols[:, o + GC_LN1:o + GC_LN1 + 8] = ln1[l].reshape(8, 128).T
        gcols[:, o + GC_LN2:o + GC_LN2 + 8] = ln2[l].reshape(8, 128).T
        for k in range(4):
            gcols[:, o + GC_QK + k] = np.tile(qkg[l, k], 2)
        gcols[:, o + GC_SUB] = sub[l]
        for k in range(3):
            gcols[:, o + GC_CW + k * 44:o + GC_CW + (k + 1) * 44] = cw[l, k].reshape(44, 128).T
        gcols[:, o + GC_CB:o + GC_CB + 44] = cb[l].reshape(44, 128).T
    lamb = np.broadcast_to(f(inp["diff_lambda"]).reshape(1, L * 256), (128, L * 256)).copy()
    consts = np.zeros((128, 384), np.float32)
    consts[:, 0:128] = np.eye(128, dtype=np.float32)
    consts[:, 128:256] = 1.0
    consts[0:64, 256:320] = 1.0
    consts[64:128, 320:384] = 1.0
    cnte = np.zeros((128, 8, 72), np.float32)
    cnto = np.zeros((128, 8, 8), np.float32)
    for b in range(8):
        for n in range(8):
            c = 0.0 if n < b else (-10.0 if n == b else 10.0)
            cnte[64, b, 64 + n] = c
            cnto[64, b, n] = c
            if n < b:
                for n2 in range(b):
                    cnte[n * 8 + n2, b, 64 + n] = 1.0
                    cnto[n * 8 + n2, b, n] = 1.0
    kind = np.zeros((8, S), np.float32)
    for n in range(8):
        kind[n, n * 256:(n + 1) * 256] = 1.0
    shared = dict(win=win.reshape(L, 8, 128, 3072), wout=wout.reshape(L, 3, 128, 3072), wffn=wffn, biasT=biasT, gcols=gcols,
                  lamb=lamb, consts=consts, cnte=cnte.reshape(128, 576), cnto=cnto.reshape(128, 64), kind=kind)
    return shared


_NC_CACHE = {}


def kernel(**inputs):
    x = np.ascontiguousarray(np.asarray(inputs["x"], dtype=np.float32))
    shared = host_layout(inputs)
    if "nc" not in _NC_CACHE:
        _NC_CACHE["nc"] = build(DEPTH, 0)
    nc = _NC_CACHE["nc"]
    in_maps = [dict(shared, x=x[b]) for b in range(8)]
    res = run_bass_kernel_spmd(nc, in_maps, core_ids=list(range(8)))
    return np.stack([np.asarray(r["out"], dtype=np.float32) for r in res.results], axis=0)
```
